# Optimizing a Trainium2 kernel written in Bass

```python
import jax, jax.numpy as jnp
from jax import lax
import numpy as np

D_MODEL = 1024
BATCH = 8
SEQ = 2048
DEPTH = 1
DEC_BATCH = 128
DEC_SEQ = 1
PAST_LEN = 16384
PAGE_SIZE = 128

POOL_WINDOWS = (2, 4, 8, 16)
N_POOL_GROUPS = len(POOL_WINDOWS)
POOL_GROUP_DIM = D_MODEL // 8
POOL_DIM = N_POOL_GROUPS * POOL_GROUP_DIM
POOL_STATE = max(POOL_WINDOWS) - 1
SSD_HEAD_DIM = 64
SSD_HEADS = D_MODEL // SSD_HEAD_DIM
SSD_DIM = SSD_HEADS * SSD_HEAD_DIM
SSD_GROUPS = 2
SSD_STATE = 128
SSD_CONV = 4
SSD_CHUNK = 128
CONV_DIM = SSD_DIM + 2 * SSD_GROUPS * SSD_STATE
MIX_DIM = POOL_DIM + SSD_DIM
IN_DIM = POOL_DIM + SSD_DIM + CONV_DIM + SSD_HEADS
FFN_DIM = 2816
EPS = 1e-6

kernel_name = "hymba_pool_ssd_macaron_step"


def _rms_norm(x, g):
    xf = x.astype(jnp.float32)
    y = xf * lax.rsqrt(jnp.mean(xf * xf, axis=-1, keepdims=True) + EPS)
    return (y * g.astype(jnp.float32)).astype(x.dtype)


def _swiglu(x, wg, wu, wd):
    return (jax.nn.silu(x @ wg) * (x @ wu)) @ wd


def _pool_mix(u_new, prefix, start_pos, w_pool, pool_scale):
    b, T, _ = u_new.shape
    u = jnp.concatenate([prefix.astype(u_new.dtype), u_new], axis=1)
    cs = jnp.cumsum(u.astype(jnp.float32), axis=1)
    cs = jnp.concatenate([jnp.zeros((b, 1, POOL_DIM), jnp.float32), cs], axis=1)
    end = cs[:, POOL_STATE + 1:]
    pos = start_pos + jnp.arange(T)
    outs = []
    for gi, w in enumerate(POOL_WINDOWS):
        sl = slice(gi * POOL_GROUP_DIM, (gi + 1) * POOL_GROUP_DIM)
        start = POOL_STATE + 1 - w
        win_sum = end[..., sl] - cs[:, start:start + T, sl]
        cnt = jnp.minimum(w, pos + 1).astype(jnp.float32)
        outs.append(win_sum / cnt[None, :, None])
    pooled = jnp.concatenate(outs, axis=-1) - u_new.astype(jnp.float32)
    pg = pooled.reshape(b, T, N_POOL_GROUPS, POOL_GROUP_DIM)
    mixed = jnp.einsum('btgc,gcd->btgd', pg, w_pool.astype(jnp.float32)).reshape(b, T, POOL_DIM)
    out = mixed * pool_scale.astype(jnp.float32)
    return out, u[:, -POOL_STATE:]


def _causal_conv(u_new, prefix, w, bias):
    T = u_new.shape[1]
    u = jnp.concatenate([prefix.astype(u_new.dtype), u_new], axis=1)
    out = bias + u[:, 0:T] * w[0]
    for k in range(1, SSD_CONV):
        out = out + u[:, k:k + T] * w[k]
    return jax.nn.silu(out), u[:, -(SSD_CONV - 1):]


def _ssd_scan(x, dt, A, Bm, Cm, h0, chunk):
    b, T, H, P = x.shape
    G, N = SSD_GROUPS, SSD_STATE
    Hg = H // G
    nc = T // chunk
    xc = x.reshape(b, nc, chunk, G, Hg, P)
    dtc = dt.reshape(b, nc, chunk, G, Hg)
    Bc = Bm.reshape(b, nc, chunk, G, N)
    Cc = Cm.reshape(b, nc, chunk, G, N)
    Acs = jnp.cumsum(dtc * A.reshape(G, Hg), axis=2)
    mask = jnp.tril(jnp.ones((chunk, chunk), dtype=bool))[None, None, :, :, None, None]
    seg = Acs[:, :, :, None] - Acs[:, :, None, :]
    decay = jnp.exp(jnp.where(mask, seg, -jnp.inf))
    scores = jnp.einsum('bclgn,bcsgn->bclsg', Cc, Bc)
    y_diag = jnp.einsum('bclsg,bclsgh,bcsgh,bcsghp->bclghp', scores, decay, dtc, xc)
    decay_end = jnp.exp(Acs[:, :, -1:] - Acs)
    states = jnp.einsum('bclgn,bclgh,bclghp->bcghpn', Bc, decay_end * dtc, xc)
    chunk_decay = jnp.exp(Acs[:, :, -1])

    def step(h, inp):
        s, a = inp
        return a[..., None, None] * h + s, h

    hT, h_prev = lax.scan(step, h0.reshape(b, G, Hg, P, N),
                          (jnp.moveaxis(states, 1, 0), jnp.moveaxis(chunk_decay, 1, 0)))
    h_prev = jnp.moveaxis(h_prev, 0, 1)
    y_off = jnp.einsum('bclgn,bcghpn,bclgh->bclghp', Cc, h_prev, jnp.exp(Acs))
    y = (y_diag + y_off).reshape(b, T, H, P)
    return y, hT.reshape(b, H, P, N)


def _hybrid_mixer(h, pool_prefix, conv_prefix, ssm_h0, start_pos, w_in, w_pool, pool_scale,
                  conv_w, conv_b, dt_bias, a_log, d_skip, ssd_norm, w_out):
    b, T, _ = h.shape
    proj = h @ w_in
    u_pool, z, xbc, dt_raw = jnp.split(
        proj, [POOL_DIM, POOL_DIM + SSD_DIM, POOL_DIM + SSD_DIM + CONV_DIM], axis=-1)
    pool_out, new_pool = _pool_mix(u_pool, pool_prefix, start_pos, w_pool, pool_scale)
    xbc_act, new_conv = _causal_conv(xbc, conv_prefix, conv_w, conv_b)
    xs, Bm, Cm = jnp.split(xbc_act.astype(jnp.float32),
                           [SSD_DIM, SSD_DIM + SSD_GROUPS * SSD_STATE], axis=-1)
    dt = jax.nn.softplus(dt_raw.astype(jnp.float32) + dt_bias.astype(jnp.float32))
    A = -jnp.exp(a_log.astype(jnp.float32))
    xs_h = xs.reshape(b, T, SSD_HEADS, SSD_HEAD_DIM)
    chunk = SSD_CHUNK if T % SSD_CHUNK == 0 else T
    y, h_new = _ssd_scan(xs_h, dt, A,
                         Bm.reshape(b, T, SSD_GROUPS, SSD_STATE),
                         Cm.reshape(b, T, SSD_GROUPS, SSD_STATE),
                         ssm_h0.astype(jnp.float32), chunk)
    y = y + d_skip.astype(jnp.float32)[:, None] * xs_h
    y = y.reshape(b, T, SSD_DIM) * jax.nn.silu(z.astype(jnp.float32))
    yg = y.reshape(b, T, SSD_GROUPS, SSD_DIM // SSD_GROUPS)
    yg = yg * lax.rsqrt(jnp.mean(yg * yg, axis=-1, keepdims=True) + EPS)
    y = yg.reshape(b, T, SSD_DIM) * ssd_norm.astype(jnp.float32)
    cat = jnp.concatenate([pool_out, y], axis=-1).astype(h.dtype)
    return cat @ w_out, new_pool, new_conv, h_new.astype(ssm_h0.dtype)


def _trunk(x, st_pool, st_conv, st_ssm, start_pos,
           ffn1_norm, ffn1_w_gate, ffn1_w_up, ffn1_w_down, mix_norm, w_in, w_pool, pool_scale,
           conv_w, conv_b, dt_bias, a_log, d_skip, ssd_norm, w_out,
           ffn2_norm, ffn2_w_gate, ffn2_w_up, ffn2_w_down, final_norm):
    h = x
    new_pool, new_conv, new_ssm = [], [], []
    for l in range(DEPTH):
        h = h + 0.5 * _swiglu(_rms_norm(h, ffn1_norm[l]), ffn1_w_gate[l], ffn1_w_up[l], ffn1_w_down[l])
        m, p, c, s = _hybrid_mixer(_rms_norm(h, mix_norm[l]), st_pool[l], st_conv[l], st_ssm[l],
                                   start_pos, w_in[l], w_pool[l], pool_scale[l], conv_w[l],
                                   conv_b[l], dt_bias[l], a_log[l], d_skip[l], ssd_norm[l], w_out[l])
        h = h + m
        h = h + 0.5 * _swiglu(_rms_norm(h, ffn2_norm[l]), ffn2_w_gate[l], ffn2_w_up[l], ffn2_w_down[l])
        new_pool.append(p)
        new_conv.append(c)
        new_ssm.append(s)
    return _rms_norm(h, final_norm), jnp.stack(new_pool), jnp.stack(new_conv), jnp.stack(new_ssm)


def setup_inputs(seed: int = 0) -> dict:
    key = jax.random.key(seed)
    ks = jax.random.split(key, 24)
    f32 = jnp.float32
    nrm = lambda k, shape, s: jax.random.normal(k, shape, f32) * s
    gain = lambda k, shape: 1.0 + 0.05 * jax.random.normal(k, shape, f32)
    dt0 = jnp.exp(jax.random.uniform(ks[13], (DEPTH, SSD_HEADS), f32,
                                     np.log(1e-3).astype(np.float32), np.log(1e-1).astype(np.float32)))
    return {
        "x_prompt": nrm(ks[0], (BATCH, SEQ, D_MODEL), 1.0),
        "x_sample": nrm(ks[1], (DEC_BATCH, DEC_SEQ, D_MODEL), 1.0),
        "state_pool": nrm(ks[2], (DEPTH, DEC_BATCH, POOL_STATE, POOL_DIM), 1.0),
        "state_conv": nrm(ks[3], (DEPTH, DEC_BATCH, SSD_CONV - 1, CONV_DIM), 1.0),
        "state_ssm": nrm(ks[4], (DEPTH, DEC_BATCH, SSD_HEADS, SSD_HEAD_DIM, SSD_STATE), 0.1),
        "ffn1_norm": gain(ks[5], (DEPTH, D_MODEL)),
        "ffn1_w_gate": nrm(ks[6], (DEPTH, D_MODEL, FFN_DIM), D_MODEL ** -0.5),
        "ffn1_w_up": nrm(ks[7], (DEPTH, D_MODEL, FFN_DIM), D_MODEL ** -0.5),
        "ffn1_w_down": nrm(ks[8], (DEPTH, FFN_DIM, D_MODEL), FFN_DIM ** -0.5),
        "mix_norm": gain(ks[9], (DEPTH, D_MODEL)),
        "w_in": nrm(ks[10], (DEPTH, D_MODEL, IN_DIM), D_MODEL ** -0.5),
        "w_pool": nrm(ks[11], (DEPTH, N_POOL_GROUPS, POOL_GROUP_DIM, POOL_GROUP_DIM), POOL_GROUP_DIM ** -0.5),
        "pool_scale": gain(ks[12], (DEPTH, POOL_DIM)),
        "conv_w": nrm(ks[14], (DEPTH, SSD_CONV, CONV_DIM), SSD_CONV ** -0.5),
        "conv_b": nrm(ks[15], (DEPTH, CONV_DIM), 0.01),
        "dt_bias": dt0 + jnp.log(-jnp.expm1(-dt0)),
        "a_log": jnp.log(jax.random.uniform(ks[16], (DEPTH, SSD_HEADS), f32, 1.0, 16.0)),
        "d_skip": gain(ks[17], (DEPTH, SSD_HEADS)),
        "ssd_norm": gain(ks[18], (DEPTH, SSD_DIM)),
        "w_out": nrm(ks[19], (DEPTH, MIX_DIM, D_MODEL), MIX_DIM ** -0.5),
        "ffn2_norm": gain(ks[20], (DEPTH, D_MODEL)),
        "ffn2_w_gate": nrm(ks[21], (DEPTH, D_MODEL, FFN_DIM), D_MODEL ** -0.5),
        "ffn2_w_up": nrm(ks[22], (DEPTH, D_MODEL, FFN_DIM), D_MODEL ** -0.5),
        "ffn2_w_down": nrm(ks[23], (DEPTH, FFN_DIM, D_MODEL), FFN_DIM ** -0.5),
        "final_norm": gain(jax.random.fold_in(key, 99), (D_MODEL,)),
    }


def reference(x_prompt, x_sample, state_pool, state_conv, state_ssm,
              ffn1_norm, ffn1_w_gate, ffn1_w_up, ffn1_w_down, mix_norm, w_in, w_pool, pool_scale,
              conv_w, conv_b, dt_bias, a_log, d_skip, ssd_norm, w_out,
              ffn2_norm, ffn2_w_gate, ffn2_w_up, ffn2_w_down, final_norm):
    bp = x_prompt.shape[0]
    zero_pool = jnp.zeros((DEPTH, bp, POOL_STATE, POOL_DIM), x_prompt.dtype)
    zero_conv = jnp.zeros((DEPTH, bp, SSD_CONV - 1, CONV_DIM), x_prompt.dtype)
    zero_ssm = jnp.zeros((DEPTH, bp, SSD_HEADS, SSD_HEAD_DIM, SSD_STATE), state_ssm.dtype)
    y_prompt, pool_p, conv_p, ssm_p = _trunk(
        x_prompt, zero_pool, zero_conv, zero_ssm, 0,
        ffn1_norm, ffn1_w_gate, ffn1_w_up, ffn1_w_down, mix_norm, w_in, w_pool, pool_scale,
        conv_w, conv_b, dt_bias, a_log, d_skip, ssd_norm, w_out,
        ffn2_norm, ffn2_w_gate, ffn2_w_up, ffn2_w_down, final_norm)
    y_sample, pool_s, conv_s, ssm_s = _trunk(
        x_sample, state_pool, state_conv, state_ssm, PAST_LEN,
        ffn1_norm, ffn1_w_gate, ffn1_w_up, ffn1_w_down, mix_norm, w_in, w_pool, pool_scale,
        conv_w, conv_b, dt_bias, a_log, d_skip, ssd_norm, w_out,
        ffn2_norm, ffn2_w_gate, ffn2_w_up, ffn2_w_down, final_norm)
    return (y_prompt, y_sample, pool_p, conv_p, ssm_p, pool_s, conv_s, ssm_s)
```

```python
import numpy as np
import concourse.bass as bass
import concourse.mybir as mybir
from concourse.bass_utils import run_bass_kernel_spmd

F32 = mybir.dt.float32
BF16 = mybir.dt.bfloat16
U8 = mybir.dt.uint8
ALU = mybir.AluOpType
AF = mybir.ActivationFunctionType
AX = mybir.AxisListType

NCORES = 8
D = 1024
KC = 8
TP = 2048
NS = 16
NT = TP + NS
FF = 2816
FC = 22
G_FFN = 2
FG = FC // G_FFN
IN_DIM = 3088
EPS = 1e-6
TT = [(i * 344, 344) for i in range(6)]
MT = 256
NEG = -30000.0

COMPUTE = ("pe", "act", "dve", "pool")
ENGS = ("pe", "act", "dve", "pool", "sp")
NDMA_SEM = 24


class Op:
    __slots__ = ("eng", "fn", "idx", "waits", "signal", "sigval", "is_dma",
                 "slot", "slotval", "clock", "dma_known")

    def __init__(self, eng, fn, is_dma=False):
        self.eng = eng
        self.fn = fn
        self.is_dma = is_dma
        self.waits = []
        self.signal = False
        self.sigval = None
        self.slot = None
        self.slotval = None


class Prog:
    def __init__(self, nc):
        self.nc = nc
        self.ops = {e: [] for e in ENGS}
        self.clock = {e: {} for e in ENGS}
        self.dma_known = {e: set() for e in ENGS}
        self.last_w = {}
        self.readers = {}
        self.dma_slot_last = [None] * NDMA_SEM
        self.dma_slot_cnt = [0] * NDMA_SEM
        self.dma_next = 0
        self.dma_next_sw = 0
        self.pending = {e: [] for e in ENGS}
        self.bank_fn = lambda k: None

    def _need(self, eng, dep, raw):
        if dep.is_dma:
            return dep not in self.dma_known[eng]
        if dep.eng == eng:
            if eng == "pe" or eng == "sp":
                return False
            if not raw:
                return False
        return self.clock[eng].get(dep.eng, -1) < dep.idx

    def _learn(self, eng, dep):
        ck = self.clock[eng]
        for e2, i2 in dep.clock.items():
            if ck.get(e2, -1) < i2:
                ck[e2] = i2
        self.dma_known[eng] |= dep.dma_known
        if dep.is_dma:
            self.dma_known[eng].add(dep)
        elif ck.get(dep.eng, -1) < dep.idx:
            ck[dep.eng] = dep.idx

    def barrier(self):
        lasts = []
        for e in COMPUTE:
            if self.ops[e]:
                lasts.append(self.ops[e][-1])
        dmas = [d for d in self.dma_slot_last if d is not None]
        for e in ENGS:
            self.pending[e] = [(d, True) for d in lasts + dmas]

    def add(self, eng, fn, reads=(), writes=(), dma=False):
        op = Op(eng, fn, is_dma=dma)
        op.idx = len(self.ops[eng])
        banks = set()
        for k in list(reads) + list(writes):
            b = self.bank_fn(k)
            if b is not None:
                banks.add(("bank", b))
        writes = list(writes) + list(banks)
        deps = list(self.pending[eng])
        self.pending[eng] = []
        for k in reads:
            w = self.last_w.get(k)
            if w is not None:
                deps.append((w, True))
        for k in writes:
            w = self.last_w.get(k)
            if w is not None:
                deps.append((w, False))
            for r in self.readers.get(k, ()):
                deps.append((r, False))
        if dma:
            half = NDMA_SEM // 2
            if eng == "pool":
                slot = half + self.dma_next_sw % half
                self.dma_next_sw += 1
            else:
                slot = self.dma_next % half
                self.dma_next += 1
            prev = self.dma_slot_last[slot]
            if prev is not None:
                deps.append((prev, True))
            op.slot = slot
            self.dma_slot_cnt[slot] += 16
            op.slotval = self.dma_slot_cnt[slot]
            self.dma_slot_last[slot] = op
        seen = set()
        for dep, raw in sorted(deps, key=lambda t: -int(t[1])):
            if dep is op or id(dep) in seen:
                continue
            if self._need(eng, dep, raw):
                seen.add(id(dep))
                op.waits.append(dep)
                self._learn(eng, dep)
        best = {}
        keep = []
        for d in op.waits:
            if d.is_dma:
                keep.append(d)
            else:
                b = best.get(d.eng)
                if b is None or d.idx > b.idx:
                    best[d.eng] = d
        op.waits = keep + list(best.values())
        for d in best.values():
            d.signal = True
        op.clock = dict(self.clock[eng])
        op.dma_known = set(self.dma_known[eng])
        self.ops[eng].append(op)
        for k in reads:
            self.readers.setdefault(k, []).append(op)
        for k in writes:
            self.last_w[k] = op
            self.readers[k] = []
        return op

    def emit(self):
        nc = self.nc
        sems = {}
        ctx = []
        for e in COMPUTE:
            s = nc.semaphore("s_" + e)
            ctx.append(s)
            sems[e] = s.__enter__()
        dsems = []
        for i in range(NDMA_SEM):
            s = nc.semaphore("d_%d" % i)
            ctx.append(s)
            dsems.append(s.__enter__())
        for e in COMPUTE:
            c = 0
            for op in self.ops[e]:
                if op.signal:
                    c += 1
                    op.sigval = c
        final = [(dsems[i], self.dma_slot_cnt[i]) for i in range(NDMA_SEM)
                 if self.dma_slot_cnt[i] > 0]

        def run(e, eng):
            for op in self.ops[e]:
                for d in op.waits:
                    if d.is_dma:
                        eng.wait_ge(dsems[d.slot], d.slotval)
                    else:
                        eng.wait_ge(sems[d.eng], d.sigval)
                ins = op.fn(eng)
                if op.is_dma:
                    ins.then_inc(dsems[op.slot], 16)
                elif op.signal:
                    ins.then_inc(sems[e], 1)
            if e == "sp":
                for s, v in final:
                    eng.wait_ge(s, v)

        with nc.Block() as block:
            @block.tensor
            def _(eng):
                run("pe", eng)

            @block.scalar
            def _(eng):
                run("act", eng)

            @block.vector
            def _(eng):
                run("dve", eng)

            @block.gpsimd
            def _(eng):
                run("pool", eng)

            @block.sync
            def _(eng):
                run("sp", eng)
        for s in reversed(ctx):
            s.__exit__(None, None, None)


V_N1, V_NM, V_N2, V_NF = 0, 8, 16, 24
V_PS = 32
V_CW = 36
V_CB = 84
V_DTB = 96
V_ALOG = 104
V_DSK = 112
V_SSN = 120
NV = 128
R_DTB, R_ALOG, R_DSK, R_SSN, R_INV = 0, 16, 32, 48, 48 + 1024
NROWS = R_INV + 4 * MT


def build_program(dbg=None):
    nc = bass.Bass("TRN2", target_bir_lowering=False)
    P = Prog(nc)

    def din(name, shape):
        return nc.dram_tensor(name, list(shape), F32, kind="ExternalInput").ap()

    def dout(name, shape):
        return nc.dram_tensor(name, list(shape), F32, kind="ExternalOutput").ap()

    xT = din("xT", [KC, 128, NT])
    vecs_d = din("vecs", [128, NV])
    rows_d = din("rows", [1, NROWS])
    wg_d = [din("wg1", [FC, 128, D]), din("wg2", [FC, 128, D])]
    wu_d = [din("wu1", [FC, 128, D]), din("wu2", [FC, 128, D])]
    wd_d = [din("wd1", [G_FFN, KC, 128, FG * 128]), din("wd2", [G_FFN, KC, 128, FG * 128])]
    win_d = din("win", [128, KC, IN_DIM])
    wout_d = din("wout", [128, 12, D])
    wpool_d = din("wpool", [128, 4, 128])
    spool_d = din("spool", [128, 4 * NS * 15])
    sconv_d = din("sconv", [128, 12 * NS * 3])
    sssm_d = din("sssm", [128, 8, NS * 128])

    yT = dout("yT", [KC, 128, NT])
    poolp_o = dout("poolp", [128, 4 * 15])
    convp_o = dout("convp", [128, 12 * 3])
    ssmp_o = dout("ssmp", [128, 1024])
    pools_o = dout("pools", [128, 4 * NS * 15])
    convs_o = dout("convs", [128, 12 * NS * 3])
    ssms_o = dout("ssms", [128, 8, NS * 128])

    ARENA = 210432
    arena = nc.alloc_sbuf_tensor("arena", [128, ARENA], U8)
    cur = [0]

    def alloc(shape, dt):
        n = int(np.prod(shape[1:])) * (4 if dt == F32 else 2)
        n = (n + 63) // 64 * 64
        assert cur[0] + n <= ARENA, ("arena overflow", cur[0], n)
        a = arena[:, cur[0]:cur[0] + n].bitcast(dt)
        nel = int(np.prod(shape[1:]))
        a = a[:, 0:nel]
        cur[0] += n
        if len(shape) == 3:
            a = a.rearrange("p (a b) -> p a b", b=shape[2])
        elif len(shape) == 4:
            a = a.rearrange("p (a b c) -> p a b c", b=shape[2], c=shape[3])
        return a

    H = alloc([128, KC, NT], F32)
    vecs = alloc([128, NV], F32)
    rows = alloc([128, NROWS], F32)
    A_b = alloc([128, 16], F32)
    ones_bf = alloc([128, 128], BF16)
    ident_bf = alloc([128, 128], BF16)
    ident_f = alloc([128, 128], F32)
    tri_f = alloc([128, 128], F32)
    mask4 = alloc([128, 512], BF16)
    phase_base = cur[0]

    PS = [nc.alloc_psum_tensor("ps%d" % i, [128, 512], F32) for i in range(6)]
    PR = nc.alloc_psum_tensor("pr", [128, 1024], F32)

    def bank_fn(k):
        name = k[0] if isinstance(k, tuple) else k
        if not isinstance(name, str):
            return None
        if name == "pg":
            return k[1]
        if name == "pu":
            return 2 + k[1]
        if name == "py":
            return 4 + k[1]
        if name == "pstat":
            return 6
        if name == "B0":
            return k[1]
        if name == "B2":
            return 2
        if name.startswith("B3"):
            return 3
        if name.startswith("B4"):
            return 4
        if name == "B7":
            return 5
        if name == "R":
            return 6 if k[1] < 4 else 7
        if name == "RB":
            return (6, 7, 1, 2)[k[1]]
        return None
    P.bank_fn = bank_fn

    def mm(out, lhsT, rhs, start, stop, r, w):
        P.add("pe", lambda e: e.matmul(out, lhsT=lhsT, rhs=rhs, start=start, stop=stop), r, w)

    def trp(out, in_, r, w):
        P.add("pe", lambda e: e.transpose(out, in_, ident_bf), list(r) + ["ident_bf"], w)

    def actf(out, in_, func, r, w, bias=None, scale=None, accum=None):
        kw = {}
        if bias is not None:
            kw["bias"] = bias
        if scale is not None:
            kw["scale"] = scale
        if accum is not None:
            kw["accum_out"] = accum
        P.add("act", lambda e: e.activation(out=out, in_=in_, func=func, **kw), r, w)

    def tt(eng, out, a, b, op, r, w):
        P.add(eng, lambda e: e.tensor_tensor(out=out, in0=a, in1=b, op=op), r, w)

    def ts(eng, out, a, s1, op0, r, w, s2=None, op1=None):
        if op1 is None:
            P.add(eng, lambda e: e.tensor_scalar(out=out, in0=a, scalar1=s1, scalar2=None, op0=op0), r, w)
        else:
            P.add(eng, lambda e: e.tensor_scalar(out=out, in0=a, scalar1=s1, scalar2=s2, op0=op0, op1=op1), r, w)

    def stt(out, a, s, b, op0, op1, r, w):
        P.add("dve", lambda e: e.scalar_tensor_tensor(out=out, in0=a, scalar=s, in1=b, op0=op0, op1=op1), r, w)

    def cp(eng, out, in_, r, w):
        if eng == "act":
            P.add("act", lambda e: e.copy(out=out, in_=in_), r, w)
        else:
            P.add(eng, lambda e: e.tensor_copy(out=out, in_=in_), r, w)

    def dma(q, out, in_, r, w):
        P.add(q, lambda e: e.dma_start(out=out, in_=in_), r, w, dma=True)

    ndump = [0]

    def dump(name, ap, key):
        if dbg is None or name not in dbg:
            return
        shape = list(ap.shape)
        o = dout("dbg_" + name, shape)
        dma("sp", o, ap, [key] if not isinstance(key, list) else key, [])

    def hkeys(ks, off, n):
        q0, q1 = off // 256, (off + n - 1) // 256
        return [("h", k, q) for k in ks for q in range(q0, q1 + 1)]

    XH0 = TT[0][1]
    for (c0, c1) in ((0, XH0), (XH0, 3 * XH0), (3 * XH0, NT)):
        for k in range(KC):
            dma("sp", H[:, k, c0:c1], xT[k][:, c0:c1], [], hkeys([k], c0, c1 - c0))
    dma("sp", vecs, vecs_d, [], ["vecs"])
    dma("sp", rows, rows_d.partition_broadcast(128).rearrange("p a n -> p (a n)"), [], ["rows"])
    actf(A_b, rows[:, R_ALOG:R_ALOG + 16], AF.Exp, ["rows"], ["A_b"])
    ts("dve", A_b, A_b, -1.0, ALU.mult, ["A_b"], ["A_b"])
    P.add("pool", lambda e: e.memset(ones_bf, 1.0), [], ["ones_bf"])
    P.add("pool", lambda e: e.memset(ident_bf, 1.0), [], ["ident_bf"])
    P.add("pool", lambda e: e.affine_select(out=ident_bf, in_=ident_bf, pattern=[[1, 128]], compare_op=ALU.is_equal,
                                            fill=0.0, base=0, channel_multiplier=-1), ["ident_bf"], ["ident_bf"])
    P.add("pool", lambda e: e.memset(ident_f, 1.0), [], ["ident_f"])
    P.add("pool", lambda e: e.affine_select(out=ident_f, in_=ident_f, pattern=[[1, 128]], compare_op=ALU.is_equal,
                                            fill=0.0, base=0, channel_multiplier=-1), ["ident_f"], ["ident_f"])
    P.add("pool", lambda e: e.memset(tri_f, 1.0), [], ["tri_f"])
    P.add("pool", lambda e: e.affine_select(out=tri_f, in_=tri_f, pattern=[[1, 128]], compare_op=ALU.is_ge,
                                            fill=0.0, base=0, channel_multiplier=-1), ["tri_f"], ["tri_f"])
    P.add("pool", lambda e: e.memset(mask4, NEG), [], ["mask4"])
    P.add("pool", lambda e: e.affine_select(out=mask4, in_=mask4, pattern=[[0, 4], [-1, 128]], compare_op=ALU.is_gt,
                                            fill=0.0, base=0, channel_multiplier=1), ["mask4"], ["mask4"])

    def rmsnorm(gcol, off, n, sqb, rs, pstat, pkey, out_fn, out_keys_fn, extra_w=()):
        hk = hkeys(range(KC), off, n)
        actf(sqb[:, :, 0:n], H[:, :, off:off + n], AF.Square, hk, ["sqb"] + list(extra_w))
        for k in range(KC):
            mm(pstat[:, 0:n], ones_bf, sqb[:, k, 0:n], k == 0, k == KC - 1, ["ones_bf", "sqb"], [pkey])
        actf(rs[:, 0:n], pstat[:, 0:n], AF.Ln, [pkey], ["rs"], bias=EPS, scale=1.0 / D)
        actf(rs[:, 0:n], rs[:, 0:n], AF.Exp, ["rs"], ["rs"], scale=-0.5)
        for k in range(KC):
            stt(out_fn(k), H[:, k, off:off + n], vecs[:, gcol + k:gcol + k + 1], rs[:, 0:n], ALU.mult, ALU.mult,
                hkeys([k], off, n) + ["vecs", "rs"], out_keys_fn(k))

    WBLK = [(0, 512), (3072, 3088), (1536, 2304), (2304, 3072), (512, 1024), (1024, 1536)]

    def wkey(col):
        for bi, (c0, c1) in enumerate(WBLK):
            if c0 <= col < c1:
                return ("winb", bi)
        raise AssertionError(col)

    def ffn_phase(which, gcol):
        if which == 1:
            P.barrier()
        cur[0] = phase_base
        hn = alloc([128, KC, NT], BF16)
        actT = alloc([128, FG, NT], BF16)
        NWB = 3
        wgb = [alloc([128, KC, 128], BF16) for _ in range(NWB)]
        wub = [alloc([128, KC, 128], BF16) for _ in range(NWB)]
        wdb = [alloc([128, FG, 128], BF16) for _ in range(NWB)]
        sgt = [alloc([128, 512], F32) for _ in range(2)]
        sqb = alloc([128, KC, 512], BF16)
        rs = alloc([128, 512], F32)
        pg, pu, py = [PS[0], PS[1]], [PS[2], PS[3]], [PS[4], PS[5]]
        pstat = PR[:, 0:512]

        def norm_tile(ti):
            off, n = TT[ti]
            rmsnorm(gcol, off, n, sqb, rs, pstat, "pstat",
                    lambda k, off=off, n=n: hn[:, k, off:off + n],
                    lambda k, ti=ti: [("hn", k, ti)])
        norm_tile(0)
        norm_tile(1)

        cnt = [0, 0]
        for g in range(G_FFN):
            for fi in range(FG):
                f = g * FG + fi
                b = f % NWB
                dma("pool", wgb[b].rearrange("p k c -> p (k c)"), wg_d[which][f], [], [("wg", b)])
                dma("pool", wub[b].rearrange("p k c -> p (k c)"), wu_d[which][f], [], [("wu", b)])
                for ti, (off, n) in enumerate(TT):
                    if f == 0 and ti + 2 < len(TT):
                        norm_tile(ti + 2)
                    pb = cnt[0] % 2
                    cnt[0] += 1
                    for k in range(KC):
                        mm(pg[pb][:, 0:n], wgb[b][:, k, :], hn[:, k, off:off + n], k == 0, k == KC - 1,
                           [("wg", b), ("hn", k, ti)], [("pg", pb)])
                    for k in range(KC):
                        mm(pu[pb][:, 0:n], wub[b][:, k, :], hn[:, k, off:off + n], k == 0, k == KC - 1,
                           [("wu", b), ("hn", k, ti)], [("pu", pb)])
                    actf(sgt[pb][:, 0:n], pg[pb][:, 0:n], AF.Silu, [("pg", pb)], [("sg", pb)])
                    tt("dve", actT[:, fi, off:off + n], sgt[pb][:, 0:n], pu[pb][:, 0:n], ALU.mult,
                       [("sg", pb), ("pu", pb)], [("act", fi, ti)])
            for d in range(KC):
                b = (g * KC + d) % NWB
                dma("pool", wdb[b].rearrange("p f c -> p (f c)"), wd_d[which][g, d], [], [("wd", b)])
                for ti, (off, n) in enumerate(TT):
                    pb = cnt[1] % 2
                    cnt[1] += 1
                    for fi in range(FG):
                        mm(py[pb][:, 0:n], wdb[b][:, fi, :], actT[:, fi, off:off + n], fi == 0, fi == FG - 1,
                           [("wd", b), ("act", fi, ti)], [("py", pb)])
                    hk = hkeys([d], off, n)
                    stt(H[:, d, off:off + n], py[pb][:, 0:n], 0.5, H[:, d, off:off + n], ALU.mult, ALU.add,
                        [("py", pb)] + hk, hk)

    def mixer_phase():
        P.barrier()
        cur[0] = phase_base
        win = alloc([128, KC, IN_DIM], BF16)
        wout = alloc([128, 12, D], BF16)
        wpool = alloc([128, 4, 128], BF16)
        for bi, (c0, c1) in enumerate(WBLK):
            dma("pool", win[:, :, c0:c1], win_d[:, :, c0:c1], [], [("winb", bi)])
            if bi == 0:
                dma("pool", wpool.rearrange("p g c -> p (g c)"), wpool_d.rearrange("p g c -> p (g c)"), [], ["wpool"])
        for j in range(12):
            dma("pool", wout[:, j, :], wout_d[:, j, :], [], [("wout", j)])
        WIN = [("win", k) for k in range(KC)]
        WOUT = [("wout", j) for j in range(12)]
        tile_base = cur[0]

        hnm = alloc([128, KC, MT], BF16)
        rs = alloc([128, MT], F32)
        U = alloc([128, 4, 15 + MT], F32)
        pooled = alloc([128, 4, MT], BF16)
        X = [alloc([128, 3 + MT], F32) for _ in range(2)]
        XH = alloc([128, 12, 3], F32)
        acc = [alloc([128, MT], F32) for _ in range(2)]
        xbcT = alloc([128, 12, MT], BF16)
        sqb = xbcT.rearrange("p a b -> p (a b)")[:, 0:KC * MT].rearrange("p (a b) -> p a b", b=MT)
        catT = alloc([128, 12, MT], BF16)
        NCH = MT // 128
        sm = alloc([128, 14, NCH * 16], F32)
        dtr, e1, dt_t, lndt, dtA, Acs, eAcs, bias_t, tw, wdec, cd, rr1, rr2 = [sm[:, i, :] for i in range(13)]
        dts = alloc([128, 3, NCH * 16], BF16)
        tri_bf = alloc([128, 128], BF16)
        ssq = alloc([128, 2, 2], F32)
        mhalf = alloc([128, 1], F32)
        sT = alloc([128, 128], F32)
        Dt = [alloc([128, 128], F32) for _ in range(4)]
        MTt = [alloc([128, 128], BF16) for _ in range(8)]
        xtokb = [alloc([128, 512], BF16) for _ in range(2)]
        Btokb = [alloc([128, 128], BF16) for _ in range(2)]
        xw = alloc([128, 512], BF16)
        szb = [alloc([128, 512], F32) for _ in range(2)]
        xdb = alloc([128, 512], BF16)
        tqb = [alloc([128, 512], F32) for _ in range(2)]
        yn = alloc([128, 512], BF16)
        hT = alloc([128, 1024], F32)
        hTb = alloc([128, 1024], BF16)
        prompt_end = cur[0]
        print("arena: phase_base", phase_base, "tile_base", tile_base, "prompt_end", prompt_end)

        B0 = [PS[0][:, :], PS[1][:, :]]
        B2, B3, B7 = PS[2], PS[3], PS[5]
        B3bf = PS[3][:, :].bitcast(BF16)
        B4bf = PS[4][:, :].bitcast(BF16)
        b0cnt = [0]

        def b0next():
            i = b0cnt[0] % 2
            b0cnt[0] += 1
            return B0[i], ("B0", i)

        P.add("dve", lambda e: e.memset(U[:, :, 0:15], 0.0), [], ["U"])
        P.add("dve", lambda e: e.memset(XH, 0.0), [], ["XH"])
        P.add("dve", lambda e: e.memset(hT, 0.0), [], ["hT0", "hT1"])
        P.add("pool", lambda e: e.memset(hTb, 0.0), [], ["hTb0", "hTb1"])

        P.add("pool", lambda e: e.memset(mhalf, -0.5), [], ["mhalf"])
        cp("pool", tri_bf, tri_f, ["tri_f"], ["tri_bf"])
        bc64 = lambda ap8: ap8.unsqueeze(2).to_broadcast([128, 8, 64])
        v64 = lambda ap: ap.rearrange("p (h q) -> p h q", q=64)

        def uctx(cl, g, ui):
            ub = ui % 2
            return dict(cl=cl, g=g, ub=ub, cs=cl * 128, hs=slice(cl * 16 + g * 8, cl * 16 + g * 8 + 8),
                        BTc=xbcT[:, 8 + g, cl * 128:cl * 128 + 128], CTc=xbcT[:, 10 + g, cl * 128:cl * 128 + 128],
                        xtok=xtokb[ub], Btok=Btokb[ub], sz=szb[ub], tq=tqb[ub], zk=[None, None])

        RK = [("R", hh) for hh in range(8)]

        def s1_pe(u):
            cl, g, cs = u["cl"], u["g"], u["cs"]
            for b in range(2):
                bank = PR[:, b * 512:(b + 1) * 512]
                mm(bank, ident_bf, mask4, True, False, ["ident_bf", "mask4"], [("R", b * 4)])
                for hq in range(4):
                    hh = b * 4 + hq
                    col = cl * 16 + g * 8 + hh
                    for part in range(3):
                        mm(PR[:, hh * 128:(hh + 1) * 128], dts[:, part, col:col + 1].to_broadcast([128, 128]), tri_bf,
                           False, hq == 3 and part == 2, ["dts", "tri_bf"], [("R", hh)])
            mm(B3[:, 128:256], u["BTc"], u["CTc"], True, True, [("xbc", 8 + g), ("xbc", 10 + g)], ["B3sc"])
            for i in range(4):
                trp(B4bf[:, i * 128:(i + 1) * 128], xbcT[:, g * 4 + i, cs:cs + 128], [("xbc", g * 4 + i)], ["B4x"])
            trp(B3bf[:, 512:640], u["BTc"], [("xbc", 8 + g)], ["B3bt"])
            psz, kz = b0next()
            u["zk"] = [psz, kz]
            for k in range(KC):
                mm(psz, hnm[:, k, cs:cs + 128], win[:, k, 512 + g * 512:1024 + g * 512],
                   k == 0, k == KC - 1, [("hnm", k), wkey(512 + g * 512)], [kz])
            mm(B2[:, :], u["CTc"], hTb[:, g * 512:(g + 1) * 512], True, True, [("xbc", 10 + g), "hTb%d" % g], ["B2"])

        def s1_elem(u):
            g, ub, hs = u["g"], u["ub"], u["hs"]
            r127 = PR[:, 127:1024:128]
            tt("dve", tw[:, hs], r127, Acs[:, hs], ALU.subtract, RK + ["Acs"], [("tw", ub)])
            actf(cd[:, hs], r127, AF.Exp, RK, [("cd", ub)])
            actf(wdec[:, hs], tw[:, hs], AF.Exp, [("tw", ub)], [("wdec", ub)])
            tt("dve", wdec[:, hs], wdec[:, hs], dt_t[:, hs], ALU.mult, [("wdec", ub), "dt"], [("wdec", ub)])
            cp("act", sT, B3[:, 128:256], ["B3sc"], ["sT"])
            cp("act", u["xtok"], B4bf[:, 0:512], ["B4x"], [("xtok", ub)])
            cp("act", u["Btok"], B3bf[:, 512:640], ["B3bt"], [("Btok", ub)])
            psz, kz = u["zk"]
            actf(u["sz"], psz, AF.Tanh, [kz], [("sz", ub)], scale=0.5)
            stt(u["sz"], u["sz"], 1.0, psz, ALU.add, ALU.mult, [("sz", ub), kz], [("sz", ub)])
            tt("pool", v64(xdb), v64(u["xtok"]), bc64(rows[:, R_DSK + g * 8:R_DSK + g * 8 + 8]), ALU.mult,
               [("xtok", ub), "rows"], ["xd"])

        def s1_tail(u):
            cl, g, ub, hs = u["cl"], u["g"], u["ub"], u["hs"]
            xtok, tq = u["xtok"], u["tq"]
            mm(B7[:, :], ident_bf, xdb, True, False, ["ident_bf", "xd"], ["B7"])
            for hh in range(8):
                col = cl * 16 + g * 8 + hh
                Dh = Dt[hh % 4]
                Mh = MTt[hh]
                actf(Dh, PR[:, hh * 128:(hh + 1) * 128], AF.Exp, [("R", hh), "bias"], [("D", hh % 4)],
                     bias=bias_t[:, col:col + 1], scale=1.0)
                tt("pool" if hh % 2 == 0 else "dve", Mh, Dh, sT, ALU.mult, [("D", hh % 4), "sT"], [("M", hh)])
                mm(B7[:, hh * 64:(hh + 1) * 64], Mh, xtok[:, hh * 64:(hh + 1) * 64], False, hh == 7,
                   [("M", hh), ("xtok", ub)], ["B7"])
            tt("dve", v64(tq), v64(B2[:, :]), bc64(eAcs[:, hs]), ALU.mult, ["B2", "eAcs"], [("tq", ub)])
            tt("dve", tq, tq, B7[:, :], ALU.add, [("tq", ub), "B7"], [("tq", ub)])

        def s2_front(u):
            g, ub = u["g"], u["ub"]
            tq, sz = u["tq"], u["sz"]
            sq = ssq[:, ub, :]
            stt(tq, tq, 0.5, sz, ALU.mult, ALU.mult, [("tq", ub), ("sz", ub)], [("tq", ub)])
            actf(yn, tq, AF.Square, [("tq", ub)], ["yn", ("ssq", ub)], accum=sq[:, 0:1])
            ts("pool", sq[:, 1:2], sq[:, 0:1], 1.0 / 512, ALU.mult, [("ssq", ub)], [("ssq", ub)], s2=EPS, op1=ALU.add)
            tt("pool", sq[:, 1:2], sq[:, 1:2], mhalf, ALU.pow, [("ssq", ub), "mhalf"], [("ssq", ub)])
            stt(yn, tq, sq[:, 1:2], rows[:, R_SSN + g * 512:R_SSN + (g + 1) * 512], ALU.mult, ALU.mult,
                [("tq", ub), ("ssq", ub), "rows"], ["yn"])

        def s2_mid(u):
            g, ub, cs, hs = u["g"], u["ub"], u["cs"], u["hs"]
            for i in range(4):
                trp(B4bf[:, 512 + i * 128:512 + (i + 1) * 128], yn[:, i * 128:(i + 1) * 128], ["yn"], ["B4y"])
            cp("act", catT[:, 4 + g * 4:8 + g * 4, cs:cs + 128],
               B4bf[:, 512:1024].rearrange("p (a b) -> p a b", b=128), ["B4y"],
               [("cat", 4 + g * 4 + i) for i in range(4)])

        def s2_state(u):
            g, ub, cs, hs = u["g"], u["ub"], u["cs"], u["hs"]
            tt("pool", v64(xw), v64(u["xtok"]), bc64(wdec[:, hs]), ALU.mult, [("xtok", ub), ("wdec", ub)], ["xw"])
            pst, kst = b0next()
            mm(pst, u["Btok"], xw, True, True, [("Btok", ub), "xw"], [kst])
            hg = hT[:, g * 512:(g + 1) * 512]
            tt("pool", v64(hg), v64(hg), bc64(cd[:, hs]), ALU.mult, ["hT%d" % g, ("cd", ub)], ["hT%d" % g])
            tt("dve", hg, hg, pst, ALU.add, ["hT%d" % g, kst], ["hT%d" % g])
            cp("act", hTb[:, g * 512:(g + 1) * 512], hg, ["hT%d" % g], ["hTb%d" % g])

        def tile_norm(ti):
            ps0, k0 = b0next()
            rmsnorm(V_NM, ti * MT, MT, sqb, rs, ps0[:, 0:MT], k0,
                    lambda k: hnm[:, k, :], lambda k: [("hnm", k)],
                    extra_w=[("xbc", c) for c in range(8)])

        def tile_dt():
            for cl in range(NCH):
                for k in range(KC):
                    mm(B3[:, cl * 16:(cl + 1) * 16], hnm[:, k, cl * 128:(cl + 1) * 128], win[:, k, 3072:3088],
                       k == 0, k == KC - 1, [("hnm", k), wkey(3072)], ["B3dt"])
            v16 = lambda ap: ap.rearrange("p (c h) -> p c h", h=16)
            tt("dve", v16(dtr), v16(B3[:, 0:NCH * 16]),
               rows[:, R_DTB:R_DTB + 16].unsqueeze(1).to_broadcast([128, NCH, 16]), ALU.add, ["B3dt", "rows"], ["dtr"])
            actf(e1, dtr, AF.Exp, ["dtr"], ["e1"])
            actf(dt_t, e1, AF.Ln, ["e1"], ["dt"], bias=1.0, scale=1.0)
            actf(lndt, dt_t, AF.Ln, ["dt"], ["lndt"])
            tt("dve", v16(dtA), v16(dt_t), A_b.unsqueeze(1).to_broadcast([128, NCH, 16]), ALU.mult, ["dt", "A_b"], ["dtA"])
            mm(B3[:, 64:64 + NCH * 16], tri_f, dtA, True, True, ["tri_f", "dtA"], ["B3acs"])
            cp("pool", dts[:, 0, :], dtA, ["dtA"], ["dts"])
            tt("pool", rr1, dtA, dts[:, 0, :], ALU.subtract, ["dtA", "dts"], ["rr1"])
            cp("pool", dts[:, 1, :], rr1, ["rr1"], ["dts"])
            tt("pool", rr2, rr1, dts[:, 1, :], ALU.subtract, ["rr1", "dts"], ["rr2"])
            cp("pool", dts[:, 2, :], rr2, ["rr2"], ["dts"])
            cp("dve", Acs, B3[:, 64:64 + NCH * 16], ["B3acs"], ["Acs"])
            actf(eAcs, Acs, AF.Exp, ["Acs"], ["eAcs"])
            tt("dve", bias_t, lndt, Acs, ALU.subtract, ["lndt", "Acs"], ["bias"])
        _laoff = 12 * MT // 2 - 4 * (15 + MT)
        LA = xbcT.rearrange("p a b -> p (a b)").bitcast(F32)[:, _laoff:_laoff + 4 * (15 + MT)].rearrange("p (g e) -> p g e", e=15 + MT)
        assert tqb[1].offset == tqb[0].offset + 512
        LB = arena[:, tqb[0].offset * 4:tqb[0].offset * 4 + 4096].bitcast(F32)[:, 0:3 * (15 + MT)].rearrange("p (g e) -> p g e", e=15 + MT)
        LAK = ["LA", "sqb"] + [("xbc", c) for c in range((_laoff * 2) // MT, 12)]
        LBK = ["LB", ("tq", 0), ("tq", 1)]

        def pool_inproj():
            for gi in range(4):
                ps0, k0 = b0next()
                for k in range(KC):
                    mm(ps0[:, 0:MT], win[:, k, gi * 128:(gi + 1) * 128], hnm[:, k, :], k == 0, k == KC - 1,
                       [wkey(gi * 128), ("hnm", k)], [k0])
                cp("act", U[:, gi, 15:15 + MT], ps0[:, 0:MT], [k0], ["U"])

        def outproj(tj, d):
            ps0, k0 = b0next()
            for j in range(12):
                mm(ps0[:, 0:MT], wout[:, j, d * 128:(d + 1) * 128], catT[:, j, :], j == 0, j == 11,
                   [("wout", j), ("cat", j)], [k0])
            hk = hkeys([d], tj * MT, MT)
            tt("dve", H[:, d, tj * MT:(tj + 1) * MT], ps0[:, 0:MT], H[:, d, tj * MT:(tj + 1) * MT], ALU.add,
               [k0] + hk, hk)

        NTILE = TP // MT
        for ti in range(NTILE):
            t0 = ti * MT
            if ti == 0:
                tile_norm(0)
            if ti == 0:
                tile_dt()
            if ti == 0:
                pool_inproj()
            E = 15 + MT
            tt("dve", LA[:, :, 1:E], U[:, :, 1:E], U[:, :, 0:E - 1], ALU.add, ["U"], LAK)
            tt("dve", LB[:, :, 3:E], LA[:, 1:4, 3:E], LA[:, 1:4, 1:E - 2], ALU.add, LAK, LBK + LAK)
            tt("dve", LA[:, 2:4, 7:E], LB[:, 1:3, 7:E], LB[:, 1:3, 3:E - 4], ALU.add, LBK, LAK + LBK)
            tt("dve", LB[:, 2, 15:E], LA[:, 3, 15:E], LA[:, 3, 7:E - 8], ALU.add, LAK, LBK + LAK)
            res = [LA[:, 0, 15:E], LB[:, 0, 15:E], LA[:, 2, 15:E], LB[:, 2, 15:E]]
            for gi in range(4):
                w = 2 << gi
                if ti == 0:
                    tt("dve", res[gi], res[gi], rows[:, R_INV + gi * MT:R_INV + (gi + 1) * MT], ALU.mult,
                       ["rows"], LAK + LBK)
                    tt("dve", pooled[:, gi, :], res[gi], U[:, gi, 15:E], ALU.subtract, LAK + LBK + ["U"],
                       [("pooled", gi)])
                else:
                    stt(pooled[:, gi, :], res[gi], 1.0 / w, U[:, gi, 15:E], ALU.mult, ALU.subtract,
                        LAK + LBK + ["U"], [("pooled", gi)])
            for c in range(12):
                ps0, k0 = b0next()
                col = 1536 + c * 128
                for k in range(KC):
                    mm(ps0[:, 0:MT], win[:, k, col:col + 128], hnm[:, k, :], k == 0, k == KC - 1,
                       [wkey(col), ("hnm", k)], [k0])
                xb = c % 2
                Xc = X[xb]
                cp("pool", Xc[:, 0:3], XH[:, c, :], ["XH"], [("X", xb)])
                cp("act", Xc[:, 3:3 + MT], ps0[:, 0:MT], [k0], [("X", xb)])
                cp("dve", XH[:, c, :], Xc[:, MT:MT + 3], [("X", xb)], ["XH"])
                a = acc[xb]
                cw = lambda tap: vecs[:, V_CW + tap * 12 + c:V_CW + tap * 12 + c + 1]
                actf(a, Xc[:, 0:MT], AF.Identity, [("X", xb), "vecs"], [("acc", xb)],
                     bias=vecs[:, V_CB + c:V_CB + c + 1], scale=cw(0))
                if c > 0:
                    actf(xbcT[:, c - 1, :], acc[(c - 1) % 2], AF.Silu, [("acc", (c - 1) % 2)], [("xbc", c - 1), "sqb"])
                for tap in (1, 2, 3):
                    stt(a, Xc[:, tap:tap + MT], cw(tap), a, ALU.mult, ALU.add,
                        [("X", xb), "vecs", ("acc", xb)], [("acc", xb)])
                if ti > 0 and c < KC:
                    outproj(ti - 1, c)
            actf(xbcT[:, 11, :], acc[1], AF.Silu, [("acc", 1)], [("xbc", 11), "sqb"])
            cp("pool", U[:, :, 0:15], U[:, :, MT:MT + 15], ["U"], ["U"])
            for gi in range(4):
                ps1, k1 = b0next()
                mm(ps1[:, 0:MT], wpool[:, gi, :], pooled[:, gi, :], True, True, ["wpool", ("pooled", gi)], [k1])
                ts("dve", catT[:, gi, :], ps1[:, 0:MT], vecs[:, V_PS + gi:V_PS + gi + 1], ALU.mult, [k1, "vecs"],
                   [("cat", gi)])
            units = [uctx(cl, g, cl * 2 + g) for cl in range(MT // 128) for g in range(2)]
            s1_pe(units[0])
            s1_elem(units[0])
            s1_tail(units[0])
            for ui in range(len(units)):
                nxt = units[ui + 1] if ui + 1 < len(units) else None
                if nxt is None and ti + 1 < NTILE:
                    tile_norm(ti + 1)
                    tile_dt()
                s2_front(units[ui])
                if nxt is not None:
                    s1_pe(nxt)
                    s1_elem(nxt)
                elif ti + 1 < NTILE:
                    pool_inproj()
                s2_mid(units[ui])
                if nxt is not None:
                    s1_tail(nxt)
                s2_state(units[ui])
            if ti + 1 == NTILE:
                for d in range(KC):
                    outproj(ti, d)

        dma("sp", poolp_o.rearrange("p (g j) -> p g j", j=15), U[:, :, 0:15], ["U"], [])
        dma("sp", convp_o.rearrange("p (c j) -> p c j", j=3), XH, ["XH"], [])
        dma("sp", ssmp_o, hT, ["hT0", "hT1"], [])
        dump("hmix", H[:, :, 0:TP], hkeys(range(KC), 0, TP))

        P.barrier()
        cur[0] = tile_base
        S0 = TP
        hns = alloc([128, KC, NS], BF16)
        sqs = alloc([128, KC, NS], BF16)
        rss = alloc([128, NS], F32)
        us = alloc([128, 4, NS], F32)
        Pp = alloc([128, 4, NS, 15], F32)
        NPp = alloc([128, 4, NS, 15], F32)
        ssum = alloc([128, 4, NS], F32)
        pls = alloc([128, 4, NS], BF16)
        xsr = alloc([128, 12, NS], F32)
        CP = alloc([128, 12, NS, 3], F32)
        NCv = alloc([128, 12, NS, 3], F32)
        xsa = alloc([128, 12, NS], F32)
        ctmpb = [alloc([128, 12, NS], F32) for _ in range(3)]
        zs = alloc([128, 8, NS], F32)
        dth = alloc([128, 8, NS], F32)
        ah = alloc([128, 8, NS], F32)
        Ah = alloc([128, 8], F32)
        xdt = alloc([128, 8, NS], F32)
        ys = alloc([128, 8, NS], F32)
        ygs = alloc([128, 8, NS], F32)
        rgs = alloc([128, 2, NS], F32)
        cats = alloc([128, 12, NS], BF16)
        HB = 8
        Bb = alloc([128, NS, 128], F32)
        Cb = alloc([128, NS, 128], F32)
        H0 = [alloc([128, HB, 128], F32) for _ in range(3)]
        Hn = [alloc([128, HB, 128], F32) for _ in range(2)]
        print("arena: sample_end", cur[0])

        dma("sp", Pp.rearrange("p g b j -> p (g b j)"), spool_d, [], ["Pp"])
        dma("sp", CP.rearrange("p c b j -> p (c b j)"), sconv_d, [], ["CP"])
        rmsnorm(V_NM, S0, NS, sqs, rss, PS[0][:, 0:NS], ("B0", 0),
                lambda k: hns[:, k, :], lambda k: [("hns", k)])
        for gi in range(4):
            ps0, k0 = b0next()
            for k in range(KC):
                mm(ps0[:, 0:NS], win[:, k, gi * 128:(gi + 1) * 128], hns[:, k, :], k == 0, k == KC - 1,
                   [wkey(gi * 128), ("hns", k)], [k0])
            cp("act", us[:, gi, :], ps0[:, 0:NS], [k0], ["us"])
        for c in range(12):
            ps0, k0 = b0next()
            col = 1536 + c * 128
            for k in range(KC):
                mm(ps0[:, 0:NS], win[:, k, col:col + 128], hns[:, k, :], k == 0, k == KC - 1,
                   [wkey(col), ("hns", k)], [k0])
            cp("act", xsr[:, c, :], ps0[:, 0:NS], [k0], ["xsr"])
        for j in range(8):
            ps0, k0 = b0next()
            col = 512 + j * 128
            for k in range(KC):
                mm(ps0[:, 0:NS], win[:, k, col:col + 128], hns[:, k, :], k == 0, k == KC - 1,
                   [wkey(col), ("hns", k)], [k0])
            actf(zs[:, j, :], ps0[:, 0:NS], AF.Silu, [k0], ["zs"])
        for j in range(8):
            for h2 in range(2):
                for k in range(KC):
                    c0 = 3072 + 2 * j + h2
                    lw = win[:, k, c0:c0 + 1].to_broadcast([128, 64])
                    mm(B3[h2 * 64:(h2 + 1) * 64, j * NS:(j + 1) * NS], lw, hns[:, k, :], k == 0, k == KC - 1,
                       [wkey(3072), ("hns", k)], ["B3s"])
        B3v = B3[:, 0:8 * NS].rearrange("p (j b) -> p j b", b=NS)
        bc8 = lambda col: vecs[:, col:col + 8].unsqueeze(2).to_broadcast([128, 8, NS])
        tt("dve", dth, B3v, bc8(V_DTB), ALU.add, ["B3s", "vecs"], ["dth"])
        actf(dth, dth, AF.Exp, ["dth"], ["dth"])
        actf(dth, dth, AF.Ln, ["dth"], ["dth"], bias=1.0, scale=1.0)
        actf(Ah, vecs[:, V_ALOG:V_ALOG + 8], AF.Exp, ["vecs"], ["Ah"])
        tt("dve", ah, dth, Ah.unsqueeze(2).to_broadcast([128, 8, NS]), ALU.mult, ["dth", "Ah"], ["ah"])
        actf(ah, ah, AF.Exp, ["ah"], ["ah"], scale=-1.0)
        for gi in range(4):
            w = 2 << gi
            P.add("dve", lambda e, gi=gi, w=w: e.tensor_reduce(out=ssum[:, gi, :], in_=Pp[:, gi, :, 16 - w:15],
                                                              axis=AX.X, op=ALU.add), ["Pp"], ["ssum"])
        tt("dve", ssum, ssum, us, ALU.add, ["ssum", "us"], ["ssum"])
        for gi in range(4):
            w = 2 << gi
            stt(pls[:, gi, :], ssum[:, gi, :], 1.0 / w, us[:, gi, :], ALU.mult, ALU.subtract, ["ssum", "us"], ["pls"])
            ps0, k0 = b0next()
            mm(ps0[:, 0:NS], wpool[:, gi, :], pls[:, gi, :], True, True, ["wpool", "pls"], [k0])
            ts("dve", cats[:, gi, :], ps0[:, 0:NS], vecs[:, V_PS + gi:V_PS + gi + 1], ALU.mult, [k0, "vecs"],
               [("cats", gi)])
        cp("pool", NPp[:, :, :, 0:14], Pp[:, :, :, 1:15], ["Pp"], ["NPp"])
        cp("pool", NPp[:, :, :, 14], us, ["us"], ["NPp"])
        dma("sp", pools_o, NPp.rearrange("p g b j -> p (g b j)"), ["NPp"], [])
        cwb = lambda tap: vecs[:, V_CW + tap * 12:V_CW + tap * 12 + 12].unsqueeze(2).to_broadcast([128, 12, NS])
        tt("dve", xsa, xsr, cwb(3), ALU.mult, ["xsr", "vecs"], ["xsa"])
        for tap in range(3):
            ctmp = ctmpb[tap]
            tt("dve", ctmp, CP[:, :, :, tap], cwb(tap), ALU.mult, ["CP", "vecs"], [("ctmp", tap)])
            tt("dve", xsa, xsa, ctmp, ALU.add, ["xsa", ("ctmp", tap)], ["xsa"])
        tt("dve", xsa, xsa, vecs[:, V_CB:V_CB + 12].unsqueeze(2).to_broadcast([128, 12, NS]), ALU.add,
           ["xsa", "vecs"], ["xsa"])
        actf(xsa, xsa, AF.Silu, ["xsa"], ["xsa"])
        cp("pool", NCv[:, :, :, 0:2], CP[:, :, :, 1:3], ["CP"], ["NCv"])
        cp("pool", NCv[:, :, :, 2], xsr, ["xsr"], ["NCv"])
        dma("sp", convs_o, NCv.rearrange("p c b j -> p (c b j)"), ["NCv"], [])
        tt("dve", xdt, dth, xsa[:, 0:8, :], ALU.mult, ["dth", "xsa"], ["xdt"])
        RB = [PR[:, 0:512], PR[:, 512:1024], PS[1][:, :], PS[2][:, :]]

        def ssm_load(it):
            j, hb = it // (NS // HB), it % (NS // HB)
            src = sssm_d[:, j, hb * HB * 128:(hb + 1) * HB * 128]
            dma("sp", H0[it % 3].rearrange("p b n -> p (b n)"), src, [], [("H0", it % 3)])

        for g in range(2):
            for which, dstb, chunk in ((0, Bb, 8 + g), (1, Cb, 10 + g)):
                for q in range(NS // 4):
                    rb = RB[(which * 4 + q) % 4]
                    for bi in range(4):
                        b = q * 4 + bi
                        mm(rb[:, bi * 128:(bi + 1) * 128], xsa[:, chunk, b:b + 1].to_broadcast([128, 128]), ident_f,
                           True, True, ["xsa", "ident_f"], [("RB", (which * 4 + q) % 4)])
                    cp("act", dstb[:, q * 4:(q + 1) * 4, :], rb.rearrange("p (a n) -> p a n", n=128),
                       [("RB", (which * 4 + q) % 4)], ["Bb" if which == 0 else "Cb"])
            for j in range(g * 4, g * 4 + 4):
                for hb in range(NS // HB):
                    bs = slice(hb * HB, (hb + 1) * HB)
                    it = j * 2 + hb
                    i0, i1 = it % 3, it % 2
                    if it == 0:
                        ssm_load(0)
                        ssm_load(1)
                    if it + 2 < 8 * (NS // HB):
                        ssm_load(it + 2)
                    tt("dve", H0[i0], H0[i0], ah[:, j, bs].unsqueeze(2).to_broadcast([128, HB, 128]), ALU.mult,
                       [("H0", i0), "ah"], [("H0", i0)])
                    tt("pool", Hn[i1], Bb[:, bs, :], xdt[:, j, bs].unsqueeze(2).to_broadcast([128, HB, 128]), ALU.mult,
                       ["Bb", "xdt"], [("Hn", i1)])
                    tt("pool", Hn[i1], Hn[i1], H0[i0], ALU.add, [("Hn", i1), ("H0", i0)], [("Hn", i1)])
                    dma("sp", ssms_o[:, j, hb * HB * 128:(hb + 1) * HB * 128], Hn[i1].rearrange("p b n -> p (b n)"),
                        [("Hn", i1)], [])
                    tt("dve", H0[i0], Hn[i1], Cb[:, bs, :], ALU.mult, [("Hn", i1), "Cb"], [("H0", i0)])
                    P.add("dve", lambda e, j=j, bs=bs, i0=i0: e.tensor_reduce(out=ys[:, j, bs], in_=H0[i0], axis=AX.X,
                                                                          op=ALU.add), [("H0", i0)], ["ys"])
        tt("dve", ygs, xsa[:, 0:8, :], bc8(V_DSK), ALU.mult, ["xsa", "vecs"], ["ygs"])
        tt("dve", ygs, ygs, ys, ALU.add, ["ygs", "ys"], ["ygs"])
        tt("dve", ygs, ygs, zs, ALU.mult, ["ygs", "zs"], ["ygs"])
        tt("dve", ys, ygs, ygs, ALU.mult, ["ygs"], ["ys"])
        ones_f = H0[0].rearrange("p b n -> p (b n)")[:, 0:128]
        P.add("dve", lambda e: e.memset(ones_f, 1.0), [], [("H0", 0)])
        for g in range(2):
            for jj in range(4):
                mm(B3[:, 256 + g * NS:256 + (g + 1) * NS], ones_f, ys[:, g * 4 + jj, :], jj == 0, jj == 3,
                   [("H0", 0), "ys"], ["B3g"])
        actf(rgs, B3[:, 256:256 + 2 * NS].rearrange("p (g b) -> p g b", b=NS), AF.Sqrt, ["B3g"], ["rgs"],
             bias=EPS, scale=1.0 / 512)
        P.add("dve", lambda e: e.reciprocal(out=rgs, in_=rgs), ["rgs"], ["rgs"])
        for g in range(2):
            tt("dve", ygs[:, g * 4:g * 4 + 4, :], ygs[:, g * 4:g * 4 + 4, :],
               rgs[:, g:g + 1, :].to_broadcast([128, 4, NS]), ALU.mult, ["ygs", "rgs"], ["ygs"])
        tt("dve", cats[:, 4:12, :], ygs, bc8(V_SSN), ALU.mult, ["ygs", "vecs"], [("cats", j) for j in range(4, 12)])
        for d in range(KC):
            ps0, k0 = b0next()
            for j in range(12):
                mm(ps0[:, 0:NS], wout[:, j, d * 128:(d + 1) * 128], cats[:, j, :], j == 0, j == 11,
                   [("wout", j), ("cats", j)], [k0])
            hk = hkeys([d], S0, NS)
            tt("dve", H[:, d, S0:S0 + NS], ps0[:, 0:NS], H[:, d, S0:S0 + NS], ALU.add, [k0] + hk, hk)

    def final_phase():
        P.barrier()
        cur[0] = phase_base
        sqb = alloc([128, KC, 512], BF16)
        rs = alloc([128, 512], F32)
        yst = [alloc([128, KC, 512], F32) for _ in range(2)]
        yTv = yT.rearrange("k p t -> p k t")
        for ti, (off, n) in enumerate(TT):
            yb = yst[ti % 2]
            rmsnorm(V_NF, off, n, sqb, rs, PR[:, 0:512], "pstat",
                    lambda k, yb=yb, n=n: yb[:, k, 0:n], lambda k, ti=ti: [("yst", ti % 2)])
            dma("sp", yTv[:, :, off:off + n], yb[:, :, 0:n], [("yst", ti % 2)], [])

    ffn_phase(0, V_N1)
    dump("h1", H[:, :, :], hkeys(range(KC), 0, NT))
    mixer_phase()
    dump("h2", H[:, :, :], hkeys(range(KC), 0, NT))
    ffn_phase(1, V_N2)
    final_phase()
    P.emit()
    return nc


_NC_CACHE = {}


def _prep_shared(inp):
    f = lambda a: np.ascontiguousarray(np.asarray(a, dtype=np.float32))
    sh = {}
    for i, (g, u, d) in enumerate((("ffn1_w_gate", "ffn1_w_up", "ffn1_w_down"),
                                   ("ffn2_w_gate", "ffn2_w_up", "ffn2_w_down")), start=1):
        wg = f(inp[g])[0].reshape(KC, 128, FC, 128)
        wu = f(inp[u])[0].reshape(KC, 128, FC, 128)
        sh["wg%d" % i] = np.ascontiguousarray(wg.transpose(2, 1, 0, 3)).reshape(FC, 128, D)
        sh["wu%d" % i] = np.ascontiguousarray(wu.transpose(2, 1, 0, 3)).reshape(FC, 128, D)
        wd = f(inp[d])[0].reshape(G_FFN, FG, 128, KC, 128)
        sh["wd%d" % i] = np.ascontiguousarray(wd.transpose(0, 3, 2, 1, 4)).reshape(G_FFN, KC, 128, FG * 128)
    sh["win"] = np.ascontiguousarray(f(inp["w_in"])[0].reshape(KC, 128, IN_DIM).transpose(1, 0, 2))
    sh["wout"] = np.ascontiguousarray(f(inp["w_out"])[0].reshape(12, 128, D).transpose(1, 0, 2))
    sh["wpool"] = np.ascontiguousarray(f(inp["w_pool"])[0].transpose(1, 0, 2))
    vecs = np.zeros((128, NV), np.float32)
    col = lambda v, n: np.ascontiguousarray(f(v).reshape(n, 128).T)
    vecs[:, V_N1:V_N1 + 8] = col(inp["ffn1_norm"][0], 8)
    vecs[:, V_NM:V_NM + 8] = col(inp["mix_norm"][0], 8)
    vecs[:, V_N2:V_N2 + 8] = col(inp["ffn2_norm"][0], 8)
    vecs[:, V_NF:V_NF + 8] = col(inp["final_norm"], 8)
    vecs[:, V_PS:V_PS + 4] = col(inp["pool_scale"][0], 4)
    cw = f(inp["conv_w"])[0]
    for tap in range(4):
        vecs[:, V_CW + tap * 12:V_CW + tap * 12 + 12] = col(cw[tap], 12)
    vecs[:, V_CB:V_CB + 12] = col(inp["conv_b"][0], 12)
    hp = lambda v: np.ascontiguousarray(np.repeat(f(v).reshape(8, 2, 1), 64, axis=2).reshape(8, 128).T)
    vecs[:, V_DTB:V_DTB + 8] = hp(inp["dt_bias"][0])
    vecs[:, V_ALOG:V_ALOG + 8] = hp(inp["a_log"][0])
    vecs[:, V_DSK:V_DSK + 8] = hp(inp["d_skip"][0])
    vecs[:, V_SSN:V_SSN + 8] = col(inp["ssd_norm"][0], 8)
    sh["vecs"] = vecs
    rows = np.zeros((1, NROWS), np.float32)
    rows[0, R_DTB:R_DTB + 16] = f(inp["dt_bias"])[0]
    rows[0, R_ALOG:R_ALOG + 16] = f(inp["a_log"])[0]
    rows[0, R_DSK:R_DSK + 16] = f(inp["d_skip"])[0]
    rows[0, R_SSN:R_SSN + 1024] = f(inp["ssd_norm"])[0]
    t = np.arange(MT)
    for gi in range(4):
        rows[0, R_INV + gi * MT:R_INV + (gi + 1) * MT] = 1.0 / np.minimum(2 << gi, t + 1)
    sh["rows"] = rows
    return sh


def _run(inp, dbg=None):
    key = tuple(sorted(dbg)) if dbg else None
    if key not in _NC_CACHE:
        _NC_CACHE[key] = build_program(dbg)
    nc = _NC_CACHE[key]
    f = lambda a: np.asarray(a, dtype=np.float32)
    sh = _prep_shared(inp)
    xp, xs = f(inp["x_prompt"]), f(inp["x_sample"])
    sp, sc, ss = f(inp["state_pool"])[0], f(inp["state_conv"])[0], f(inp["state_ssm"])[0]
    in_maps = []
    for i in range(NCORES):
        sl = slice(i * NS, (i + 1) * NS)
        x = np.concatenate([xp[i], xs[sl, 0, :]], axis=0)
        m = dict(sh)
        m["xT"] = np.ascontiguousarray(x.T).reshape(KC, 128, NT)
        m["spool"] = np.ascontiguousarray(sp[sl].reshape(NS, 15, 4, 128).transpose(3, 2, 0, 1)).reshape(128, -1)
        m["sconv"] = np.ascontiguousarray(sc[sl].reshape(NS, 3, 12, 128).transpose(3, 2, 0, 1)).reshape(128, -1)
        m["sssm"] = np.ascontiguousarray(ss[sl].reshape(NS, 8, 2, 64, 128).transpose(2, 3, 1, 0, 4)).reshape(128, 8, NS * 128)
        in_maps.append(m)
    res = run_bass_kernel_spmd(nc, in_maps, core_ids=list(range(NCORES)))
    R = res.results
    y_p = np.empty((8, TP, D), np.float32)
    y_s = np.empty((128, 1, D), np.float32)
    pool_p = np.empty((1, 8, 15, 512), np.float32)
    conv_p = np.empty((1, 8, 3, 1536), np.float32)
    ssm_p = np.empty((1, 8, 16, 64, 128), np.float32)
    pool_s = np.empty((1, 128, 15, 512), np.float32)
    conv_s = np.empty((1, 128, 3, 1536), np.float32)
    ssm_s = np.empty((1, 128, 16, 64, 128), np.float32)
    for i in range(NCORES):
        r = R[i]
        sl = slice(i * NS, (i + 1) * NS)
        y = np.asarray(r["yT"]).reshape(D, NT).T
        y_p[i] = y[:TP]
        y_s[sl, 0, :] = y[TP:]
        pool_p[0, i] = np.asarray(r["poolp"]).reshape(128, 4, 15).transpose(2, 1, 0).reshape(15, 512)
        conv_p[0, i] = np.asarray(r["convp"]).reshape(128, 12, 3).transpose(2, 1, 0).reshape(3, 1536)
        ssm_p[0, i] = np.asarray(r["ssmp"]).reshape(128, 16, 64).transpose(1, 2, 0)
        pool_s[0, sl] = np.asarray(r["pools"]).reshape(128, 4, NS, 15).transpose(2, 3, 1, 0).reshape(NS, 15, 512)
        conv_s[0, sl] = np.asarray(r["convs"]).reshape(128, 12, NS, 3).transpose(2, 3, 1, 0).reshape(NS, 3, 1536)
        ssm_s[0, sl] = np.asarray(r["ssms"]).reshape(2, 64, 8, NS, 128).transpose(3, 2, 0, 1, 4).reshape(NS, 16, 64, 128)
    outs = (y_p, y_s, pool_p, conv_p, ssm_p, pool_s, conv_s, ssm_s)
    if dbg:
        return outs, [{k: np.asarray(v) for k, v in r.items() if k.startswith("dbg_")} for r in R]
    return outs


def kernel(**inputs):
    return _run(inputs)
```

```python
import numpy as np
import concourse.bass as bass
import concourse.mybir as mybir
from concourse.bass_utils import run_bass_kernel_spmd

F32 = mybir.dt.float32
BF16 = mybir.dt.bfloat16
U8 = mybir.dt.uint8
ALU = mybir.AluOpType
AF = mybir.ActivationFunctionType
AX = mybir.AxisListType

NCORES = 8
D = 1024
KC = 8
TP = 2048
NS = 16
NT = TP + NS
FF = 2816
FC = 22
G_FFN = 2
FG = FC // G_FFN
IN_DIM = 3088
EPS = 1e-6
TT = [(i * 344, 344) for i in range(6)]
MT = 256
NEG = -30000.0

COMPUTE = ("pe", "act", "dve", "pool")
ENGS = ("pe", "act", "dve", "pool", "sp")
NDMA_SEM = 24


class Op:
    __slots__ = ("eng", "fn", "idx", "waits", "signal", "sigval", "is_dma",
                 "slot", "slotval", "clock", "dma_known")

    def __init__(self, eng, fn, is_dma=False):
        self.eng = eng
        self.fn = fn
        self.is_dma = is_dma
        self.waits = []
        self.signal = False
        self.sigval = None
        self.slot = None
        self.slotval = None


class Prog:
    def __init__(self, nc):
        self.nc = nc
        self.ops = {e: [] for e in ENGS}
        self.clock = {e: {} for e in ENGS}
        self.dma_known = {e: set() for e in ENGS}
        self.last_w = {}
        self.readers = {}
        self.dma_slot_last = [None] * NDMA_SEM
        self.dma_slot_cnt = [0] * NDMA_SEM
        self.dma_next = 0
        self.dma_next_sw = 0
        self.pending = {e: [] for e in ENGS}
        self.bank_fn = lambda k: None

    def _need(self, eng, dep, raw):
        if dep.is_dma:
            return dep not in self.dma_known[eng]
        if dep.eng == eng:
            if eng == "pe" or eng == "sp":
                return False
            if not raw:
                return False
        return self.clock[eng].get(dep.eng, -1) < dep.idx

    def _learn(self, eng, dep):
        ck = self.clock[eng]
        for e2, i2 in dep.clock.items():
            if ck.get(e2, -1) < i2:
                ck[e2] = i2
        self.dma_known[eng] |= dep.dma_known
        if dep.is_dma:
            self.dma_known[eng].add(dep)
        elif ck.get(dep.eng, -1) < dep.idx:
            ck[dep.eng] = dep.idx

    def barrier(self):
        lasts = []
        for e in COMPUTE:
            if self.ops[e]:
                lasts.append(self.ops[e][-1])
        dmas = [d for d in self.dma_slot_last if d is not None]
        for e in ENGS:
            self.pending[e] = [(d, True) for d in lasts + dmas]

    def add(self, eng, fn, reads=(), writes=(), dma=False):
        op = Op(eng, fn, is_dma=dma)
        op.idx = len(self.ops[eng])
        banks = set()
        for k in list(reads) + list(writes):
            b = self.bank_fn(k)
            if b is not None:
                banks.add(("bank", b))
        writes = list(writes) + list(banks)
        deps = list(self.pending[eng])
        self.pending[eng] = []
        for k in reads:
            w = self.last_w.get(k)
            if w is not None:
                deps.append((w, True))
        for k in writes:
            w = self.last_w.get(k)
            if w is not None:
                deps.append((w, False))
            for r in self.readers.get(k, ()):
                deps.append((r, False))
        if dma:
            half = NDMA_SEM // 2
            if eng == "pool":
                slot = half + self.dma_next_sw % half
                self.dma_next_sw += 1
            else:
                slot = self.dma_next % half
                self.dma_next += 1
            prev = self.dma_slot_last[slot]
            if prev is not None:
                deps.append((prev, True))
            op.slot = slot
            self.dma_slot_cnt[slot] += 16
            op.slotval = self.dma_slot_cnt[slot]
            self.dma_slot_last[slot] = op
        seen = set()
        for dep, raw in sorted(deps, key=lambda t: -int(t[1])):
            if dep is op or id(dep) in seen:
                continue
            if self._need(eng, dep, raw):
                seen.add(id(dep))
                op.waits.append(dep)
                self._learn(eng, dep)
        best = {}
        keep = []
        for d in op.waits:
            if d.is_dma:
                keep.append(d)
            else:
                b = best.get(d.eng)
                if b is None or d.idx > b.idx:
                    best[d.eng] = d
        op.waits = keep + list(best.values())
        for d in best.values():
            d.signal = True
        op.clock = dict(self.clock[eng])
        op.dma_known = set(self.dma_known[eng])
        self.ops[eng].append(op)
        for k in reads:
            self.readers.setdefault(k, []).append(op)
        for k in writes:
            self.last_w[k] = op
            self.readers[k] = []
        return op

    def emit(self):
        nc = self.nc
        sems = {}
        ctx = []
        for e in COMPUTE:
            s = nc.semaphore("s_" + e)
            ctx.append(s)
            sems[e] = s.__enter__()
        dsems = []
        for i in range(NDMA_SEM):
            s = nc.semaphore("d_%d" % i)
            ctx.append(s)
            dsems.append(s.__enter__())
        for e in COMPUTE:
            c = 0
            for op in self.ops[e]:
                if op.signal:
                    c += 1
                    op.sigval = c
        final = [(dsems[i], self.dma_slot_cnt[i]) for i in range(NDMA_SEM)
                 if self.dma_slot_cnt[i] > 0]

        def run(e, eng):
            for op in self.ops[e]:
                for d in op.waits:
                    if d.is_dma:
                        eng.wait_ge(dsems[d.slot], d.slotval)
                    else:
                        eng.wait_ge(sems[d.eng], d.sigval)
                ins = op.fn(eng)
                if op.is_dma:
                    ins.then_inc(dsems[op.slot], 16)
                elif op.signal:
                    ins.then_inc(sems[e], 1)
            if e == "sp":
                for s, v in final:
                    eng.wait_ge(s, v)

        with nc.Block() as block:
            @block.tensor
            def _(eng):
                run("pe", eng)

            @block.scalar
            def _(eng):
                run("act", eng)

            @block.vector
            def _(eng):
                run("dve", eng)

            @block.gpsimd
            def _(eng):
                run("pool", eng)

            @block.sync
            def _(eng):
                run("sp", eng)
        for s in reversed(ctx):
            s.__exit__(None, None, None)


V_N1, V_NM, V_N2, V_NF = 0, 8, 16, 24
V_PS = 32
V_CW = 36
V_CB = 84
V_DTB = 96
V_ALOG = 104
V_DSK = 112
V_SSN = 120
NV = 128
R_DTB, R_ALOG, R_DSK, R_SSN, R_INV = 0, 16, 32, 48, 48 + 1024
NROWS = R_INV + 4 * MT


def build_program(dbg=None):
    nc = bass.Bass("TRN2", target_bir_lowering=False)
    P = Prog(nc)

    def din(name, shape):
        return nc.dram_tensor(name, list(shape), F32, kind="ExternalInput").ap()

    def dout(name, shape):
        return nc.dram_tensor(name, list(shape), F32, kind="ExternalOutput").ap()

    xT = din("xT", [KC, 128, NT])
    vecs_d = din("vecs", [128, NV])
    rows_d = din("rows", [1, NROWS])
    wg_d = [din("wg1", [FC, 128, D]), din("wg2", [FC, 128, D])]
    wu_d = [din("wu1", [FC, 128, D]), din("wu2", [FC, 128, D])]
    wd_d = [din("wd1", [G_FFN, KC, 128, FG * 128]), din("wd2", [G_FFN, KC, 128, FG * 128])]
    win_d = din("win", [128, KC, IN_DIM])
    wout_d = din("wout", [128, 12, D])
    wpool_d = din("wpool", [128, 4, 128])
    spool_d = din("spool", [128, 4 * NS * 15])
    sconv_d = din("sconv", [128, 12 * NS * 3])
    sssm_d = din("sssm", [128, 8, NS * 128])

    yT = dout("yT", [KC, 128, NT])
    poolp_o = dout("poolp", [128, 4 * 15])
    convp_o = dout("convp", [128, 12 * 3])
    ssmp_o = dout("ssmp", [128, 1024])
    pools_o = dout("pools", [128, 4 * NS * 15])
    convs_o = dout("convs", [128, 12 * NS * 3])
    ssms_o = dout("ssms", [128, 8, NS * 128])

    ARENA = 210432
    arena = nc.alloc_sbuf_tensor("arena", [128, ARENA], U8)
    cur = [0]

    def alloc(shape, dt):
        n = int(np.prod(shape[1:])) * (4 if dt == F32 else 2)
        n = (n + 63) // 64 * 64
        assert cur[0] + n <= ARENA, ("arena overflow", cur[0], n)
        a = arena[:, cur[0]:cur[0] + n].bitcast(dt)
        nel = int(np.prod(shape[1:]))
        a = a[:, 0:nel]
        cur[0] += n
        if len(shape) == 3:
            a = a.rearrange("p (a b) -> p a b", b=shape[2])
        elif len(shape) == 4:
            a = a.rearrange("p (a b c) -> p a b c", b=shape[2], c=shape[3])
        return a

    H = alloc([128, KC, NT], F32)
    vecs = alloc([128, NV], F32)
    rows = alloc([128, NROWS], F32)
    A_b = alloc([128, 16], F32)
    ones_bf = alloc([128, 128], BF16)
    ident_bf = alloc([128, 128], BF16)
    ident_f = alloc([128, 128], F32)
    tri_f = alloc([128, 128], F32)
    mask4 = alloc([128, 512], BF16)
    phase_base = cur[0]

    PS = [nc.alloc_psum_tensor("ps%d" % i, [128, 512], F32) for i in range(6)]
    PR = nc.alloc_psum_tensor("pr", [128, 1024], F32)

    def bank_fn(k):
        name = k[0] if isinstance(k, tuple) else k
        if not isinstance(name, str):
            return None
        if name == "pg":
            return k[1]
        if name == "pu":
            return 2 + k[1]
        if name == "py":
            return 4 + k[1]
        if name == "pstat":
            return 6
        if name == "B0":
            return k[1]
        if name == "B2":
            return 2
        if name.startswith("B3"):
            return 3
        if name.startswith("B4"):
            return 4
        if name == "B7":
            return 5
        if name == "R":
            return 6 if k[1] < 4 else 7
        if name == "RB":
            return (6, 7, 1, 2)[k[1]]
        return None
    P.bank_fn = bank_fn

    def mm(out, lhsT, rhs, start, stop, r, w):
        P.add("pe", lambda e: e.matmul(out, lhsT=lhsT, rhs=rhs, start=start, stop=stop), r, w)

    def trp(out, in_, r, w):
        P.add("pe", lambda e: e.transpose(out, in_, ident_bf), list(r) + ["ident_bf"], w)

    def actf(out, in_, func, r, w, bias=None, scale=None, accum=None):
        kw = {}
        if bias is not None:
            kw["bias"] = bias
        if scale is not None:
            kw["scale"] = scale
        if accum is not None:
            kw["accum_out"] = accum
        P.add("act", lambda e: e.activation(out=out, in_=in_, func=func, **kw), r, w)

    def tt(eng, out, a, b, op, r, w):
        P.add(eng, lambda e: e.tensor_tensor(out=out, in0=a, in1=b, op=op), r, w)

    def ts(eng, out, a, s1, op0, r, w, s2=None, op1=None):
        if op1 is None:
            P.add(eng, lambda e: e.tensor_scalar(out=out, in0=a, scalar1=s1, scalar2=None, op0=op0), r, w)
        else:
            P.add(eng, lambda e: e.tensor_scalar(out=out, in0=a, scalar1=s1, scalar2=s2, op0=op0, op1=op1), r, w)

    def stt(out, a, s, b, op0, op1, r, w):
        P.add("dve", lambda e: e.scalar_tensor_tensor(out=out, in0=a, scalar=s, in1=b, op0=op0, op1=op1), r, w)

    def cp(eng, out, in_, r, w):
        if eng == "act":
            P.add("act", lambda e: e.copy(out=out, in_=in_), r, w)
        else:
            P.add(eng, lambda e: e.tensor_copy(out=out, in_=in_), r, w)

    def dma(q, out, in_, r, w):
        P.add(q, lambda e: e.dma_start(out=out, in_=in_), r, w, dma=True)

    ndump = [0]

    def dump(name, ap, key):
        if dbg is None or name not in dbg:
            return
        shape = list(ap.shape)
        o = dout("dbg_" + name, shape)
        dma("sp", o, ap, [key] if not isinstance(key, list) else key, [])

    def hkeys(ks, off, n):
        q0, q1 = off // 256, (off + n - 1) // 256
        return [("h", k, q) for k in ks for q in range(q0, q1 + 1)]

    for (c0, c1) in ((0, 512), (512, 1280), (1280, NT)):
        for k in range(KC):
            dma("sp", H[:, k, c0:c1], xT[k][:, c0:c1], [], hkeys([k], c0, c1 - c0))
    dma("sp", vecs, vecs_d, [], ["vecs"])
    dma("sp", rows, rows_d.partition_broadcast(128).rearrange("p a n -> p (a n)"), [], ["rows"])
    actf(A_b, rows[:, R_ALOG:R_ALOG + 16], AF.Exp, ["rows"], ["A_b"])
    ts("dve", A_b, A_b, -1.0, ALU.mult, ["A_b"], ["A_b"])
    P.add("pool", lambda e: e.memset(ones_bf, 1.0), [], ["ones_bf"])
    P.add("pool", lambda e: e.memset(ident_bf, 1.0), [], ["ident_bf"])
    P.add("pool", lambda e: e.affine_select(out=ident_bf, in_=ident_bf, pattern=[[1, 128]], compare_op=ALU.is_equal,
                                            fill=0.0, base=0, channel_multiplier=-1), ["ident_bf"], ["ident_bf"])
    P.add("pool", lambda e: e.memset(ident_f, 1.0), [], ["ident_f"])
    P.add("pool", lambda e: e.affine_select(out=ident_f, in_=ident_f, pattern=[[1, 128]], compare_op=ALU.is_equal,
                                            fill=0.0, base=0, channel_multiplier=-1), ["ident_f"], ["ident_f"])
    P.add("pool", lambda e: e.memset(tri_f, 1.0), [], ["tri_f"])
    P.add("pool", lambda e: e.affine_select(out=tri_f, in_=tri_f, pattern=[[1, 128]], compare_op=ALU.is_ge,
                                            fill=0.0, base=0, channel_multiplier=-1), ["tri_f"], ["tri_f"])
    P.add("pool", lambda e: e.memset(mask4, NEG), [], ["mask4"])
    P.add("pool", lambda e: e.affine_select(out=mask4, in_=mask4, pattern=[[0, 4], [-1, 128]], compare_op=ALU.is_gt,
                                            fill=0.0, base=0, channel_multiplier=1), ["mask4"], ["mask4"])

    def rmsnorm(gcol, off, n, sqb, rs, pstat, pkey, out_fn, out_keys_fn, extra_w=()):
        hk = hkeys(range(KC), off, n)
        actf(sqb[:, :, 0:n], H[:, :, off:off + n], AF.Square, hk, ["sqb"] + list(extra_w))
        for k in range(KC):
            mm(pstat[:, 0:n], ones_bf, sqb[:, k, 0:n], k == 0, k == KC - 1, ["ones_bf", "sqb"], [pkey])
        actf(rs[:, 0:n], pstat[:, 0:n], AF.Ln, [pkey], ["rs"], bias=EPS, scale=1.0 / D)
        actf(rs[:, 0:n], rs[:, 0:n], AF.Exp, ["rs"], ["rs"], scale=-0.5)
        for k in range(KC):
            stt(out_fn(k), H[:, k, off:off + n], vecs[:, gcol + k:gcol + k + 1], rs[:, 0:n], ALU.mult, ALU.mult,
                hkeys([k], off, n) + ["vecs", "rs"], out_keys_fn(k))

    WBLK = [(0, 512), (3072, 3088), (1536, 2304), (2304, 3072), (512, 1024), (1024, 1536)]

    def wkey(col):
        for bi, (c0, c1) in enumerate(WBLK):
            if c0 <= col < c1:
                return ("winb", bi)
        raise AssertionError(col)

    def ffn_phase(which, gcol):
        if which == 1:
            P.barrier()
        cur[0] = phase_base
        hn = alloc([128, KC, NT], BF16)
        actT = alloc([128, FG, NT], BF16)
        NWB = 3
        wgb = [alloc([128, KC, 128], BF16) for _ in range(NWB)]
        wub = [alloc([128, KC, 128], BF16) for _ in range(NWB)]
        wdb = [alloc([128, FG, 128], BF16) for _ in range(NWB)]
        sgt = [alloc([128, 512], F32) for _ in range(2)]
        sqb = alloc([128, KC, 512], BF16)
        rs = alloc([128, 512], F32)
        pg, pu, py = [PS[0], PS[1]], [PS[2], PS[3]], [PS[4], PS[5]]
        pstat = PR[:, 0:512]

        def norm_tile(ti):
            off, n = TT[ti]
            rmsnorm(gcol, off, n, sqb, rs, pstat, "pstat",
                    lambda k, off=off, n=n: hn[:, k, off:off + n],
                    lambda k, ti=ti: [("hn", k, ti)])
        norm_tile(0)
        norm_tile(1)

        cnt = [0, 0]
        for g in range(G_FFN):
            for fi in range(FG):
                f = g * FG + fi
                b = f % NWB
                dma("pool", wgb[b].rearrange("p k c -> p (k c)"), wg_d[which][f], [], [("wg", b)])
                dma("pool", wub[b].rearrange("p k c -> p (k c)"), wu_d[which][f], [], [("wu", b)])
                for ti, (off, n) in enumerate(TT):
                    if f == 0 and ti + 2 < len(TT):
                        norm_tile(ti + 2)
                    pb = cnt[0] % 2
                    cnt[0] += 1
                    for k in range(KC):
                        mm(pg[pb][:, 0:n], wgb[b][:, k, :], hn[:, k, off:off + n], k == 0, k == KC - 1,
                           [("wg", b), ("hn", k, ti)], [("pg", pb)])
                    for k in range(KC):
                        mm(pu[pb][:, 0:n], wub[b][:, k, :], hn[:, k, off:off + n], k == 0, k == KC - 1,
                           [("wu", b), ("hn", k, ti)], [("pu", pb)])
                    actf(sgt[pb][:, 0:n], pg[pb][:, 0:n], AF.Silu, [("pg", pb)], [("sg", pb)])
                    tt("dve", actT[:, fi, off:off + n], sgt[pb][:, 0:n], pu[pb][:, 0:n], ALU.mult,
                       [("sg", pb), ("pu", pb)], [("act", fi, ti)])
            for d in range(KC):
                b = (g * KC + d) % NWB
                dma("pool", wdb[b].rearrange("p f c -> p (f c)"), wd_d[which][g, d], [], [("wd", b)])
                for ti, (off, n) in enumerate(TT):
                    pb = cnt[1] % 2
                    cnt[1] += 1
                    for fi in range(FG):
                        mm(py[pb][:, 0:n], wdb[b][:, fi, :], actT[:, fi, off:off + n], fi == 0, fi == FG - 1,
                           [("wd", b), ("act", fi, ti)], [("py", pb)])
                    hk = hkeys([d], off, n)
                    stt(H[:, d, off:off + n], py[pb][:, 0:n], 0.5, H[:, d, off:off + n], ALU.mult, ALU.add,
                        [("py", pb)] + hk, hk)

    def mixer_phase():
        P.barrier()
        cur[0] = phase_base
        win = alloc([128, KC, IN_DIM], BF16)
        wout = alloc([128, 12, D], BF16)
        wpool = alloc([128, 4, 128], BF16)
        for bi, (c0, c1) in enumerate(WBLK):
            dma("pool", win[:, :, c0:c1], win_d[:, :, c0:c1], [], [("winb", bi)])
            if bi == 0:
                dma("pool", wpool.rearrange("p g c -> p (g c)"), wpool_d.rearrange("p g c -> p (g c)"), [], ["wpool"])
        for j in range(12):
            dma("pool", wout[:, j, :], wout_d[:, j, :], [], [("wout", j)])
        WIN = [("win", k) for k in range(KC)]
        WOUT = [("wout", j) for j in range(12)]
        tile_base = cur[0]

        hnm = alloc([128, KC, MT], BF16)
        rs = alloc([128, MT], F32)
        U = alloc([128, 4, 15 + MT], F32)
        pooled = alloc([128, 4, MT], BF16)
        X = [alloc([128, 3 + MT], F32) for _ in range(2)]
        XH = alloc([128, 12, 3], F32)
        acc = [alloc([128, MT], F32) for _ in range(2)]
        xbcT = alloc([128, 12, MT], BF16)
        sqb = xbcT.rearrange("p a b -> p (a b)")[:, 0:KC * MT].rearrange("p (a b) -> p a b", b=MT)
        catT = alloc([128, 12, MT], BF16)
        NCH = MT // 128
        sm = alloc([128, 14, NCH * 16], F32)
        dtr, e1, dt_t, lndt, dtA, Acs, eAcs, bias_t, tw, wdec, cd, rr1, rr2 = [sm[:, i, :] for i in range(13)]
        dts = alloc([128, 3, NCH * 16], BF16)
        tri_bf = alloc([128, 128], BF16)
        ssq = alloc([128, 2, 2], F32)
        mhalf = alloc([128, 1], F32)
        sT = alloc([128, 128], F32)
        Dt = [alloc([128, 128], F32) for _ in range(4)]
        MTt = [alloc([128, 128], BF16) for _ in range(8)]
        xtokb = [alloc([128, 512], BF16) for _ in range(2)]
        Btokb = [alloc([128, 128], BF16) for _ in range(2)]
        xw = alloc([128, 512], BF16)
        szb = [alloc([128, 512], F32) for _ in range(2)]
        xdb = alloc([128, 512], BF16)
        tqb = [alloc([128, 512], F32) for _ in range(2)]
        yn = alloc([128, 512], BF16)
        hT = alloc([128, 1024], F32)
        hTb = alloc([128, 1024], BF16)
        prompt_end = cur[0]
        print("arena: phase_base", phase_base, "tile_base", tile_base, "prompt_end", prompt_end)

        B0 = [PS[0][:, :], PS[1][:, :]]
        B2, B3, B7 = PS[2], PS[3], PS[5]
        B3bf = PS[3][:, :].bitcast(BF16)
        B4bf = PS[4][:, :].bitcast(BF16)
        b0cnt = [0]

        def b0next():
            i = b0cnt[0] % 2
            b0cnt[0] += 1
            return B0[i], ("B0", i)

        P.add("dve", lambda e: e.memset(U[:, :, 0:15], 0.0), [], ["U"])
        P.add("dve", lambda e: e.memset(XH, 0.0), [], ["XH"])
        P.add("dve", lambda e: e.memset(hT, 0.0), [], ["hT0", "hT1"])
        P.add("pool", lambda e: e.memset(hTb, 0.0), [], ["hTb0", "hTb1"])

        P.add("pool", lambda e: e.memset(mhalf, -0.5), [], ["mhalf"])
        cp("pool", tri_bf, tri_f, ["tri_f"], ["tri_bf"])
        bc64 = lambda ap8: ap8.unsqueeze(2).to_broadcast([128, 8, 64])
        v64 = lambda ap: ap.rearrange("p (h q) -> p h q", q=64)

        def uctx(cl, g, ui):
            ub = ui % 2
            return dict(cl=cl, g=g, ub=ub, cs=cl * 128, hs=slice(cl * 16 + g * 8, cl * 16 + g * 8 + 8),
                        BTc=xbcT[:, 8 + g, cl * 128:cl * 128 + 128], CTc=xbcT[:, 10 + g, cl * 128:cl * 128 + 128],
                        xtok=xtokb[ub], Btok=Btokb[ub], sz=szb[ub], tq=tqb[ub], zk=[None, None])

        RK = [("R", hh) for hh in range(8)]

        def s1_pe(u):
            cl, g, cs = u["cl"], u["g"], u["cs"]
            for b in range(2):
                bank = PR[:, b * 512:(b + 1) * 512]
                mm(bank, ident_bf, mask4, True, False, ["ident_bf", "mask4"], [("R", b * 4)])
                for hq in range(4):
                    hh = b * 4 + hq
                    col = cl * 16 + g * 8 + hh
                    for part in range(3):
                        mm(PR[:, hh * 128:(hh + 1) * 128], dts[:, part, col:col + 1].to_broadcast([128, 128]), tri_bf,
                           False, hq == 3 and part == 2, ["dts", "tri_bf"], [("R", hh)])
            mm(B3[:, 128:256], u["BTc"], u["CTc"], True, True, [("xbc", 8 + g), ("xbc", 10 + g)], ["B3sc"])
            for i in range(4):
                trp(B4bf[:, i * 128:(i + 1) * 128], xbcT[:, g * 4 + i, cs:cs + 128], [("xbc", g * 4 + i)], ["B4x"])
            trp(B3bf[:, 512:640], u["BTc"], [("xbc", 8 + g)], ["B3bt"])
            psz, kz = b0next()
            u["zk"] = [psz, kz]
            for k in range(KC):
                mm(psz, hnm[:, k, cs:cs + 128], win[:, k, 512 + g * 512:1024 + g * 512],
                   k == 0, k == KC - 1, [("hnm", k), wkey(512 + g * 512)], [kz])
            mm(B2[:, :], u["CTc"], hTb[:, g * 512:(g + 1) * 512], True, True, [("xbc", 10 + g), "hTb%d" % g], ["B2"])

        def s1_elem(u):
            g, ub, hs = u["g"], u["ub"], u["hs"]
            r127 = PR[:, 127:1024:128]
            tt("dve", tw[:, hs], r127, Acs[:, hs], ALU.subtract, RK + ["Acs"], [("tw", ub)])
            actf(cd[:, hs], r127, AF.Exp, RK, [("cd", ub)])
            actf(wdec[:, hs], tw[:, hs], AF.Exp, [("tw", ub)], [("wdec", ub)])
            tt("dve", wdec[:, hs], wdec[:, hs], dt_t[:, hs], ALU.mult, [("wdec", ub), "dt"], [("wdec", ub)])
            cp("act", sT, B3[:, 128:256], ["B3sc"], ["sT"])
            cp("act", u["xtok"], B4bf[:, 0:512], ["B4x"], [("xtok", ub)])
            cp("act", u["Btok"], B3bf[:, 512:640], ["B3bt"], [("Btok", ub)])
            psz, kz = u["zk"]
            actf(u["sz"], psz, AF.Tanh, [kz], [("sz", ub)], scale=0.5)
            stt(u["sz"], u["sz"], 1.0, psz, ALU.add, ALU.mult, [("sz", ub), kz], [("sz", ub)])
            tt("pool", v64(xdb), v64(u["xtok"]), bc64(rows[:, R_DSK + g * 8:R_DSK + g * 8 + 8]), ALU.mult,
               [("xtok", ub), "rows"], ["xd"])

        def s1_tail(u):
            cl, g, ub, hs = u["cl"], u["g"], u["ub"], u["hs"]
            xtok, tq = u["xtok"], u["tq"]
            mm(B7[:, :], ident_bf, xdb, True, False, ["ident_bf", "xd"], ["B7"])
            for hh in range(8):
                col = cl * 16 + g * 8 + hh
                Dh = Dt[hh % 4]
                Mh = MTt[hh]
                actf(Dh, PR[:, hh * 128:(hh + 1) * 128], AF.Exp, [("R", hh), "bias"], [("D", hh % 4)],
                     bias=bias_t[:, col:col + 1], scale=1.0)
                tt("pool" if hh % 2 == 0 else "dve", Mh, Dh, sT, ALU.mult, [("D", hh % 4), "sT"], [("M", hh)])
                mm(B7[:, hh * 64:(hh + 1) * 64], Mh, xtok[:, hh * 64:(hh + 1) * 64], False, hh == 7,
                   [("M", hh), ("xtok", ub)], ["B7"])
            tt("dve", v64(tq), v64(B2[:, :]), bc64(eAcs[:, hs]), ALU.mult, ["B2", "eAcs"], [("tq", ub)])
            tt("dve", tq, tq, B7[:, :], ALU.add, [("tq", ub), "B7"], [("tq", ub)])

        def s2_front(u):
            g, ub = u["g"], u["ub"]
            tq, sz = u["tq"], u["sz"]
            sq = ssq[:, ub, :]
            stt(tq, tq, 0.5, sz, ALU.mult, ALU.mult, [("tq", ub), ("sz", ub)], [("tq", ub)])
            actf(yn, tq, AF.Square, [("tq", ub)], ["yn", ("ssq", ub)], accum=sq[:, 0:1])
            ts("pool", sq[:, 1:2], sq[:, 0:1], 1.0 / 512, ALU.mult, [("ssq", ub)], [("ssq", ub)], s2=EPS, op1=ALU.add)
            tt("pool", sq[:, 1:2], sq[:, 1:2], mhalf, ALU.pow, [("ssq", ub), "mhalf"], [("ssq", ub)])
            stt(yn, tq, sq[:, 1:2], rows[:, R_SSN + g * 512:R_SSN + (g + 1) * 512], ALU.mult, ALU.mult,
                [("tq", ub), ("ssq", ub), "rows"], ["yn"])

        def s2_mid(u):
            g, ub, cs, hs = u["g"], u["ub"], u["cs"], u["hs"]
            for i in range(4):
                trp(B4bf[:, 512 + i * 128:512 + (i + 1) * 128], yn[:, i * 128:(i + 1) * 128], ["yn"], ["B4y"])
            cp("act", catT[:, 4 + g * 4:8 + g * 4, cs:cs + 128],
               B4bf[:, 512:1024].rearrange("p (a b) -> p a b", b=128), ["B4y"],
               [("cat", 4 + g * 4 + i) for i in range(4)])

        def s2_state(u):
            g, ub, cs, hs = u["g"], u["ub"], u["cs"], u["hs"]
            tt("pool", v64(xw), v64(u["xtok"]), bc64(wdec[:, hs]), ALU.mult, [("xtok", ub), ("wdec", ub)], ["xw"])
            pst, kst = b0next()
            mm(pst, u["Btok"], xw, True, True, [("Btok", ub), "xw"], [kst])
            hg = hT[:, g * 512:(g + 1) * 512]
            tt("pool", v64(hg), v64(hg), bc64(cd[:, hs]), ALU.mult, ["hT%d" % g, ("cd", ub)], ["hT%d" % g])
            tt("dve", hg, hg, pst, ALU.add, ["hT%d" % g, kst], ["hT%d" % g])
            cp("act", hTb[:, g * 512:(g + 1) * 512], hg, ["hT%d" % g], ["hTb%d" % g])

        def tile_norm(ti):
            ps0, k0 = b0next()
            rmsnorm(V_NM, ti * MT, MT, sqb, rs, ps0[:, 0:MT], k0,
                    lambda k: hnm[:, k, :], lambda k: [("hnm", k)],
                    extra_w=[("xbc", c) for c in range(8)])

        def tile_dt():
            for cl in range(NCH):
                for k in range(KC):
                    mm(B3[:, cl * 16:(cl + 1) * 16], hnm[:, k, cl * 128:(cl + 1) * 128], win[:, k, 3072:3088],
                       k == 0, k == KC - 1, [("hnm", k), wkey(3072)], ["B3dt"])
            v16 = lambda ap: ap.rearrange("p (c h) -> p c h", h=16)
            tt("dve", v16(dtr), v16(B3[:, 0:NCH * 16]),
               rows[:, R_DTB:R_DTB + 16].unsqueeze(1).to_broadcast([128, NCH, 16]), ALU.add, ["B3dt", "rows"], ["dtr"])
            actf(e1, dtr, AF.Exp, ["dtr"], ["e1"])
            actf(dt_t, e1, AF.Ln, ["e1"], ["dt"], bias=1.0, scale=1.0)
            actf(lndt, dt_t, AF.Ln, ["dt"], ["lndt"])
            tt("dve", v16(dtA), v16(dt_t), A_b.unsqueeze(1).to_broadcast([128, NCH, 16]), ALU.mult, ["dt", "A_b"], ["dtA"])
            mm(B3[:, 64:64 + NCH * 16], tri_f, dtA, True, True, ["tri_f", "dtA"], ["B3acs"])
            cp("pool", dts[:, 0, :], dtA, ["dtA"], ["dts"])
            tt("pool", rr1, dtA, dts[:, 0, :], ALU.subtract, ["dtA", "dts"], ["rr1"])
            cp("pool", dts[:, 1, :], rr1, ["rr1"], ["dts"])
            tt("pool", rr2, rr1, dts[:, 1, :], ALU.subtract, ["rr1", "dts"], ["rr2"])
            cp("pool", dts[:, 2, :], rr2, ["rr2"], ["dts"])
            cp("dve", Acs, B3[:, 64:64 + NCH * 16], ["B3acs"], ["Acs"])
            actf(eAcs, Acs, AF.Exp, ["Acs"], ["eAcs"])
            tt("dve", bias_t, lndt, Acs, ALU.subtract, ["lndt", "Acs"], ["bias"])
        _laoff = 12 * MT // 2 - 4 * (15 + MT)
        LA = xbcT.rearrange("p a b -> p (a b)").bitcast(F32)[:, _laoff:_laoff + 4 * (15 + MT)].rearrange("p (g e) -> p g e", e=15 + MT)
        assert tqb[1].offset == tqb[0].offset + 512
        LB = arena[:, tqb[0].offset * 4:tqb[0].offset * 4 + 4096].bitcast(F32)[:, 0:3 * (15 + MT)].rearrange("p (g e) -> p g e", e=15 + MT)
        LAK = ["LA", "sqb"] + [("xbc", c) for c in range((_laoff * 2) // MT, 12)]
        LBK = ["LB", ("tq", 0), ("tq", 1)]

        def pool_inproj():
            for gi in range(4):
                ps0, k0 = b0next()
                for k in range(KC):
                    mm(ps0[:, 0:MT], win[:, k, gi * 128:(gi + 1) * 128], hnm[:, k, :], k == 0, k == KC - 1,
                       [wkey(gi * 128), ("hnm", k)], [k0])
                cp("act", U[:, gi, 15:15 + MT], ps0[:, 0:MT], [k0], ["U"])

        def outproj(tj, d):
            ps0, k0 = b0next()
            for j in range(12):
                mm(ps0[:, 0:MT], wout[:, j, d * 128:(d + 1) * 128], catT[:, j, :], j == 0, j == 11,
                   [("wout", j), ("cat", j)], [k0])
            hk = hkeys([d], tj * MT, MT)
            tt("dve", H[:, d, tj * MT:(tj + 1) * MT], ps0[:, 0:MT], H[:, d, tj * MT:(tj + 1) * MT], ALU.add,
               [k0] + hk, hk)

        NTILE = TP // MT
        for ti in range(NTILE):
            t0 = ti * MT
            if ti == 0:
                tile_norm(0)
            if ti == 0:
                tile_dt()
            if ti == 0:
                pool_inproj()
            E = 15 + MT
            tt("dve", LA[:, :, 1:E], U[:, :, 1:E], U[:, :, 0:E - 1], ALU.add, ["U"], LAK)
            tt("dve", LB[:, :, 3:E], LA[:, 1:4, 3:E], LA[:, 1:4, 1:E - 2], ALU.add, LAK, LBK + LAK)
            tt("dve", LA[:, 2:4, 7:E], LB[:, 1:3, 7:E], LB[:, 1:3, 3:E - 4], ALU.add, LBK, LAK + LBK)
            tt("dve", LB[:, 2, 15:E], LA[:, 3, 15:E], LA[:, 3, 7:E - 8], ALU.add, LAK, LBK + LAK)
            res = [LA[:, 0, 15:E], LB[:, 0, 15:E], LA[:, 2, 15:E], LB[:, 2, 15:E]]
            for gi in range(4):
                w = 2 << gi
                if ti == 0:
                    tt("dve", res[gi], res[gi], rows[:, R_INV + gi * MT:R_INV + (gi + 1) * MT], ALU.mult,
                       ["rows"], LAK + LBK)
                    tt("dve", pooled[:, gi, :], res[gi], U[:, gi, 15:E], ALU.subtract, LAK + LBK + ["U"],
                       [("pooled", gi)])
                else:
                    stt(pooled[:, gi, :], res[gi], 1.0 / w, U[:, gi, 15:E], ALU.mult, ALU.subtract,
                        LAK + LBK + ["U"], [("pooled", gi)])
            for c in range(12):
                ps0, k0 = b0next()
                col = 1536 + c * 128
                for k in range(KC):
                    mm(ps0[:, 0:MT], win[:, k, col:col + 128], hnm[:, k, :], k == 0, k == KC - 1,
                       [wkey(col), ("hnm", k)], [k0])
                xb = c % 2
                Xc = X[xb]
                cp("pool", Xc[:, 0:3], XH[:, c, :], ["XH"], [("X", xb)])
                cp("act", Xc[:, 3:3 + MT], ps0[:, 0:MT], [k0], [("X", xb)])
                cp("dve", XH[:, c, :], Xc[:, MT:MT + 3], [("X", xb)], ["XH"])
                a = acc[xb]
                cw = lambda tap: vecs[:, V_CW + tap * 12 + c:V_CW + tap * 12 + c + 1]
                actf(a, Xc[:, 0:MT], AF.Identity, [("X", xb), "vecs"], [("acc", xb)],
                     bias=vecs[:, V_CB + c:V_CB + c + 1], scale=cw(0))
                if c > 0:
                    actf(xbcT[:, c - 1, :], acc[(c - 1) % 2], AF.Silu, [("acc", (c - 1) % 2)], [("xbc", c - 1), "sqb"])
                for tap in (1, 2, 3):
                    stt(a, Xc[:, tap:tap + MT], cw(tap), a, ALU.mult, ALU.add,
                        [("X", xb), "vecs", ("acc", xb)], [("acc", xb)])
                if ti > 0 and c < KC:
                    outproj(ti - 1, c)
            actf(xbcT[:, 11, :], acc[1], AF.Silu, [("acc", 1)], [("xbc", 11), "sqb"])
            cp("pool", U[:, :, 0:15], U[:, :, MT:MT + 15], ["U"], ["U"])
            for gi in range(4):
                ps1, k1 = b0next()
                mm(ps1[:, 0:MT], wpool[:, gi, :], pooled[:, gi, :], True, True, ["wpool", ("pooled", gi)], [k1])
                ts("dve", catT[:, gi, :], ps1[:, 0:MT], vecs[:, V_PS + gi:V_PS + gi + 1], ALU.mult, [k1, "vecs"],
                   [("cat", gi)])
            units = [uctx(cl, g, cl * 2 + g) for cl in range(MT // 128) for g in range(2)]
            s1_pe(units[0])
            s1_elem(units[0])
            s1_tail(units[0])
            for ui in range(len(units)):
                nxt = units[ui + 1] if ui + 1 < len(units) else None
                if nxt is None and ti + 1 < NTILE:
                    tile_norm(ti + 1)
                    tile_dt()
                s2_front(units[ui])
                if nxt is not None:
                    s1_pe(nxt)
                    s1_elem(nxt)
                elif ti + 1 < NTILE:
                    pool_inproj()
                s2_mid(units[ui])
                if nxt is not None:
                    s1_tail(nxt)
                s2_state(units[ui])
            if ti + 1 == NTILE:
                for d in range(KC):
                    outproj(ti, d)

        dma("sp", poolp_o.rearrange("p (g j) -> p g j", j=15), U[:, :, 0:15], ["U"], [])
        dma("sp", convp_o.rearrange("p (c j) -> p c j", j=3), XH, ["XH"], [])
        dma("sp", ssmp_o, hT, ["hT0", "hT1"], [])
        dump("hmix", H[:, :, 0:TP], hkeys(range(KC), 0, TP))

        P.barrier()
        cur[0] = tile_base
        S0 = TP
        hns = alloc([128, KC, NS], BF16)
        sqs = alloc([128, KC, NS], BF16)
        rss = alloc([128, NS], F32)
        us = alloc([128, 4, NS], F32)
        Pp = alloc([128, 4, NS, 15], F32)
        NPp = alloc([128, 4, NS, 15], F32)
        ssum = alloc([128, 4, NS], F32)
        pls = alloc([128, 4, NS], BF16)
        xsr = alloc([128, 12, NS], F32)
        CP = alloc([128, 12, NS, 3], F32)
        NCv = alloc([128, 12, NS, 3], F32)
        xsa = alloc([128, 12, NS], F32)
        ctmpb = [alloc([128, 12, NS], F32) for _ in range(3)]
        zs = alloc([128, 8, NS], F32)
        dth = alloc([128, 8, NS], F32)
        ah = alloc([128, 8, NS], F32)
        Ah = alloc([128, 8], F32)
        xdt = alloc([128, 8, NS], F32)
        ys = alloc([128, 8, NS], F32)
        ygs = alloc([128, 8, NS], F32)
        rgs = alloc([128, 2, NS], F32)
        cats = alloc([128, 12, NS], BF16)
        HB = 8
        Bb = alloc([128, NS, 128], F32)
        Cb = alloc([128, NS, 128], F32)
        H0 = [alloc([128, HB, 128], F32) for _ in range(3)]
        Hn = [alloc([128, HB, 128], F32) for _ in range(2)]
        print("arena: sample_end", cur[0])

        dma("sp", Pp.rearrange("p g b j -> p (g b j)"), spool_d, [], ["Pp"])
        dma("sp", CP.rearrange("p c b j -> p (c b j)"), sconv_d, [], ["CP"])
        rmsnorm(V_NM, S0, NS, sqs, rss, PS[0][:, 0:NS], ("B0", 0),
                lambda k: hns[:, k, :], lambda k: [("hns", k)])
        for gi in range(4):
            ps0, k0 = b0next()
            for k in range(KC):
                mm(ps0[:, 0:NS], win[:, k, gi * 128:(gi + 1) * 128], hns[:, k, :], k == 0, k == KC - 1,
                   [wkey(gi * 128), ("hns", k)], [k0])
            cp("act", us[:, gi, :], ps0[:, 0:NS], [k0], ["us"])
        for c in range(12):
            ps0, k0 = b0next()
            col = 1536 + c * 128
            for k in range(KC):
                mm(ps0[:, 0:NS], win[:, k, col:col + 128], hns[:, k, :], k == 0, k == KC - 1,
                   [wkey(col), ("hns", k)], [k0])
            cp("act", xsr[:, c, :], ps0[:, 0:NS], [k0], ["xsr"])
        for j in range(8):
            ps0, k0 = b0next()
            col = 512 + j * 128
            for k in range(KC):
                mm(ps0[:, 0:NS], win[:, k, col:col + 128], hns[:, k, :], k == 0, k == KC - 1,
                   [wkey(col), ("hns", k)], [k0])
            actf(zs[:, j, :], ps0[:, 0:NS], AF.Silu, [k0], ["zs"])
        for j in range(8):
            for h2 in range(2):
                for k in range(KC):
                    c0 = 3072 + 2 * j + h2
                    lw = win[:, k, c0:c0 + 1].to_broadcast([128, 64])
                    mm(B3[h2 * 64:(h2 + 1) * 64, j * NS:(j + 1) * NS], lw, hns[:, k, :], k == 0, k == KC - 1,
                       [wkey(3072), ("hns", k)], ["B3s"])
        B3v = B3[:, 0:8 * NS].rearrange("p (j b) -> p j b", b=NS)
        bc8 = lambda col: vecs[:, col:col + 8].unsqueeze(2).to_broadcast([128, 8, NS])
        tt("dve", dth, B3v, bc8(V_DTB), ALU.add, ["B3s", "vecs"], ["dth"])
        actf(dth, dth, AF.Exp, ["dth"], ["dth"])
        actf(dth, dth, AF.Ln, ["dth"], ["dth"], bias=1.0, scale=1.0)
        actf(Ah, vecs[:, V_ALOG:V_ALOG + 8], AF.Exp, ["vecs"], ["Ah"])
        tt("dve", ah, dth, Ah.unsqueeze(2).to_broadcast([128, 8, NS]), ALU.mult, ["dth", "Ah"], ["ah"])
        actf(ah, ah, AF.Exp, ["ah"], ["ah"], scale=-1.0)
        for gi in range(4):
            w = 2 << gi
            P.add("dve", lambda e, gi=gi, w=w: e.tensor_reduce(out=ssum[:, gi, :], in_=Pp[:, gi, :, 16 - w:15],
                                                              axis=AX.X, op=ALU.add), ["Pp"], ["ssum"])
        tt("dve", ssum, ssum, us, ALU.add, ["ssum", "us"], ["ssum"])
        for gi in range(4):
            w = 2 << gi
            stt(pls[:, gi, :], ssum[:, gi, :], 1.0 / w, us[:, gi, :], ALU.mult, ALU.subtract, ["ssum", "us"], ["pls"])
            ps0, k0 = b0next()
            mm(ps0[:, 0:NS], wpool[:, gi, :], pls[:, gi, :], True, True, ["wpool", "pls"], [k0])
            ts("dve", cats[:, gi, :], ps0[:, 0:NS], vecs[:, V_PS + gi:V_PS + gi + 1], ALU.mult, [k0, "vecs"],
               [("cats", gi)])
        cp("pool", NPp[:, :, :, 0:14], Pp[:, :, :, 1:15], ["Pp"], ["NPp"])
        cp("pool", NPp[:, :, :, 14], us, ["us"], ["NPp"])
        dma("sp", pools_o, NPp.rearrange("p g b j -> p (g b j)"), ["NPp"], [])
        cwb = lambda tap: vecs[:, V_CW + tap * 12:V_CW + tap * 12 + 12].unsqueeze(2).to_broadcast([128, 12, NS])
        tt("dve", xsa, xsr, cwb(3), ALU.mult, ["xsr", "vecs"], ["xsa"])
        for tap in range(3):
            ctmp = ctmpb[tap]
            tt("dve", ctmp, CP[:, :, :, tap], cwb(tap), ALU.mult, ["CP", "vecs"], [("ctmp", tap)])
            tt("dve", xsa, xsa, ctmp, ALU.add, ["xsa", ("ctmp", tap)], ["xsa"])
        tt("dve", xsa, xsa, vecs[:, V_CB:V_CB + 12].unsqueeze(2).to_broadcast([128, 12, NS]), ALU.add,
           ["xsa", "vecs"], ["xsa"])
        actf(xsa, xsa, AF.Silu, ["xsa"], ["xsa"])
        cp("pool", NCv[:, :, :, 0:2], CP[:, :, :, 1:3], ["CP"], ["NCv"])
        cp("pool", NCv[:, :, :, 2], xsr, ["xsr"], ["NCv"])
        dma("sp", convs_o, NCv.rearrange("p c b j -> p (c b j)"), ["NCv"], [])
        tt("dve", xdt, dth, xsa[:, 0:8, :], ALU.mult, ["dth", "xsa"], ["xdt"])
        RB = [PR[:, 0:512], PR[:, 512:1024], PS[1][:, :], PS[2][:, :]]

        def ssm_load(it):
            j, hb = it // (NS // HB), it % (NS // HB)
            src = sssm_d[:, j, hb * HB * 128:(hb + 1) * HB * 128]
            dma("sp", H0[it % 3].rearrange("p b n -> p (b n)"), src, [], [("H0", it % 3)])

        for g in range(2):
            for which, dstb, chunk in ((0, Bb, 8 + g), (1, Cb, 10 + g)):
                for q in range(NS // 4):
                    rb = RB[(which * 4 + q) % 4]
                    for bi in range(4):
                        b = q * 4 + bi
                        mm(rb[:, bi * 128:(bi + 1) * 128], xsa[:, chunk, b:b + 1].to_broadcast([128, 128]), ident_f,
                           True, True, ["xsa", "ident_f"], [("RB", (which * 4 + q) % 4)])
                    cp("act", dstb[:, q * 4:(q + 1) * 4, :], rb.rearrange("p (a n) -> p a n", n=128),
                       [("RB", (which * 4 + q) % 4)], ["Bb" if which == 0 else "Cb"])
            for j in range(g * 4, g * 4 + 4):
                for hb in range(NS // HB):
                    bs = slice(hb * HB, (hb + 1) * HB)
                    it = j * 2 + hb
                    i0, i1 = it % 3, it % 2
                    if it == 0:
                        ssm_load(0)
                        ssm_load(1)
                    if it + 2 < 8 * (NS // HB):
                        ssm_load(it + 2)
                    tt("dve", H0[i0], H0[i0], ah[:, j, bs].unsqueeze(2).to_broadcast([128, HB, 128]), ALU.mult,
                       [("H0", i0), "ah"], [("H0", i0)])
                    tt("pool", Hn[i1], Bb[:, bs, :], xdt[:, j, bs].unsqueeze(2).to_broadcast([128, HB, 128]), ALU.mult,
                       ["Bb", "xdt"], [("Hn", i1)])
                    tt("pool", Hn[i1], Hn[i1], H0[i0], ALU.add, [("Hn", i1), ("H0", i0)], [("Hn", i1)])
                    dma("sp", ssms_o[:, j, hb * HB * 128:(hb + 1) * HB * 128], Hn[i1].rearrange("p b n -> p (b n)"),
                        [("Hn", i1)], [])
                    tt("dve", H0[i0], Hn[i1], Cb[:, bs, :], ALU.mult, [("Hn", i1), "Cb"], [("H0", i0)])
                    P.add("dve", lambda e, j=j, bs=bs, i0=i0: e.tensor_reduce(out=ys[:, j, bs], in_=H0[i0], axis=AX.X,
                                                                          op=ALU.add), [("H0", i0)], ["ys"])
        tt("dve", ygs, xsa[:, 0:8, :], bc8(V_DSK), ALU.mult, ["xsa", "vecs"], ["ygs"])
        tt("dve", ygs, ygs, ys, ALU.add, ["ygs", "ys"], ["ygs"])
        tt("dve", ygs, ygs, zs, ALU.mult, ["ygs", "zs"], ["ygs"])
        tt("dve", ys, ygs, ygs, ALU.mult, ["ygs"], ["ys"])
        ones_f = H0[0].rearrange("p b n -> p (b n)")[:, 0:128]
        P.add("dve", lambda e: e.memset(ones_f, 1.0), [], [("H0", 0)])
        for g in range(2):
            for jj in range(4):
                mm(B3[:, 256 + g * NS:256 + (g + 1) * NS], ones_f, ys[:, g * 4 + jj, :], jj == 0, jj == 3,
                   [("H0", 0), "ys"], ["B3g"])
        actf(rgs, B3[:, 256:256 + 2 * NS].rearrange("p (g b) -> p g b", b=NS), AF.Sqrt, ["B3g"], ["rgs"],
             bias=EPS, scale=1.0 / 512)
        P.add("dve", lambda e: e.reciprocal(out=rgs, in_=rgs), ["rgs"], ["rgs"])
        for g in range(2):
            tt("dve", ygs[:, g * 4:g * 4 + 4, :], ygs[:, g * 4:g * 4 + 4, :],
               rgs[:, g:g + 1, :].to_broadcast([128, 4, NS]), ALU.mult, ["ygs", "rgs"], ["ygs"])
        tt("dve", cats[:, 4:12, :], ygs, bc8(V_SSN), ALU.mult, ["ygs", "vecs"], [("cats", j) for j in range(4, 12)])
        for d in range(KC):
            ps0, k0 = b0next()
            for j in range(12):
                mm(ps0[:, 0:NS], wout[:, j, d * 128:(d + 1) * 128], cats[:, j, :], j == 0, j == 11,
                   [("wout", j), ("cats", j)], [k0])
            hk = hkeys([d], S0, NS)
            tt("dve", H[:, d, S0:S0 + NS], ps0[:, 0:NS], H[:, d, S0:S0 + NS], ALU.add, [k0] + hk, hk)

    def final_phase():
        P.barrier()
        cur[0] = phase_base
        sqb = alloc([128, KC, 512], BF16)
        rs = alloc([128, 512], F32)
        yst = [alloc([128, KC, 512], F32) for _ in range(2)]
        yTv = yT.rearrange("k p t -> p k t")
        for ti, (off, n) in enumerate(TT):
            yb = yst[ti % 2]
            rmsnorm(V_NF, off, n, sqb, rs, PR[:, 0:512], "pstat",
                    lambda k, yb=yb, n=n: yb[:, k, 0:n], lambda k, ti=ti: [("yst", ti % 2)])
            dma("sp", yTv[:, :, off:off + n], yb[:, :, 0:n], [("yst", ti % 2)], [])

    ffn_phase(0, V_N1)
    dump("h1", H[:, :, :], hkeys(range(KC), 0, NT))
    mixer_phase()
    dump("h2", H[:, :, :], hkeys(range(KC), 0, NT))
    ffn_phase(1, V_N2)
    final_phase()
    P.emit()
    return nc


_NC_CACHE = {}


def _prep_shared(inp):
    f = lambda a: np.ascontiguousarray(np.asarray(a, dtype=np.float32))
    sh = {}
    for i, (g, u, d) in enumerate((("ffn1_w_gate", "ffn1_w_up", "ffn1_w_down"),
                                   ("ffn2_w_gate", "ffn2_w_up", "ffn2_w_down")), start=1):
        wg = f(inp[g])[0].reshape(KC, 128, FC, 128)
        wu = f(inp[u])[0].reshape(KC, 128, FC, 128)
        sh["wg%d" % i] = np.ascontiguousarray(wg.transpose(2, 1, 0, 3)).reshape(FC, 128, D)
        sh["wu%d" % i] = np.ascontiguousarray(wu.transpose(2, 1, 0, 3)).reshape(FC, 128, D)
        wd = f(inp[d])[0].reshape(G_FFN, FG, 128, KC, 128)
        sh["wd%d" % i] = np.ascontiguousarray(wd.transpose(0, 3, 2, 1, 4)).reshape(G_FFN, KC, 128, FG * 128)
    sh["win"] = np.ascontiguousarray(f(inp["w_in"])[0].reshape(KC, 128, IN_DIM).transpose(1, 0, 2))
    sh["wout"] = np.ascontiguousarray(f(inp["w_out"])[0].reshape(12, 128, D).transpose(1, 0, 2))
    sh["wpool"] = np.ascontiguousarray(f(inp["w_pool"])[0].transpose(1, 0, 2))
    vecs = np.zeros((128, NV), np.float32)
    col = lambda v, n: np.ascontiguousarray(f(v).reshape(n, 128).T)
    vecs[:, V_N1:V_N1 + 8] = col(inp["ffn1_norm"][0], 8)
    vecs[:, V_NM:V_NM + 8] = col(inp["mix_norm"][0], 8)
    vecs[:, V_N2:V_N2 + 8] = col(inp["ffn2_norm"][0], 8)
    vecs[:, V_NF:V_NF + 8] = col(inp["final_norm"], 8)
    vecs[:, V_PS:V_PS + 4] = col(inp["pool_scale"][0], 4)
    cw = f(inp["conv_w"])[0]
    for tap in range(4):
        vecs[:, V_CW + tap * 12:V_CW + tap * 12 + 12] = col(cw[tap], 12)
    vecs[:, V_CB:V_CB + 12] = col(inp["conv_b"][0], 12)
    hp = lambda v: np.ascontiguousarray(np.repeat(f(v).reshape(8, 2, 1), 64, axis=2).reshape(8, 128).T)
    vecs[:, V_DTB:V_DTB + 8] = hp(inp["dt_bias"][0])
    vecs[:, V_ALOG:V_ALOG + 8] = hp(inp["a_log"][0])
    vecs[:, V_DSK:V_DSK + 8] = hp(inp["d_skip"][0])
    vecs[:, V_SSN:V_SSN + 8] = col(inp["ssd_norm"][0], 8)
    sh["vecs"] = vecs
    rows = np.zeros((1, NROWS), np.float32)
    rows[0, R_DTB:R_DTB + 16] = f(inp["dt_bias"])[0]
    rows[0, R_ALOG:R_ALOG + 16] = f(inp["a_log"])[0]
    rows[0, R_DSK:R_DSK + 16] = f(inp["d_skip"])[0]
    rows[0, R_SSN:R_SSN + 1024] = f(inp["ssd_norm"])[0]
    t = np.arange(MT)
    for gi in range(4):
        rows[0, R_INV + gi * MT:R_INV + (gi + 1) * MT] = 1.0 / np.minimum(2 << gi, t + 1)
    sh["rows"] = rows
    return sh


def _run(inp, dbg=None):
    key = tuple(sorted(dbg)) if dbg else None
    if key not in _NC_CACHE:
        _NC_CACHE[key] = build_program(dbg)
    nc = _NC_CACHE[key]
    f = lambda a: np.asarray(a, dtype=np.float32)
    sh = _prep_shared(inp)
    xp, xs = f(inp["x_prompt"]), f(inp["x_sample"])
    sp, sc, ss = f(inp["state_pool"])[0], f(inp["state_conv"])[0], f(inp["state_ssm"])[0]
    in_maps = []
    for i in range(NCORES):
        sl = slice(i * NS, (i + 1) * NS)
        x = np.concatenate([xp[i], xs[sl, 0, :]], axis=0)
        m = dict(sh)
        m["xT"] = np.ascontiguousarray(x.T).reshape(KC, 128, NT)
        m["spool"] = np.ascontiguousarray(sp[sl].reshape(NS, 15, 4, 128).transpose(3, 2, 0, 1)).reshape(128, -1)
        m["sconv"] = np.ascontiguousarray(sc[sl].reshape(NS, 3, 12, 128).transpose(3, 2, 0, 1)).reshape(128, -1)
        m["sssm"] = np.ascontiguousarray(ss[sl].reshape(NS, 8, 2, 64, 128).transpose(2, 3, 1, 0, 4)).reshape(128, 8, NS * 128)
        in_maps.append(m)
    res = run_bass_kernel_spmd(nc, in_maps, core_ids=list(range(NCORES)))
    R = res.results
    y_p = np.empty((8, TP, D), np.float32)
    y_s = np.empty((128, 1, D), np.float32)
    pool_p = np.empty((1, 8, 15, 512), np.float32)
    conv_p = np.empty((1, 8, 3, 1536), np.float32)
    ssm_p = np.empty((1, 8, 16, 64, 128), np.float32)
    pool_s = np.empty((1, 128, 15, 512), np.float32)
    conv_s = np.empty((1, 128, 3, 1536), np.float32)
    ssm_s = np.empty((1, 128, 16, 64, 128), np.float32)
    for i in range(NCORES):
        r = R[i]
        sl = slice(i * NS, (i + 1) * NS)
        y = np.asarray(r["yT"]).reshape(D, NT).T
        y_p[i] = y[:TP]
        y_s[sl, 0, :] = y[TP:]
        pool_p[0, i] = np.asarray(r["poolp"]).reshape(128, 4, 15).transpose(2, 1, 0).reshape(15, 512)
        conv_p[0, i] = np.asarray(r["convp"]).reshape(128, 12, 3).transpose(2, 1, 0).reshape(3, 1536)
        ssm_p[0, i] = np.asarray(r["ssmp"]).reshape(128, 16, 64).transpose(1, 2, 0)
        pool_s[0, sl] = np.asarray(r["pools"]).reshape(128, 4, NS, 15).transpose(2, 3, 1, 0).reshape(NS, 15, 512)
        conv_s[0, sl] = np.asarray(r["convs"]).reshape(128, 12, NS, 3).transpose(2, 3, 1, 0).reshape(NS, 3, 1536)
        ssm_s[0, sl] = np.asarray(r["ssms"]).reshape(2, 64, 8, NS, 128).transpose(3, 2, 0, 1, 4).reshape(NS, 16, 64, 128)
    outs = (y_p, y_s, pool_p, conv_p, ssm_p, pool_s, conv_s, ssm_s)
    if dbg:
        return outs, [{k: np.asarray(v) for k, v in r.items() if k.startswith("dbg_")} for r in R]
    return outs


def kernel(**inputs):
    return _run(inputs)
```

```python
import numpy as np
import concourse.bass as bass
import concourse.mybir as mybir
from concourse.bass_utils import run_bass_kernel_spmd

F32 = mybir.dt.float32
BF16 = mybir.dt.bfloat16
U8 = mybir.dt.uint8
ALU = mybir.AluOpType
AF = mybir.ActivationFunctionType
AX = mybir.AxisListType

NCORES = 8
D = 1024
KC = 8
TP = 2048
NS = 16
NT = TP + NS
FF = 2816
FC = 22
G_FFN = 2
FG = FC // G_FFN
IN_DIM = 3088
EPS = 1e-6
TT = [(i * 344, 344) for i in range(6)]
MT = 256
NEG = -30000.0

COMPUTE = ("pe", "act", "dve", "pool")
ENGS = ("pe", "act", "dve", "pool", "sp")
NDMA_SEM = 24


class Op:
    __slots__ = ("eng", "fn", "idx", "waits", "signal", "sigval", "is_dma",
                 "slot", "slotval", "clock", "dma_known")

    def __init__(self, eng, fn, is_dma=False):
        self.eng = eng
        self.fn = fn
        self.is_dma = is_dma
        self.waits = []
        self.signal = False
        self.sigval = None
        self.slot = None
        self.slotval = None


class Prog:
    def __init__(self, nc):
        self.nc = nc
        self.ops = {e: [] for e in ENGS}
        self.clock = {e: {} for e in ENGS}
        self.dma_known = {e: set() for e in ENGS}
        self.last_w = {}
        self.readers = {}
        self.dma_slot_last = [None] * NDMA_SEM
        self.dma_slot_cnt = [0] * NDMA_SEM
        self.dma_next = 0
        self.dma_next_sw = 0
        self.pending = {e: [] for e in ENGS}
        self.bank_fn = lambda k: None

    def _need(self, eng, dep, raw):
        if dep.is_dma:
            return dep not in self.dma_known[eng]
        if dep.eng == eng:
            if eng == "pe" or eng == "sp":
                return False
            if not raw:
                return False
        return self.clock[eng].get(dep.eng, -1) < dep.idx

    def _learn(self, eng, dep):
        ck = self.clock[eng]
        for e2, i2 in dep.clock.items():
            if ck.get(e2, -1) < i2:
                ck[e2] = i2
        self.dma_known[eng] |= dep.dma_known
        if dep.is_dma:
            self.dma_known[eng].add(dep)
        elif ck.get(dep.eng, -1) < dep.idx:
            ck[dep.eng] = dep.idx

    def barrier(self):
        lasts = []
        for e in COMPUTE:
            if self.ops[e]:
                lasts.append(self.ops[e][-1])
        dmas = [d for d in self.dma_slot_last if d is not None]
        for e in ENGS:
            self.pending[e] = [(d, True) for d in lasts + dmas]

    def add(self, eng, fn, reads=(), writes=(), dma=False):
        op = Op(eng, fn, is_dma=dma)
        op.idx = len(self.ops[eng])
        banks = set()
        for k in list(reads) + list(writes):
            b = self.bank_fn(k)
            if b is not None:
                banks.add(("bank", b))
        writes = list(writes) + list(banks)
        deps = list(self.pending[eng])
        self.pending[eng] = []
        for k in reads:
            w = self.last_w.get(k)
            if w is not None:
                deps.append((w, True))
        for k in writes:
            w = self.last_w.get(k)
            if w is not None:
                deps.append((w, False))
            for r in self.readers.get(k, ()):
                deps.append((r, False))
        if dma:
            half = NDMA_SEM // 2
            if eng == "pool":
                slot = half + self.dma_next_sw % half
                self.dma_next_sw += 1
            else:
                slot = self.dma_next % half
                self.dma_next += 1
            prev = self.dma_slot_last[slot]
            if prev is not None:
                deps.append((prev, True))
            op.slot = slot
            self.dma_slot_cnt[slot] += 16
            op.slotval = self.dma_slot_cnt[slot]
            self.dma_slot_last[slot] = op
        seen = set()
        for dep, raw in sorted(deps, key=lambda t: -int(t[1])):
            if dep is op or id(dep) in seen:
                continue
            if self._need(eng, dep, raw):
                seen.add(id(dep))
                op.waits.append(dep)
                self._learn(eng, dep)
        best = {}
        keep = []
        for d in op.waits:
            if d.is_dma:
                keep.append(d)
            else:
                b = best.get(d.eng)
                if b is None or d.idx > b.idx:
                    best[d.eng] = d
        op.waits = keep + list(best.values())
        for d in best.values():
            d.signal = True
        op.clock = dict(self.clock[eng])
        op.dma_known = set(self.dma_known[eng])
        self.ops[eng].append(op)
        for k in reads:
            self.readers.setdefault(k, []).append(op)
        for k in writes:
            self.last_w[k] = op
            self.readers[k] = []
        return op

    def emit(self):
        nc = self.nc
        sems = {}
        ctx = []
        for e in COMPUTE:
            s = nc.semaphore("s_" + e)
            ctx.append(s)
            sems[e] = s.__enter__()
        dsems = []
        for i in range(NDMA_SEM):
            s = nc.semaphore("d_%d" % i)
            ctx.append(s)
            dsems.append(s.__enter__())
        for e in COMPUTE:
            c = 0
            for op in self.ops[e]:
                if op.signal:
                    c += 1
                    op.sigval = c
        final = [(dsems[i], self.dma_slot_cnt[i]) for i in range(NDMA_SEM)
                 if self.dma_slot_cnt[i] > 0]

        def run(e, eng):
            for op in self.ops[e]:
                for d in op.waits:
                    if d.is_dma:
                        eng.wait_ge(dsems[d.slot], d.slotval)
                    else:
                        eng.wait_ge(sems[d.eng], d.sigval)
                ins = op.fn(eng)
                if op.is_dma:
                    ins.then_inc(dsems[op.slot], 16)
                elif op.signal:
                    ins.then_inc(sems[e], 1)
            if e == "sp":
                for s, v in final:
                    eng.wait_ge(s, v)

        with nc.Block() as block:
            @block.tensor
            def _(eng):
                run("pe", eng)

            @block.scalar
            def _(eng):
                run("act", eng)

            @block.vector
            def _(eng):
                run("dve", eng)

            @block.gpsimd
            def _(eng):
                run("pool", eng)

            @block.sync
            def _(eng):
                run("sp", eng)
        for s in reversed(ctx):
            s.__exit__(None, None, None)


V_N1, V_NM, V_N2, V_NF = 0, 8, 16, 24
V_PS = 32
V_CW = 36
V_CB = 84
V_DTB = 96
V_ALOG = 104
V_DSK = 112
V_SSN = 120
NV = 128
R_DTB, R_ALOG, R_DSK, R_SSN, R_INV = 0, 16, 32, 48, 48 + 1024
NROWS = R_INV + 4 * MT


def build_program(dbg=None):
    nc = bass.Bass("TRN2", target_bir_lowering=False)
    P = Prog(nc)

    def din(name, shape):
        return nc.dram_tensor(name, list(shape), F32, kind="ExternalInput").ap()

    def dout(name, shape):
        return nc.dram_tensor(name, list(shape), F32, kind="ExternalOutput").ap()

    xT = din("xT", [KC, 128, NT])
    vecs_d = din("vecs", [128, NV])
    rows_d = din("rows", [1, NROWS])
    wg_d = [din("wg1", [FC, 128, D]), din("wg2", [FC, 128, D])]
    wu_d = [din("wu1", [FC, 128, D]), din("wu2", [FC, 128, D])]
    wd_d = [din("wd1", [G_FFN, KC, 128, FG * 128]), din("wd2", [G_FFN, KC, 128, FG * 128])]
    win_d = din("win", [128, KC, IN_DIM])
    wout_d = din("wout", [128, 12, D])
    wpool_d = din("wpool", [128, 4, 128])
    spool_d = din("spool", [128, 4 * NS * 15])
    sconv_d = din("sconv", [128, 12 * NS * 3])
    sssm_d = din("sssm", [128, 8, NS * 128])

    yT = dout("yT", [KC, 128, NT])
    poolp_o = dout("poolp", [128, 4 * 15])
    convp_o = dout("convp", [128, 12 * 3])
    ssmp_o = dout("ssmp", [128, 1024])
    pools_o = dout("pools", [128, 4 * NS * 15])
    convs_o = dout("convs", [128, 12 * NS * 3])
    ssms_o = dout("ssms", [128, 8, NS * 128])

    ARENA = 210432
    arena = nc.alloc_sbuf_tensor("arena", [128, ARENA], U8)
    cur = [0]

    def alloc(shape, dt):
        n = int(np.prod(shape[1:])) * (4 if dt == F32 else 2)
        n = (n + 63) // 64 * 64
        assert cur[0] + n <= ARENA, ("arena overflow", cur[0], n)
        a = arena[:, cur[0]:cur[0] + n].bitcast(dt)
        nel = int(np.prod(shape[1:]))
        a = a[:, 0:nel]
        cur[0] += n
        if len(shape) == 3:
            a = a.rearrange("p (a b) -> p a b", b=shape[2])
        elif len(shape) == 4:
            a = a.rearrange("p (a b c) -> p a b c", b=shape[2], c=shape[3])
        return a

    H = alloc([128, KC, NT], F32)
    vecs = alloc([128, NV], F32)
    rows = alloc([128, NROWS], F32)
    A_b = alloc([128, 16], F32)
    ones_bf = alloc([128, 128], BF16)
    ident_bf = alloc([128, 128], BF16)
    ident_f = alloc([128, 128], F32)
    tri_f = alloc([128, 128], F32)
    mask4 = alloc([128, 512], BF16)
    phase_base = cur[0]

    PS = [nc.alloc_psum_tensor("ps%d" % i, [128, 512], F32) for i in range(6)]
    PR = nc.alloc_psum_tensor("pr", [128, 1024], F32)

    def bank_fn(k):
        name = k[0] if isinstance(k, tuple) else k
        if not isinstance(name, str):
            return None
        if name == "pg":
            return k[1]
        if name == "pu":
            return 2 + k[1]
        if name == "py":
            return 4 + k[1]
        if name == "pstat":
            return 6
        if name == "B0":
            return k[1]
        if name == "B2":
            return 2
        if name.startswith("B3"):
            return 3
        if name.startswith("B4"):
            return 4
        if name == "B7":
            return 5
        if name == "R":
            return 6 if k[1] < 4 else 7
        if name == "RB":
            return (6, 7, 1, 2)[k[1]]
        return None
    P.bank_fn = bank_fn

    def mm(out, lhsT, rhs, start, stop, r, w):
        P.add("pe", lambda e: e.matmul(out, lhsT=lhsT, rhs=rhs, start=start, stop=stop), r, w)

    def trp(out, in_, r, w):
        P.add("pe", lambda e: e.transpose(out, in_, ident_bf), list(r) + ["ident_bf"], w)

    def actf(out, in_, func, r, w, bias=None, scale=None, accum=None):
        kw = {}
        if bias is not None:
            kw["bias"] = bias
        if scale is not None:
            kw["scale"] = scale
        if accum is not None:
            kw["accum_out"] = accum
        P.add("act", lambda e: e.activation(out=out, in_=in_, func=func, **kw), r, w)

    def tt(eng, out, a, b, op, r, w):
        P.add(eng, lambda e: e.tensor_tensor(out=out, in0=a, in1=b, op=op), r, w)

    def ts(eng, out, a, s1, op0, r, w, s2=None, op1=None):
        if op1 is None:
            P.add(eng, lambda e: e.tensor_scalar(out=out, in0=a, scalar1=s1, scalar2=None, op0=op0), r, w)
        else:
            P.add(eng, lambda e: e.tensor_scalar(out=out, in0=a, scalar1=s1, scalar2=s2, op0=op0, op1=op1), r, w)

    def stt(out, a, s, b, op0, op1, r, w):
        P.add("dve", lambda e: e.scalar_tensor_tensor(out=out, in0=a, scalar=s, in1=b, op0=op0, op1=op1), r, w)

    def cp(eng, out, in_, r, w):
        if eng == "act":
            P.add("act", lambda e: e.copy(out=out, in_=in_), r, w)
        else:
            P.add(eng, lambda e: e.tensor_copy(out=out, in_=in_), r, w)

    def dma(q, out, in_, r, w):
        P.add(q, lambda e: e.dma_start(out=out, in_=in_), r, w, dma=True)

    ndump = [0]

    def dump(name, ap, key):
        if dbg is None or name not in dbg:
            return
        shape = list(ap.shape)
        o = dout("dbg_" + name, shape)
        dma("sp", o, ap, [key] if not isinstance(key, list) else key, [])

    def hkeys(ks, off, n):
        q0, q1 = off // 256, (off + n - 1) // 256
        return [("h", k, q) for k in ks for q in range(q0, q1 + 1)]

    XH0 = 3 * TT[0][1]
    for (c0, c1) in ((0, XH0), (XH0, NT)):
        for k in range(KC):
            dma("sp", H[:, k, c0:c1], xT[k][:, c0:c1], [], hkeys([k], c0, c1 - c0))
    dma("sp", vecs, vecs_d, [], ["vecs"])
    dma("sp", rows, rows_d.partition_broadcast(128).rearrange("p a n -> p (a n)"), [], ["rows"])
    actf(A_b, rows[:, R_ALOG:R_ALOG + 16], AF.Exp, ["rows"], ["A_b"])
    ts("dve", A_b, A_b, -1.0, ALU.mult, ["A_b"], ["A_b"])
    P.add("pool", lambda e: e.memset(ones_bf, 1.0), [], ["ones_bf"])
    P.add("pool", lambda e: e.memset(ident_bf, 1.0), [], ["ident_bf"])
    P.add("pool", lambda e: e.affine_select(out=ident_bf, in_=ident_bf, pattern=[[1, 128]], compare_op=ALU.is_equal,
                                            fill=0.0, base=0, channel_multiplier=-1), ["ident_bf"], ["ident_bf"])
    P.add("pool", lambda e: e.memset(ident_f, 1.0), [], ["ident_f"])
    P.add("pool", lambda e: e.affine_select(out=ident_f, in_=ident_f, pattern=[[1, 128]], compare_op=ALU.is_equal,
                                            fill=0.0, base=0, channel_multiplier=-1), ["ident_f"], ["ident_f"])
    P.add("pool", lambda e: e.memset(tri_f, 1.0), [], ["tri_f"])
    P.add("pool", lambda e: e.affine_select(out=tri_f, in_=tri_f, pattern=[[1, 128]], compare_op=ALU.is_ge,
                                            fill=0.0, base=0, channel_multiplier=-1), ["tri_f"], ["tri_f"])
    P.add("pool", lambda e: e.memset(mask4, NEG), [], ["mask4"])
    P.add("pool", lambda e: e.affine_select(out=mask4, in_=mask4, pattern=[[0, 4], [-1, 128]], compare_op=ALU.is_gt,
                                            fill=0.0, base=0, channel_multiplier=1), ["mask4"], ["mask4"])

    def rmsnorm(gcol, off, n, sqb, rs, pstat, pkey, out_fn, out_keys_fn, extra_w=()):
        hk = hkeys(range(KC), off, n)
        actf(sqb[:, :, 0:n], H[:, :, off:off + n], AF.Square, hk, ["sqb"] + list(extra_w))
        for k in range(KC):
            mm(pstat[:, 0:n], ones_bf, sqb[:, k, 0:n], k == 0, k == KC - 1, ["ones_bf", "sqb"], [pkey])
        actf(rs[:, 0:n], pstat[:, 0:n], AF.Ln, [pkey], ["rs"], bias=EPS, scale=1.0 / D)
        actf(rs[:, 0:n], rs[:, 0:n], AF.Exp, ["rs"], ["rs"], scale=-0.5)
        for k in range(KC):
            stt(out_fn(k), H[:, k, off:off + n], vecs[:, gcol + k:gcol + k + 1], rs[:, 0:n], ALU.mult, ALU.mult,
                hkeys([k], off, n) + ["vecs", "rs"], out_keys_fn(k))

    WBLK = [(0, 512), (3072, 3088), (1536, 2304), (2304, 3072), (512, 1024), (1024, 1536)]

    def wkey(col):
        for bi, (c0, c1) in enumerate(WBLK):
            if c0 <= col < c1:
                return ("winb", bi)
        raise AssertionError(col)

    def ffn_phase(which, gcol):
        P.barrier()
        cur[0] = phase_base
        hn = alloc([128, KC, NT], BF16)
        actT = alloc([128, FG, NT], BF16)
        NWB = 3
        wgb = [alloc([128, KC, 128], BF16) for _ in range(NWB)]
        wub = [alloc([128, KC, 128], BF16) for _ in range(NWB)]
        wdb = [alloc([128, FG, 128], BF16) for _ in range(NWB)]
        sgt = [alloc([128, 512], F32) for _ in range(2)]
        sqb = alloc([128, KC, 512], BF16)
        rs = alloc([128, 512], F32)
        pg, pu, py = [PS[0], PS[1]], [PS[2], PS[3]], [PS[4], PS[5]]
        pstat = PR[:, 0:512]

        def norm_tile(ti):
            off, n = TT[ti]
            rmsnorm(gcol, off, n, sqb, rs, pstat, "pstat",
                    lambda k, off=off, n=n: hn[:, k, off:off + n],
                    lambda k, ti=ti: [("hn", k, ti)])
        norm_tile(0)
        norm_tile(1)

        cnt = [0, 0]
        for g in range(G_FFN):
            for fi in range(FG):
                f = g * FG + fi
                b = f % NWB
                dma("pool", wgb[b].rearrange("p k c -> p (k c)"), wg_d[which][f], [], [("wg", b)])
                dma("pool", wub[b].rearrange("p k c -> p (k c)"), wu_d[which][f], [], [("wu", b)])
                for ti, (off, n) in enumerate(TT):
                    if f == 0 and ti + 2 < len(TT):
                        norm_tile(ti + 2)
                    pb = cnt[0] % 2
                    cnt[0] += 1
                    for k in range(KC):
                        mm(pg[pb][:, 0:n], wgb[b][:, k, :], hn[:, k, off:off + n], k == 0, k == KC - 1,
                           [("wg", b), ("hn", k, ti)], [("pg", pb)])
                    for k in range(KC):
                        mm(pu[pb][:, 0:n], wub[b][:, k, :], hn[:, k, off:off + n], k == 0, k == KC - 1,
                           [("wu", b), ("hn", k, ti)], [("pu", pb)])
                    actf(sgt[pb][:, 0:n], pg[pb][:, 0:n], AF.Silu, [("pg", pb)], [("sg", pb)])
                    tt("dve", actT[:, fi, off:off + n], sgt[pb][:, 0:n], pu[pb][:, 0:n], ALU.mult,
                       [("sg", pb), ("pu", pb)], [("act", fi, ti)])
            for d in range(KC):
                b = (g * KC + d) % NWB
                dma("pool", wdb[b].rearrange("p f c -> p (f c)"), wd_d[which][g, d], [], [("wd", b)])
                for ti, (off, n) in enumerate(TT):
                    pb = cnt[1] % 2
                    cnt[1] += 1
                    for fi in range(FG):
                        mm(py[pb][:, 0:n], wdb[b][:, fi, :], actT[:, fi, off:off + n], fi == 0, fi == FG - 1,
                           [("wd", b), ("act", fi, ti)], [("py", pb)])
                    hk = hkeys([d], off, n)
                    stt(H[:, d, off:off + n], py[pb][:, 0:n], 0.5, H[:, d, off:off + n], ALU.mult, ALU.add,
                        [("py", pb)] + hk, hk)

    def mixer_phase():
        P.barrier()
        cur[0] = phase_base
        win = alloc([128, KC, IN_DIM], BF16)
        wout = alloc([128, 12, D], BF16)
        wpool = alloc([128, 4, 128], BF16)
        for bi, (c0, c1) in enumerate(WBLK):
            dma("pool", win[:, :, c0:c1], win_d[:, :, c0:c1], [], [("winb", bi)])
            if bi == 0:
                dma("pool", wpool.rearrange("p g c -> p (g c)"), wpool_d.rearrange("p g c -> p (g c)"), [], ["wpool"])
        for j in range(12):
            dma("pool", wout[:, j, :], wout_d[:, j, :], [], [("wout", j)])
        WIN = [("win", k) for k in range(KC)]
        WOUT = [("wout", j) for j in range(12)]
        tile_base = cur[0]

        hnm = alloc([128, KC, MT], BF16)
        rs = alloc([128, MT], F32)
        U = alloc([128, 4, 15 + MT], F32)
        pooled = alloc([128, 4, MT], BF16)
        X = [alloc([128, 3 + MT], F32) for _ in range(2)]
        XH = alloc([128, 12, 3], F32)
        acc = [alloc([128, MT], F32) for _ in range(2)]
        xbcT = alloc([128, 12, MT], BF16)
        sqb = xbcT.rearrange("p a b -> p (a b)")[:, 0:KC * MT].rearrange("p (a b) -> p a b", b=MT)
        catT = alloc([128, 12, MT], BF16)
        NCH = MT // 128
        sm = alloc([128, 14, NCH * 16], F32)
        dtr, e1, dt_t, lndt, dtA, Acs, eAcs, bias_t, tw, wdec, cd, rr1, rr2 = [sm[:, i, :] for i in range(13)]
        dts = alloc([128, 3, NCH * 16], BF16)
        tri_bf = alloc([128, 128], BF16)
        ssq = alloc([128, 2, 2], F32)
        mhalf = alloc([128, 1], F32)
        sT = alloc([128, 128], F32)
        Dt = [alloc([128, 128], F32) for _ in range(4)]
        MTt = [alloc([128, 128], BF16) for _ in range(8)]
        xtokb = [alloc([128, 512], BF16) for _ in range(2)]
        Btokb = [alloc([128, 128], BF16) for _ in range(2)]
        xw = alloc([128, 512], BF16)
        szb = [alloc([128, 512], F32) for _ in range(2)]
        xdb = alloc([128, 512], BF16)
        tqb = [alloc([128, 512], F32) for _ in range(2)]
        yn = alloc([128, 512], BF16)
        hT = alloc([128, 1024], F32)
        hTb = alloc([128, 1024], BF16)
        prompt_end = cur[0]
        print("arena: phase_base", phase_base, "tile_base", tile_base, "prompt_end", prompt_end)

        B0 = [PS[0][:, :], PS[1][:, :]]
        B2, B3, B7 = PS[2], PS[3], PS[5]
        B3bf = PS[3][:, :].bitcast(BF16)
        B4bf = PS[4][:, :].bitcast(BF16)
        b0cnt = [0]

        def b0next():
            i = b0cnt[0] % 2
            b0cnt[0] += 1
            return B0[i], ("B0", i)

        BP = [(PS[0][:, :], ("B0", 0)), (PS[1][:, :], ("B0", 1)), (PS[2][:, :], "B2"), (PS[5][:, :], "B7")]
        bpcnt = [0]

        def bpnext():
            i = bpcnt[0] % len(BP)
            bpcnt[0] += 1
            return BP[i]

        P.add("dve", lambda e: e.memset(U[:, :, 0:15], 0.0), [], ["U"])
        P.add("dve", lambda e: e.memset(XH, 0.0), [], ["XH"])
        P.add("dve", lambda e: e.memset(hT, 0.0), [], ["hT0", "hT1"])
        P.add("pool", lambda e: e.memset(hTb, 0.0), [], ["hTb0", "hTb1"])

        P.add("pool", lambda e: e.memset(mhalf, -0.5), [], ["mhalf"])
        cp("pool", tri_bf, tri_f, ["tri_f"], ["tri_bf"])
        bc64 = lambda ap8: ap8.unsqueeze(2).to_broadcast([128, 8, 64])
        v64 = lambda ap: ap.rearrange("p (h q) -> p h q", q=64)

        def uctx(cl, g, ui):
            ub = ui % 2
            return dict(cl=cl, g=g, ub=ub, cs=cl * 128, hs=slice(cl * 16 + g * 8, cl * 16 + g * 8 + 8),
                        BTc=xbcT[:, 8 + g, cl * 128:cl * 128 + 128], CTc=xbcT[:, 10 + g, cl * 128:cl * 128 + 128],
                        xtok=xtokb[ub], Btok=Btokb[ub], sz=szb[ub], tq=tqb[ub], zk=[None, None])

        RK = [("R", hh) for hh in range(8)]

        def s1_pe(u):
            cl, g, cs = u["cl"], u["g"], u["cs"]
            for b in range(2):
                bank = PR[:, b * 512:(b + 1) * 512]
                mm(bank, ident_bf, mask4, True, False, ["ident_bf", "mask4"], [("R", b * 4)])
                for hq in range(4):
                    hh = b * 4 + hq
                    col = cl * 16 + g * 8 + hh
                    for part in range(3):
                        mm(PR[:, hh * 128:(hh + 1) * 128], dts[:, part, col:col + 1].to_broadcast([128, 128]), tri_bf,
                           False, hq == 3 and part == 2, ["dts", "tri_bf"], [("R", hh)])
            mm(B3[:, 128:256], u["BTc"], u["CTc"], True, True, [("xbc", 8 + g), ("xbc", 10 + g)], ["B3sc"])
            for i in range(4):
                trp(B4bf[:, i * 128:(i + 1) * 128], xbcT[:, g * 4 + i, cs:cs + 128], [("xbc", g * 4 + i)], ["B4x"])
            trp(B3bf[:, 512:640], u["BTc"], [("xbc", 8 + g)], ["B3bt"])
            psz, kz = b0next()
            u["zk"] = [psz, kz]
            for k in range(KC):
                mm(psz, hnm[:, k, cs:cs + 128], win[:, k, 512 + g * 512:1024 + g * 512],
                   k == 0, k == KC - 1, [("hnm", k), wkey(512 + g * 512)], [kz])
            mm(B2[:, :], u["CTc"], hTb[:, g * 512:(g + 1) * 512], True, True, [("xbc", 10 + g), "hTb%d" % g], ["B2"])

        def s1_elem(u):
            g, ub, hs = u["g"], u["ub"], u["hs"]
            r127 = PR[:, 127:1024:128]
            tt("dve", tw[:, hs], r127, Acs[:, hs], ALU.subtract, RK + ["Acs"], [("tw", ub)])
            actf(cd[:, hs], r127, AF.Exp, RK, [("cd", ub)])
            actf(wdec[:, hs], tw[:, hs], AF.Exp, [("tw", ub)], [("wdec", ub)])
            tt("dve", wdec[:, hs], wdec[:, hs], dt_t[:, hs], ALU.mult, [("wdec", ub), "dt"], [("wdec", ub)])
            cp("act", sT, B3[:, 128:256], ["B3sc"], ["sT"])
            cp("act", u["xtok"], B4bf[:, 0:512], ["B4x"], [("xtok", ub)])
            cp("act", u["Btok"], B3bf[:, 512:640], ["B3bt"], [("Btok", ub)])
            psz, kz = u["zk"]
            actf(u["sz"], psz, AF.Tanh, [kz], [("sz", ub)], scale=0.5)
            stt(u["sz"], u["sz"], 1.0, psz, ALU.add, ALU.mult, [("sz", ub), kz], [("sz", ub)])
            tt("pool", v64(xdb), v64(u["xtok"]), bc64(rows[:, R_DSK + g * 8:R_DSK + g * 8 + 8]), ALU.mult,
               [("xtok", ub), "rows"], ["xd"])

        def s1_tail(u):
            cl, g, ub, hs = u["cl"], u["g"], u["ub"], u["hs"]
            xtok, tq = u["xtok"], u["tq"]
            mm(B7[:, :], ident_bf, xdb, True, False, ["ident_bf", "xd"], ["B7"])
            for hh in range(8):
                col = cl * 16 + g * 8 + hh
                Dh = Dt[hh % 4]
                Mh = MTt[hh]
                actf(Dh, PR[:, hh * 128:(hh + 1) * 128], AF.Exp, [("R", hh), "bias"], [("D", hh % 4)],
                     bias=bias_t[:, col:col + 1], scale=1.0)
                tt("pool" if hh % 2 == 0 else "dve", Mh, Dh, sT, ALU.mult, [("D", hh % 4), "sT"], [("M", hh)])
                mm(B7[:, hh * 64:(hh + 1) * 64], Mh, xtok[:, hh * 64:(hh + 1) * 64], False, hh == 7,
                   [("M", hh), ("xtok", ub)], ["B7"])
            tt("dve", v64(tq), v64(B2[:, :]), bc64(eAcs[:, hs]), ALU.mult, ["B2", "eAcs"], [("tq", ub)])
            tt("dve", tq, tq, B7[:, :], ALU.add, [("tq", ub), "B7"], [("tq", ub)])

        def s2_front(u):
            g, ub = u["g"], u["ub"]
            tq, sz = u["tq"], u["sz"]
            sq = ssq[:, ub, :]
            stt(tq, tq, 0.5, sz, ALU.mult, ALU.mult, [("tq", ub), ("sz", ub)], [("tq", ub)])
            actf(yn, tq, AF.Square, [("tq", ub)], ["yn", ("ssq", ub)], accum=sq[:, 0:1])
            ts("pool", sq[:, 1:2], sq[:, 0:1], 1.0 / 512, ALU.mult, [("ssq", ub)], [("ssq", ub)], s2=EPS, op1=ALU.add)
            tt("pool", sq[:, 1:2], sq[:, 1:2], mhalf, ALU.pow, [("ssq", ub), "mhalf"], [("ssq", ub)])
            stt(yn, tq, sq[:, 1:2], rows[:, R_SSN + g * 512:R_SSN + (g + 1) * 512], ALU.mult, ALU.mult,
                [("tq", ub), ("ssq", ub), "rows"], ["yn"])

        def s2_mid(u):
            g, ub, cs, hs = u["g"], u["ub"], u["cs"], u["hs"]
            for i in range(4):
                trp(B4bf[:, 512 + i * 128:512 + (i + 1) * 128], yn[:, i * 128:(i + 1) * 128], ["yn"], ["B4y"])
            cp("act", catT[:, 4 + g * 4:8 + g * 4, cs:cs + 128],
               B4bf[:, 512:1024].rearrange("p (a b) -> p a b", b=128), ["B4y"],
               [("cat", 4 + g * 4 + i) for i in range(4)])

        def s2_state(u):
            g, ub, cs, hs = u["g"], u["ub"], u["cs"], u["hs"]
            tt("pool", v64(xw), v64(u["xtok"]), bc64(wdec[:, hs]), ALU.mult, [("xtok", ub), ("wdec", ub)], ["xw"])
            pst, kst = b0next()
            mm(pst, u["Btok"], xw, True, True, [("Btok", ub), "xw"], [kst])
            hg = hT[:, g * 512:(g + 1) * 512]
            tt("pool", v64(hg), v64(hg), bc64(cd[:, hs]), ALU.mult, ["hT%d" % g, ("cd", ub)], ["hT%d" % g])
            tt("dve", hg, hg, pst, ALU.add, ["hT%d" % g, kst], ["hT%d" % g])
            cp("act", hTb[:, g * 512:(g + 1) * 512], hg, ["hT%d" % g], ["hTb%d" % g])

        def tile_norm(ti):
            ps0, k0 = b0next()
            rmsnorm(V_NM, ti * MT, MT, sqb, rs, ps0[:, 0:MT], k0,
                    lambda k: hnm[:, k, :], lambda k: [("hnm", k)],
                    extra_w=[("xbc", c) for c in range(8)])

        def tile_dt():
            for cl in range(NCH):
                for k in range(KC):
                    mm(B3[:, cl * 16:(cl + 1) * 16], hnm[:, k, cl * 128:(cl + 1) * 128], win[:, k, 3072:3088],
                       k == 0, k == KC - 1, [("hnm", k), wkey(3072)], ["B3dt"])
            v16 = lambda ap: ap.rearrange("p (c h) -> p c h", h=16)
            tt("dve", v16(dtr), v16(B3[:, 0:NCH * 16]),
               rows[:, R_DTB:R_DTB + 16].unsqueeze(1).to_broadcast([128, NCH, 16]), ALU.add, ["B3dt", "rows"], ["dtr"])
            actf(e1, dtr, AF.Exp, ["dtr"], ["e1"])
            actf(dt_t, e1, AF.Ln, ["e1"], ["dt"], bias=1.0, scale=1.0)
            actf(lndt, dt_t, AF.Ln, ["dt"], ["lndt"])
            tt("dve", v16(dtA), v16(dt_t), A_b.unsqueeze(1).to_broadcast([128, NCH, 16]), ALU.mult, ["dt", "A_b"], ["dtA"])
            mm(B3[:, 64:64 + NCH * 16], tri_f, dtA, True, True, ["tri_f", "dtA"], ["B3acs"])
            cp("pool", dts[:, 0, :], dtA, ["dtA"], ["dts"])
            tt("pool", rr1, dtA, dts[:, 0, :], ALU.subtract, ["dtA", "dts"], ["rr1"])
            cp("pool", dts[:, 1, :], rr1, ["rr1"], ["dts"])
            tt("pool", rr2, rr1, dts[:, 1, :], ALU.subtract, ["rr1", "dts"], ["rr2"])
            cp("pool", dts[:, 2, :], rr2, ["rr2"], ["dts"])
            cp("dve", Acs, B3[:, 64:64 + NCH * 16], ["B3acs"], ["Acs"])
            actf(eAcs, Acs, AF.Exp, ["Acs"], ["eAcs"])
            tt("dve", bias_t, lndt, Acs, ALU.subtract, ["lndt", "Acs"], ["bias"])
        _laoff = 12 * MT // 2 - 4 * (15 + MT)
        LA = xbcT.rearrange("p a b -> p (a b)").bitcast(F32)[:, _laoff:_laoff + 4 * (15 + MT)].rearrange("p (g e) -> p g e", e=15 + MT)
        assert tqb[1].offset == tqb[0].offset + 512
        LB = arena[:, tqb[0].offset * 4:tqb[0].offset * 4 + 4096].bitcast(F32)[:, 0:3 * (15 + MT)].rearrange("p (g e) -> p g e", e=15 + MT)
        LAK = ["LA", "sqb"] + [("xbc", c) for c in range((_laoff * 2) // MT, 12)]
        LBK = ["LB", ("tq", 0), ("tq", 1)]

        def pool_inproj():
            for gi in range(4):
                ps0, k0 = b0next()
                for k in range(KC):
                    mm(ps0[:, 0:MT], win[:, k, gi * 128:(gi + 1) * 128], hnm[:, k, :], k == 0, k == KC - 1,
                       [wkey(gi * 128), ("hnm", k)], [k0])
                cp("act", U[:, gi, 15:15 + MT], ps0[:, 0:MT], [k0], ["U"])

        def outproj(tj, d):
            ps0, k0 = bpnext()
            for j in range(12):
                mm(ps0[:, 0:MT], wout[:, j, d * 128:(d + 1) * 128], catT[:, j, :], j == 0, j == 11,
                   [("wout", j), ("cat", j)], [k0])
            hk = hkeys([d], tj * MT, MT)
            tt("dve", H[:, d, tj * MT:(tj + 1) * MT], ps0[:, 0:MT], H[:, d, tj * MT:(tj + 1) * MT], ALU.add,
               [k0] + hk, hk)

        NTILE = TP // MT
        for ti in range(NTILE):
            t0 = ti * MT
            if ti == 0:
                tile_norm(0)
            if ti == 0:
                tile_dt()
            if ti == 0:
                pool_inproj()
            E = 15 + MT
            tt("dve", LA[:, :, 1:E], U[:, :, 1:E], U[:, :, 0:E - 1], ALU.add, ["U"], LAK)
            tt("dve", LB[:, :, 3:E], LA[:, 1:4, 3:E], LA[:, 1:4, 1:E - 2], ALU.add, LAK, LBK + LAK)
            tt("dve", LA[:, 2:4, 7:E], LB[:, 1:3, 7:E], LB[:, 1:3, 3:E - 4], ALU.add, LBK, LAK + LBK)
            tt("dve", LB[:, 2, 15:E], LA[:, 3, 15:E], LA[:, 3, 7:E - 8], ALU.add, LAK, LBK + LAK)
            res = [LA[:, 0, 15:E], LB[:, 0, 15:E], LA[:, 2, 15:E], LB[:, 2, 15:E]]
            for gi in range(4):
                w = 2 << gi
                if ti == 0:
                    tt("dve", res[gi], res[gi], rows[:, R_INV + gi * MT:R_INV + (gi + 1) * MT], ALU.mult,
                       ["rows"], LAK + LBK)
                    tt("dve", pooled[:, gi, :], res[gi], U[:, gi, 15:E], ALU.subtract, LAK + LBK + ["U"],
                       [("pooled", gi)])
                else:
                    stt(pooled[:, gi, :], res[gi], 1.0 / w, U[:, gi, 15:E], ALU.mult, ALU.subtract,
                        LAK + LBK + ["U"], [("pooled", gi)])
            for c in range(12):
                ps0, k0 = bpnext()
                col = 1536 + c * 128
                for k in range(KC):
                    mm(ps0[:, 0:MT], win[:, k, col:col + 128], hnm[:, k, :], k == 0, k == KC - 1,
                       [wkey(col), ("hnm", k)], [k0])
                xb = c % 2
                Xc = X[xb]
                cp("pool", Xc[:, 0:3], XH[:, c, :], ["XH"], [("X", xb)])
                cp("act", Xc[:, 3:3 + MT], ps0[:, 0:MT], [k0], [("X", xb)])
                cp("dve", XH[:, c, :], Xc[:, MT:MT + 3], [("X", xb)], ["XH"])
                a = acc[xb]
                cw = lambda tap: vecs[:, V_CW + tap * 12 + c:V_CW + tap * 12 + c + 1]
                actf(a, Xc[:, 0:MT], AF.Identity, [("X", xb), "vecs"], [("acc", xb)],
                     bias=vecs[:, V_CB + c:V_CB + c + 1], scale=cw(0))
                if c > 0:
                    actf(xbcT[:, c - 1, :], acc[(c - 1) % 2], AF.Silu, [("acc", (c - 1) % 2)], [("xbc", c - 1), "sqb"])
                for tap in (1, 2, 3):
                    stt(a, Xc[:, tap:tap + MT], cw(tap), a, ALU.mult, ALU.add,
                        [("X", xb), "vecs", ("acc", xb)], [("acc", xb)])
                if ti > 0 and c < KC:
                    outproj(ti - 1, c)
            actf(xbcT[:, 11, :], acc[1], AF.Silu, [("acc", 1)], [("xbc", 11), "sqb"])
            cp("pool", U[:, :, 0:15], U[:, :, MT:MT + 15], ["U"], ["U"])
            for gi in range(4):
                ps1, k1 = b0next()
                mm(ps1[:, 0:MT], wpool[:, gi, :], pooled[:, gi, :], True, True, ["wpool", ("pooled", gi)], [k1])
                ts("dve", catT[:, gi, :], ps1[:, 0:MT], vecs[:, V_PS + gi:V_PS + gi + 1], ALU.mult, [k1, "vecs"],
                   [("cat", gi)])
            units = [uctx(cl, g, cl * 2 + g) for cl in range(MT // 128) for g in range(2)]
            s1_pe(units[0])
            s1_elem(units[0])
            s1_tail(units[0])
            for ui in range(len(units)):
                nxt = units[ui + 1] if ui + 1 < len(units) else None
                if nxt is None and ti + 1 < NTILE:
                    tile_norm(ti + 1)
                    tile_dt()
                s2_front(units[ui])
                if nxt is not None:
                    s1_pe(nxt)
                    s1_elem(nxt)
                elif ti + 1 < NTILE:
                    pool_inproj()
                s2_mid(units[ui])
                if nxt is not None:
                    s1_tail(nxt)
                s2_state(units[ui])
            if ti + 1 == NTILE:
                for d in range(KC):
                    outproj(ti, d)

        dma("sp", poolp_o.rearrange("p (g j) -> p g j", j=15), U[:, :, 0:15], ["U"], [])
        dma("sp", convp_o.rearrange("p (c j) -> p c j", j=3), XH, ["XH"], [])
        dma("sp", ssmp_o, hT, ["hT0", "hT1"], [])
        dump("hmix", H[:, :, 0:TP], hkeys(range(KC), 0, TP))

        P.barrier()
        cur[0] = tile_base
        S0 = TP
        hns = alloc([128, KC, NS], BF16)
        sqs = alloc([128, KC, NS], BF16)
        rss = alloc([128, NS], F32)
        us = alloc([128, 4, NS], F32)
        Pp = alloc([128, 4, NS, 15], F32)
        NPp = alloc([128, 4, NS, 15], F32)
        ssum = alloc([128, 4, NS], F32)
        pls = alloc([128, 4, NS], BF16)
        xsr = alloc([128, 12, NS], F32)
        CP = alloc([128, 12, NS, 3], F32)
        NCv = alloc([128, 12, NS, 3], F32)
        xsa = alloc([128, 12, NS], F32)
        ctmpb = [alloc([128, 12, NS], F32) for _ in range(3)]
        zs = alloc([128, 8, NS], F32)
        dth = alloc([128, 8, NS], F32)
        ah = alloc([128, 8, NS], F32)
        Ah = alloc([128, 8], F32)
        xdt = alloc([128, 8, NS], F32)
        ys = alloc([128, 8, NS], F32)
        ygs = alloc([128, 8, NS], F32)
        rgs = alloc([128, 2, NS], F32)
        cats = alloc([128, 12, NS], BF16)
        HB = 8
        Bb = alloc([128, NS, 128], F32)
        Cb = alloc([128, NS, 128], F32)
        H0 = [alloc([128, HB, 128], F32) for _ in range(3)]
        Hn = [alloc([128, HB, 128], F32) for _ in range(2)]
        print("arena: sample_end", cur[0])

        dma("sp", Pp.rearrange("p g b j -> p (g b j)"), spool_d, [], ["Pp"])
        dma("sp", CP.rearrange("p c b j -> p (c b j)"), sconv_d, [], ["CP"])
        rmsnorm(V_NM, S0, NS, sqs, rss, PS[0][:, 0:NS], ("B0", 0),
                lambda k: hns[:, k, :], lambda k: [("hns", k)])
        for gi in range(4):
            ps0, k0 = b0next()
            for k in range(KC):
                mm(ps0[:, 0:NS], win[:, k, gi * 128:(gi + 1) * 128], hns[:, k, :], k == 0, k == KC - 1,
                   [wkey(gi * 128), ("hns", k)], [k0])
            cp("act", us[:, gi, :], ps0[:, 0:NS], [k0], ["us"])
        for c in range(12):
            ps0, k0 = b0next()
            col = 1536 + c * 128
            for k in range(KC):
                mm(ps0[:, 0:NS], win[:, k, col:col + 128], hns[:, k, :], k == 0, k == KC - 1,
                   [wkey(col), ("hns", k)], [k0])
            cp("act", xsr[:, c, :], ps0[:, 0:NS], [k0], ["xsr"])
        for j in range(8):
            ps0, k0 = b0next()
            col = 512 + j * 128
            for k in range(KC):
                mm(ps0[:, 0:NS], win[:, k, col:col + 128], hns[:, k, :], k == 0, k == KC - 1,
                   [wkey(col), ("hns", k)], [k0])
            actf(zs[:, j, :], ps0[:, 0:NS], AF.Silu, [k0], ["zs"])
        for j in range(8):
            for h2 in range(2):
                for k in range(KC):
                    c0 = 3072 + 2 * j + h2
                    lw = win[:, k, c0:c0 + 1].to_broadcast([128, 64])
                    mm(B3[h2 * 64:(h2 + 1) * 64, j * NS:(j + 1) * NS], lw, hns[:, k, :], k == 0, k == KC - 1,
                       [wkey(3072), ("hns", k)], ["B3s"])
        B3v = B3[:, 0:8 * NS].rearrange("p (j b) -> p j b", b=NS)
        bc8 = lambda col: vecs[:, col:col + 8].unsqueeze(2).to_broadcast([128, 8, NS])
        tt("dve", dth, B3v, bc8(V_DTB), ALU.add, ["B3s", "vecs"], ["dth"])
        actf(dth, dth, AF.Exp, ["dth"], ["dth"])
        actf(dth, dth, AF.Ln, ["dth"], ["dth"], bias=1.0, scale=1.0)
        actf(Ah, vecs[:, V_ALOG:V_ALOG + 8], AF.Exp, ["vecs"], ["Ah"])
        tt("dve", ah, dth, Ah.unsqueeze(2).to_broadcast([128, 8, NS]), ALU.mult, ["dth", "Ah"], ["ah"])
        actf(ah, ah, AF.Exp, ["ah"], ["ah"], scale=-1.0)
        for gi in range(4):
            w = 2 << gi
            P.add("dve", lambda e, gi=gi, w=w: e.tensor_reduce(out=ssum[:, gi, :], in_=Pp[:, gi, :, 16 - w:15],
                                                              axis=AX.X, op=ALU.add), ["Pp"], ["ssum"])
        tt("dve", ssum, ssum, us, ALU.add, ["ssum", "us"], ["ssum"])
        for gi in range(4):
            w = 2 << gi
            stt(pls[:, gi, :], ssum[:, gi, :], 1.0 / w, us[:, gi, :], ALU.mult, ALU.subtract, ["ssum", "us"], ["pls"])
            ps0, k0 = b0next()
            mm(ps0[:, 0:NS], wpool[:, gi, :], pls[:, gi, :], True, True, ["wpool", "pls"], [k0])
            ts("dve", cats[:, gi, :], ps0[:, 0:NS], vecs[:, V_PS + gi:V_PS + gi + 1], ALU.mult, [k0, "vecs"],
               [("cats", gi)])
        cp("pool", NPp[:, :, :, 0:14], Pp[:, :, :, 1:15], ["Pp"], ["NPp"])
        cp("pool", NPp[:, :, :, 14], us, ["us"], ["NPp"])
        dma("sp", pools_o, NPp.rearrange("p g b j -> p (g b j)"), ["NPp"], [])
        cwb = lambda tap: vecs[:, V_CW + tap * 12:V_CW + tap * 12 + 12].unsqueeze(2).to_broadcast([128, 12, NS])
        tt("dve", xsa, xsr, cwb(3), ALU.mult, ["xsr", "vecs"], ["xsa"])
        for tap in range(3):
            ctmp = ctmpb[tap]
            tt("dve", ctmp, CP[:, :, :, tap], cwb(tap), ALU.mult, ["CP", "vecs"], [("ctmp", tap)])
            tt("dve", xsa, xsa, ctmp, ALU.add, ["xsa", ("ctmp", tap)], ["xsa"])
        tt("dve", xsa, xsa, vecs[:, V_CB:V_CB + 12].unsqueeze(2).to_broadcast([128, 12, NS]), ALU.add,
           ["xsa", "vecs"], ["xsa"])
        actf(xsa, xsa, AF.Silu, ["xsa"], ["xsa"])
        cp("pool", NCv[:, :, :, 0:2], CP[:, :, :, 1:3], ["CP"], ["NCv"])
        cp("pool", NCv[:, :, :, 2], xsr, ["xsr"], ["NCv"])
        dma("sp", convs_o, NCv.rearrange("p c b j -> p (c b j)"), ["NCv"], [])
        tt("dve", xdt, dth, xsa[:, 0:8, :], ALU.mult, ["dth", "xsa"], ["xdt"])
        RB = [PR[:, 0:512], PR[:, 512:1024], PS[1][:, :], PS[2][:, :]]

        def ssm_load(it):
            j, hb = it // (NS // HB), it % (NS // HB)
            src = sssm_d[:, j, hb * HB * 128:(hb + 1) * HB * 128]
            dma("sp", H0[it % 3].rearrange("p b n -> p (b n)"), src, [], [("H0", it % 3)])

        for g in range(2):
            for which, dstb, chunk in ((0, Bb, 8 + g), (1, Cb, 10 + g)):
                for q in range(NS // 4):
                    rb = RB[(which * 4 + q) % 4]
                    for bi in range(4):
                        b = q * 4 + bi
                        mm(rb[:, bi * 128:(bi + 1) * 128], xsa[:, chunk, b:b + 1].to_broadcast([128, 128]), ident_f,
                           True, True, ["xsa", "ident_f"], [("RB", (which * 4 + q) % 4)])
                    cp("act", dstb[:, q * 4:(q + 1) * 4, :], rb.rearrange("p (a n) -> p a n", n=128),
                       [("RB", (which * 4 + q) % 4)], ["Bb" if which == 0 else "Cb"])
            for j in range(g * 4, g * 4 + 4):
                for hb in range(NS // HB):
                    bs = slice(hb * HB, (hb + 1) * HB)
                    it = j * 2 + hb
                    i0, i1 = it % 3, it % 2
                    if it == 0:
                        ssm_load(0)
                        ssm_load(1)
                    if it + 2 < 8 * (NS // HB):
                        ssm_load(it + 2)
                    tt("dve", H0[i0], H0[i0], ah[:, j, bs].unsqueeze(2).to_broadcast([128, HB, 128]), ALU.mult,
                       [("H0", i0), "ah"], [("H0", i0)])
                    tt("pool", Hn[i1], Bb[:, bs, :], xdt[:, j, bs].unsqueeze(2).to_broadcast([128, HB, 128]), ALU.mult,
                       ["Bb", "xdt"], [("Hn", i1)])
                    tt("pool", Hn[i1], Hn[i1], H0[i0], ALU.add, [("Hn", i1), ("H0", i0)], [("Hn", i1)])
                    dma("sp", ssms_o[:, j, hb * HB * 128:(hb + 1) * HB * 128], Hn[i1].rearrange("p b n -> p (b n)"),
                        [("Hn", i1)], [])
                    tt("dve", H0[i0], Hn[i1], Cb[:, bs, :], ALU.mult, [("Hn", i1), "Cb"], [("H0", i0)])
                    P.add("dve", lambda e, j=j, bs=bs, i0=i0: e.tensor_reduce(out=ys[:, j, bs], in_=H0[i0], axis=AX.X,
                                                                          op=ALU.add), [("H0", i0)], ["ys"])
        tt("dve", ygs, xsa[:, 0:8, :], bc8(V_DSK), ALU.mult, ["xsa", "vecs"], ["ygs"])
        tt("dve", ygs, ygs, ys, ALU.add, ["ygs", "ys"], ["ygs"])
        tt("dve", ygs, ygs, zs, ALU.mult, ["ygs", "zs"], ["ygs"])
        tt("dve", ys, ygs, ygs, ALU.mult, ["ygs"], ["ys"])
        ones_f = H0[0].rearrange("p b n -> p (b n)")[:, 0:128]
        P.add("dve", lambda e: e.memset(ones_f, 1.0), [], [("H0", 0)])
        for g in range(2):
            for jj in range(4):
                mm(B3[:, 256 + g * NS:256 + (g + 1) * NS], ones_f, ys[:, g * 4 + jj, :], jj == 0, jj == 3,
                   [("H0", 0), "ys"], ["B3g"])
        actf(rgs, B3[:, 256:256 + 2 * NS].rearrange("p (g b) -> p g b", b=NS), AF.Sqrt, ["B3g"], ["rgs"],
             bias=EPS, scale=1.0 / 512)
        P.add("dve", lambda e: e.reciprocal(out=rgs, in_=rgs), ["rgs"], ["rgs"])
        for g in range(2):
            tt("dve", ygs[:, g * 4:g * 4 + 4, :], ygs[:, g * 4:g * 4 + 4, :],
               rgs[:, g:g + 1, :].to_broadcast([128, 4, NS]), ALU.mult, ["ygs", "rgs"], ["ygs"])
        tt("dve", cats[:, 4:12, :], ygs, bc8(V_SSN), ALU.mult, ["ygs", "vecs"], [("cats", j) for j in range(4, 12)])
        for d in range(KC):
            ps0, k0 = b0next()
            for j in range(12):
                mm(ps0[:, 0:NS], wout[:, j, d * 128:(d + 1) * 128], cats[:, j, :], j == 0, j == 11,
                   [("wout", j), ("cats", j)], [k0])
            hk = hkeys([d], S0, NS)
            tt("dve", H[:, d, S0:S0 + NS], ps0[:, 0:NS], H[:, d, S0:S0 + NS], ALU.add, [k0] + hk, hk)

    def final_phase():
        P.barrier()
        cur[0] = phase_base
        sqb = alloc([128, KC, 512], BF16)
        rs = alloc([128, 512], F32)
        yst = [alloc([128, KC, 512], F32) for _ in range(2)]
        yTv = yT.rearrange("k p t -> p k t")
        for ti, (off, n) in enumerate(TT):
            yb = yst[ti % 2]
            rmsnorm(V_NF, off, n, sqb, rs, PR[:, 0:512], "pstat",
                    lambda k, yb=yb, n=n: yb[:, k, 0:n], lambda k, ti=ti: [("yst", ti % 2)])
            dma("sp", yTv[:, :, off:off + n], yb[:, :, 0:n], [("yst", ti % 2)], [])

    ffn_phase(0, V_N1)
    dump("h1", H[:, :, :], hkeys(range(KC), 0, NT))
    mixer_phase()
    dump("h2", H[:, :, :], hkeys(range(KC), 0, NT))
    ffn_phase(1, V_N2)
    final_phase()
    P.emit()
    return nc


_NC_CACHE = {}


def _prep_shared(inp):
    f = lambda a: np.ascontiguousarray(np.asarray(a, dtype=np.float32))
    sh = {}
    for i, (g, u, d) in enumerate((("ffn1_w_gate", "ffn1_w_up", "ffn1_w_down"),
                                   ("ffn2_w_gate", "ffn2_w_up", "ffn2_w_down")), start=1):
        wg = f(inp[g])[0].reshape(KC, 128, FC, 128)
        wu = f(inp[u])[0].reshape(KC, 128, FC, 128)
        sh["wg%d" % i] = np.ascontiguousarray(wg.transpose(2, 1, 0, 3)).reshape(FC, 128, D)
        sh["wu%d" % i] = np.ascontiguousarray(wu.transpose(2, 1, 0, 3)).reshape(FC, 128, D)
        wd = f(inp[d])[0].reshape(G_FFN, FG, 128, KC, 128)
        sh["wd%d" % i] = np.ascontiguousarray(wd.transpose(0, 3, 2, 1, 4)).reshape(G_FFN, KC, 128, FG * 128)
    sh["win"] = np.ascontiguousarray(f(inp["w_in"])[0].reshape(KC, 128, IN_DIM).transpose(1, 0, 2))
    sh["wout"] = np.ascontiguousarray(f(inp["w_out"])[0].reshape(12, 128, D).transpose(1, 0, 2))
    sh["wpool"] = np.ascontiguousarray(f(inp["w_pool"])[0].transpose(1, 0, 2))
    vecs = np.zeros((128, NV), np.float32)
    col = lambda v, n: np.ascontiguousarray(f(v).reshape(n, 128).T)
    vecs[:, V_N1:V_N1 + 8] = col(inp["ffn1_norm"][0], 8)
    vecs[:, V_NM:V_NM + 8] = col(inp["mix_norm"][0], 8)
    vecs[:, V_N2:V_N2 + 8] = col(inp["ffn2_norm"][0], 8)
    vecs[:, V_NF:V_NF + 8] = col(inp["final_norm"], 8)
    vecs[:, V_PS:V_PS + 4] = col(inp["pool_scale"][0], 4)
    cw = f(inp["conv_w"])[0]
    for tap in range(4):
        vecs[:, V_CW + tap * 12:V_CW + tap * 12 + 12] = col(cw[tap], 12)
    vecs[:, V_CB:V_CB + 12] = col(inp["conv_b"][0], 12)
    hp = lambda v: np.ascontiguousarray(np.repeat(f(v).reshape(8, 2, 1), 64, axis=2).reshape(8, 128).T)
    vecs[:, V_DTB:V_DTB + 8] = hp(inp["dt_bias"][0])
    vecs[:, V_ALOG:V_ALOG + 8] = hp(inp["a_log"][0])
    vecs[:, V_DSK:V_DSK + 8] = hp(inp["d_skip"][0])
    vecs[:, V_SSN:V_SSN + 8] = col(inp["ssd_norm"][0], 8)
    sh["vecs"] = vecs
    rows = np.zeros((1, NROWS), np.float32)
    rows[0, R_DTB:R_DTB + 16] = f(inp["dt_bias"])[0]
    rows[0, R_ALOG:R_ALOG + 16] = f(inp["a_log"])[0]
    rows[0, R_DSK:R_DSK + 16] = f(inp["d_skip"])[0]
    rows[0, R_SSN:R_SSN + 1024] = f(inp["ssd_norm"])[0]
    t = np.arange(MT)
    for gi in range(4):
        rows[0, R_INV + gi * MT:R_INV + (gi + 1) * MT] = 1.0 / np.minimum(2 << gi, t + 1)
    sh["rows"] = rows
    return sh


def _run(inp, dbg=None):
    key = tuple(sorted(dbg)) if dbg else None
    if key not in _NC_CACHE:
        _NC_CACHE[key] = build_program(dbg)
    nc = _NC_CACHE[key]
    f = lambda a: np.asarray(a, dtype=np.float32)
    sh = _prep_shared(inp)
    xp, xs = f(inp["x_prompt"]), f(inp["x_sample"])
    sp, sc, ss = f(inp["state_pool"])[0], f(inp["state_conv"])[0], f(inp["state_ssm"])[0]
    in_maps = []
    for i in range(NCORES):
        sl = slice(i * NS, (i + 1) * NS)
        x = np.concatenate([xp[i], xs[sl, 0, :]], axis=0)
        m = dict(sh)
        m["xT"] = np.ascontiguousarray(x.T).reshape(KC, 128, NT)
        m["spool"] = np.ascontiguousarray(sp[sl].reshape(NS, 15, 4, 128).transpose(3, 2, 0, 1)).reshape(128, -1)
        m["sconv"] = np.ascontiguousarray(sc[sl].reshape(NS, 3, 12, 128).transpose(3, 2, 0, 1)).reshape(128, -1)
        m["sssm"] = np.ascontiguousarray(ss[sl].reshape(NS, 8, 2, 64, 128).transpose(2, 3, 1, 0, 4)).reshape(128, 8, NS * 128)
        in_maps.append(m)
    res = run_bass_kernel_spmd(nc, in_maps, core_ids=list(range(NCORES)))
    R = res.results
    y_p = np.empty((8, TP, D), np.float32)
    y_s = np.empty((128, 1, D), np.float32)
    pool_p = np.empty((1, 8, 15, 512), np.float32)
    conv_p = np.empty((1, 8, 3, 1536), np.float32)
    ssm_p = np.empty((1, 8, 16, 64, 128), np.float32)
    pool_s = np.empty((1, 128, 15, 512), np.float32)
    conv_s = np.empty((1, 128, 3, 1536), np.float32)
    ssm_s = np.empty((1, 128, 16, 64, 128), np.float32)
    for i in range(NCORES):
        r = R[i]
        sl = slice(i * NS, (i + 1) * NS)
        y = np.asarray(r["yT"]).reshape(D, NT).T
        y_p[i] = y[:TP]
        y_s[sl, 0, :] = y[TP:]
        pool_p[0, i] = np.asarray(r["poolp"]).reshape(128, 4, 15).transpose(2, 1, 0).reshape(15, 512)
        conv_p[0, i] = np.asarray(r["convp"]).reshape(128, 12, 3).transpose(2, 1, 0).reshape(3, 1536)
        ssm_p[0, i] = np.asarray(r["ssmp"]).reshape(128, 16, 64).transpose(1, 2, 0)
        pool_s[0, sl] = np.asarray(r["pools"]).reshape(128, 4, NS, 15).transpose(2, 3, 1, 0).reshape(NS, 15, 512)
        conv_s[0, sl] = np.asarray(r["convs"]).reshape(128, 12, NS, 3).transpose(2, 3, 1, 0).reshape(NS, 3, 1536)
        ssm_s[0, sl] = np.asarray(r["ssms"]).reshape(2, 64, 8, NS, 128).transpose(3, 2, 0, 1, 4).reshape(NS, 16, 64, 128)
    outs = (y_p, y_s, pool_p, conv_p, ssm_p, pool_s, conv_s, ssm_s)
    if dbg:
        return outs, [{k: np.asarray(v) for k, v in r.items() if k.startswith("dbg_")} for r in R]
    return outs


def kernel(**inputs):
    return _run(inputs)
```

```python
import numpy as np
import concourse.bass as bass
import concourse.mybir as mybir
from concourse.bass_utils import run_bass_kernel_spmd

F32 = mybir.dt.float32
BF16 = mybir.dt.bfloat16
U8 = mybir.dt.uint8
ALU = mybir.AluOpType
AF = mybir.ActivationFunctionType
AX = mybir.AxisListType

NCORES = 8
D = 1024
KC = 8
TP = 2048
NS = 16
NT = TP + NS
FF = 2816
FC = 22
G_FFN = 2
FG = FC // G_FFN
IN_DIM = 3088
EPS = 1e-6
TT = [(i * 344, 344) for i in range(6)]
MT = 256
NEG = -30000.0

COMPUTE = ("pe", "act", "dve", "pool")
ENGS = ("pe", "act", "dve", "pool", "sp")
NDMA_SEM = 24


class Op:
    __slots__ = ("eng", "fn", "idx", "waits", "signal", "sigval", "is_dma",
                 "slot", "slotval", "clock", "dma_known")

    def __init__(self, eng, fn, is_dma=False):
        self.eng = eng
        self.fn = fn
        self.is_dma = is_dma
        self.waits = []
        self.signal = False
        self.sigval = None
        self.slot = None
        self.slotval = None


class Prog:
    def __init__(self, nc):
        self.nc = nc
        self.ops = {e: [] for e in ENGS}
        self.clock = {e: {} for e in ENGS}
        self.dma_known = {e: set() for e in ENGS}
        self.last_w = {}
        self.readers = {}
        self.dma_slot_last = [None] * NDMA_SEM
        self.dma_slot_cnt = [0] * NDMA_SEM
        self.dma_next = 0
        self.dma_next_sw = 0
        self.pending = {e: [] for e in ENGS}
        self.bank_fn = lambda k: None

    def _need(self, eng, dep, raw):
        if dep.is_dma:
            return dep not in self.dma_known[eng]
        if dep.eng == eng:
            if eng == "pe" or eng == "sp":
                return False
            if not raw:
                return False
        return self.clock[eng].get(dep.eng, -1) < dep.idx

    def _learn(self, eng, dep):
        ck = self.clock[eng]
        for e2, i2 in dep.clock.items():
            if ck.get(e2, -1) < i2:
                ck[e2] = i2
        self.dma_known[eng] |= dep.dma_known
        if dep.is_dma:
            self.dma_known[eng].add(dep)
        elif ck.get(dep.eng, -1) < dep.idx:
            ck[dep.eng] = dep.idx

    def barrier(self):
        lasts = []
        for e in COMPUTE:
            if self.ops[e]:
                lasts.append(self.ops[e][-1])
        dmas = [d for d in self.dma_slot_last if d is not None]
        for e in ENGS:
            self.pending[e] = [(d, True) for d in lasts + dmas]

    def add(self, eng, fn, reads=(), writes=(), dma=False):
        op = Op(eng, fn, is_dma=dma)
        op.idx = len(self.ops[eng])
        banks = set()
        for k in list(reads) + list(writes):
            b = self.bank_fn(k)
            if b is not None:
                banks.add(("bank", b))
        writes = list(writes) + list(banks)
        deps = list(self.pending[eng])
        self.pending[eng] = []
        for k in reads:
            w = self.last_w.get(k)
            if w is not None:
                deps.append((w, True))
        for k in writes:
            w = self.last_w.get(k)
            if w is not None:
                deps.append((w, False))
            for r in self.readers.get(k, ()):
                deps.append((r, False))
        if dma:
            half = NDMA_SEM // 2
            if eng == "pool":
                slot = half + self.dma_next_sw % half
                self.dma_next_sw += 1
            else:
                slot = self.dma_next % half
                self.dma_next += 1
            prev = self.dma_slot_last[slot]
            if prev is not None:
                deps.append((prev, True))
            op.slot = slot
            self.dma_slot_cnt[slot] += 16
            op.slotval = self.dma_slot_cnt[slot]
            self.dma_slot_last[slot] = op
        seen = set()
        for dep, raw in sorted(deps, key=lambda t: -int(t[1])):
            if dep is op or id(dep) in seen:
                continue
            if self._need(eng, dep, raw):
                seen.add(id(dep))
                op.waits.append(dep)
                self._learn(eng, dep)
        best = {}
        keep = []
        for d in op.waits:
            if d.is_dma:
                keep.append(d)
            else:
                b = best.get(d.eng)
                if b is None or d.idx > b.idx:
                    best[d.eng] = d
        op.waits = keep + list(best.values())
        for d in best.values():
            d.signal = True
        op.clock = dict(self.clock[eng])
        op.dma_known = set(self.dma_known[eng])
        self.ops[eng].append(op)
        for k in reads:
            self.readers.setdefault(k, []).append(op)
        for k in writes:
            self.last_w[k] = op
            self.readers[k] = []
        return op

    def emit(self):
        nc = self.nc
        sems = {}
        ctx = []
        for e in COMPUTE:
            s = nc.semaphore("s_" + e)
            ctx.append(s)
            sems[e] = s.__enter__()
        dsems = []
        for i in range(NDMA_SEM):
            s = nc.semaphore("d_%d" % i)
            ctx.append(s)
            dsems.append(s.__enter__())
        for e in COMPUTE:
            c = 0
            for op in self.ops[e]:
                if op.signal:
                    c += 1
                    op.sigval = c
        final = [(dsems[i], self.dma_slot_cnt[i]) for i in range(NDMA_SEM)
                 if self.dma_slot_cnt[i] > 0]

        def run(e, eng):
            for op in self.ops[e]:
                for d in op.waits:
                    if d.is_dma:
                        eng.wait_ge(dsems[d.slot], d.slotval)
                    else:
                        eng.wait_ge(sems[d.eng], d.sigval)
                ins = op.fn(eng)
                if op.is_dma:
                    ins.then_inc(dsems[op.slot], 16)
                elif op.signal:
                    ins.then_inc(sems[e], 1)
            if e == "sp":
                for s, v in final:
                    eng.wait_ge(s, v)

        with nc.Block() as block:
            @block.tensor
            def _(eng):
                run("pe", eng)

            @block.scalar
            def _(eng):
                run("act", eng)

            @block.vector
            def _(eng):
                run("dve", eng)

            @block.gpsimd
            def _(eng):
                run("pool", eng)

            @block.sync
            def _(eng):
                run("sp", eng)
        for s in reversed(ctx):
            s.__exit__(None, None, None)


V_N1, V_NM, V_N2, V_NF = 0, 8, 16, 24
V_PS = 32
V_CW = 36
V_CB = 84
V_DTB = 96
V_ALOG = 104
V_DSK = 112
V_SSN = 120
NV = 128
R_DTB, R_ALOG, R_DSK, R_SSN, R_INV = 0, 16, 32, 48, 48 + 1024
NROWS = R_INV + 4 * MT


def build_program(dbg=None):
    nc = bass.Bass("TRN2", target_bir_lowering=False)
    P = Prog(nc)

    def din(name, shape):
        return nc.dram_tensor(name, list(shape), F32, kind="ExternalInput").ap()

    def dout(name, shape):
        return nc.dram_tensor(name, list(shape), F32, kind="ExternalOutput").ap()

    xT = din("xT", [KC, 128, NT])
    vecs_d = din("vecs", [128, NV])
    rows_d = din("rows", [1, NROWS])
    wg_d = [din("wg1", [FC, 128, D]), din("wg2", [FC, 128, D])]
    wu_d = [din("wu1", [FC, 128, D]), din("wu2", [FC, 128, D])]
    wd_d = [din("wd1", [G_FFN, KC, 128, FG * 128]), din("wd2", [G_FFN, KC, 128, FG * 128])]
    win_d = din("win", [128, KC, IN_DIM])
    wout_d = din("wout", [128, 12, D])
    wpool_d = din("wpool", [128, 4, 128])
    spool_d = din("spool", [128, 4 * NS * 15])
    sconv_d = din("sconv", [128, 12 * NS * 3])
    sssm_d = din("sssm", [128, 8, NS * 128])

    yT = dout("yT", [KC, 128, NT])
    poolp_o = dout("poolp", [128, 4 * 15])
    convp_o = dout("convp", [128, 12 * 3])
    ssmp_o = dout("ssmp", [128, 1024])
    pools_o = dout("pools", [128, 4 * NS * 15])
    convs_o = dout("convs", [128, 12 * NS * 3])
    ssms_o = dout("ssms", [128, 8, NS * 128])

    ARENA = 210432
    arena = nc.alloc_sbuf_tensor("arena", [128, ARENA], U8)
    cur = [0]

    def alloc(shape, dt):
        n = int(np.prod(shape[1:])) * (4 if dt == F32 else 2)
        n = (n + 63) // 64 * 64
        assert cur[0] + n <= ARENA, ("arena overflow", cur[0], n)
        a = arena[:, cur[0]:cur[0] + n].bitcast(dt)
        nel = int(np.prod(shape[1:]))
        a = a[:, 0:nel]
        cur[0] += n
        if len(shape) == 3:
            a = a.rearrange("p (a b) -> p a b", b=shape[2])
        elif len(shape) == 4:
            a = a.rearrange("p (a b c) -> p a b c", b=shape[2], c=shape[3])
        return a

    H = alloc([128, KC, NT], F32)
    vecs = alloc([128, NV], F32)
    rows = alloc([128, NROWS], F32)
    A_b = alloc([128, 16], F32)
    ones_bf = alloc([128, 128], BF16)
    ident_bf = alloc([128, 128], BF16)
    ident_f = alloc([128, 128], F32)
    tri_f = alloc([128, 128], F32)
    mask4 = alloc([128, 512], BF16)
    phase_base = cur[0]

    PS = [nc.alloc_psum_tensor("ps%d" % i, [128, 512], F32) for i in range(6)]
    PR = nc.alloc_psum_tensor("pr", [128, 1024], F32)

    def bank_fn(k):
        name = k[0] if isinstance(k, tuple) else k
        if not isinstance(name, str):
            return None
        if name == "pg":
            return k[1]
        if name == "pu":
            return 2 + k[1]
        if name == "py":
            return 4 + k[1]
        if name == "pstat":
            return 6
        if name == "B0":
            return k[1]
        if name == "B2":
            return 2
        if name.startswith("B3"):
            return 3
        if name.startswith("B4"):
            return 4
        if name == "B7":
            return 5
        if name == "R":
            return 6 if k[1] < 4 else 7
        if name == "RB":
            return (6, 7, 1, 2)[k[1]]
        return None
    P.bank_fn = bank_fn

    def mm(out, lhsT, rhs, start, stop, r, w):
        P.add("pe", lambda e: e.matmul(out, lhsT=lhsT, rhs=rhs, start=start, stop=stop), r, w)

    def trp(out, in_, r, w):
        P.add("pe", lambda e: e.transpose(out, in_, ident_bf), list(r) + ["ident_bf"], w)

    def actf(out, in_, func, r, w, bias=None, scale=None, accum=None):
        kw = {}
        if bias is not None:
            kw["bias"] = bias
        if scale is not None:
            kw["scale"] = scale
        if accum is not None:
            kw["accum_out"] = accum
        P.add("act", lambda e: e.activation(out=out, in_=in_, func=func, **kw), r, w)

    def tt(eng, out, a, b, op, r, w):
        P.add(eng, lambda e: e.tensor_tensor(out=out, in0=a, in1=b, op=op), r, w)

    def ts(eng, out, a, s1, op0, r, w, s2=None, op1=None):
        if op1 is None:
            P.add(eng, lambda e: e.tensor_scalar(out=out, in0=a, scalar1=s1, scalar2=None, op0=op0), r, w)
        else:
            P.add(eng, lambda e: e.tensor_scalar(out=out, in0=a, scalar1=s1, scalar2=s2, op0=op0, op1=op1), r, w)

    def stt(out, a, s, b, op0, op1, r, w):
        P.add("dve", lambda e: e.scalar_tensor_tensor(out=out, in0=a, scalar=s, in1=b, op0=op0, op1=op1), r, w)

    def cp(eng, out, in_, r, w):
        if eng == "act":
            P.add("act", lambda e: e.copy(out=out, in_=in_), r, w)
        else:
            P.add(eng, lambda e: e.tensor_copy(out=out, in_=in_), r, w)

    def dma(q, out, in_, r, w):
        P.add(q, lambda e: e.dma_start(out=out, in_=in_), r, w, dma=True)

    ndump = [0]

    def dump(name, ap, key):
        if dbg is None or name not in dbg:
            return
        shape = list(ap.shape)
        o = dout("dbg_" + name, shape)
        dma("sp", o, ap, [key] if not isinstance(key, list) else key, [])

    def hkeys(ks, off, n):
        q0, q1 = off // 256, (off + n - 1) // 256
        return [("h", k, q) for k in ks for q in range(q0, q1 + 1)]

    XH0 = 3 * TT[0][1]
    for (c0, c1) in ((0, XH0), (XH0, NT)):
        for k in range(KC):
            dma("sp", H[:, k, c0:c1], xT[k][:, c0:c1], [], hkeys([k], c0, c1 - c0))
    dma("sp", vecs, vecs_d, [], ["vecs"])
    dma("sp", rows, rows_d.partition_broadcast(128).rearrange("p a n -> p (a n)"), [], ["rows"])
    actf(A_b, rows[:, R_ALOG:R_ALOG + 16], AF.Exp, ["rows"], ["A_b"])
    ts("dve", A_b, A_b, -1.0, ALU.mult, ["A_b"], ["A_b"])
    P.add("pool", lambda e: e.memset(ones_bf, 1.0), [], ["ones_bf"])
    P.add("pool", lambda e: e.memset(ident_bf, 1.0), [], ["ident_bf"])
    P.add("pool", lambda e: e.affine_select(out=ident_bf, in_=ident_bf, pattern=[[1, 128]], compare_op=ALU.is_equal,
                                            fill=0.0, base=0, channel_multiplier=-1), ["ident_bf"], ["ident_bf"])
    P.add("pool", lambda e: e.memset(ident_f, 1.0), [], ["ident_f"])
    P.add("pool", lambda e: e.affine_select(out=ident_f, in_=ident_f, pattern=[[1, 128]], compare_op=ALU.is_equal,
                                            fill=0.0, base=0, channel_multiplier=-1), ["ident_f"], ["ident_f"])
    P.add("pool", lambda e: e.memset(tri_f, 1.0), [], ["tri_f"])
    P.add("pool", lambda e: e.affine_select(out=tri_f, in_=tri_f, pattern=[[1, 128]], compare_op=ALU.is_ge,
                                            fill=0.0, base=0, channel_multiplier=-1), ["tri_f"], ["tri_f"])
    P.add("pool", lambda e: e.memset(mask4, NEG), [], ["mask4"])
    P.add("pool", lambda e: e.affine_select(out=mask4, in_=mask4, pattern=[[0, 4], [-1, 128]], compare_op=ALU.is_gt,
                                            fill=0.0, base=0, channel_multiplier=1), ["mask4"], ["mask4"])

    def rmsnorm(gcol, off, n, sqb, rs, pstat, pkey, out_fn, out_keys_fn, extra_w=()):
        hk = hkeys(range(KC), off, n)
        actf(sqb[:, :, 0:n], H[:, :, off:off + n], AF.Square, hk, ["sqb"] + list(extra_w))
        for k in range(KC):
            mm(pstat[:, 0:n], ones_bf, sqb[:, k, 0:n], k == 0, k == KC - 1, ["ones_bf", "sqb"], [pkey])
        actf(rs[:, 0:n], pstat[:, 0:n], AF.Ln, [pkey], ["rs"], bias=EPS, scale=1.0 / D)
        actf(rs[:, 0:n], rs[:, 0:n], AF.Exp, ["rs"], ["rs"], scale=-0.5)
        for k in range(KC):
            stt(out_fn(k), H[:, k, off:off + n], vecs[:, gcol + k:gcol + k + 1], rs[:, 0:n], ALU.mult, ALU.mult,
                hkeys([k], off, n) + ["vecs", "rs"], out_keys_fn(k))

    WBLK = [(0, 512), (3072, 3088), (1536, 2304), (2304, 3072), (512, 1024), (1024, 1536)]

    def wkey(col):
        for bi, (c0, c1) in enumerate(WBLK):
            if c0 <= col < c1:
                return ("winb", bi)
        raise AssertionError(col)

    def ffn_phase(which, gcol):
        P.barrier()
        cur[0] = phase_base
        hn = alloc([128, KC, NT], BF16)
        actT = alloc([128, FG, NT], BF16)
        NWB = 3
        wgb = [alloc([128, KC, 128], BF16) for _ in range(NWB)]
        wub = [alloc([128, KC, 128], BF16) for _ in range(NWB)]
        wdb = [alloc([128, FG, 128], BF16) for _ in range(NWB)]
        sgt = [alloc([128, 512], F32) for _ in range(2)]
        sqb = alloc([128, KC, 512], BF16)
        rs = alloc([128, 512], F32)
        pg, pu, py = [PS[0], PS[1]], [PS[2], PS[3]], [PS[4], PS[5]]
        pstat = PR[:, 0:512]

        def norm_tile(ti):
            off, n = TT[ti]
            rmsnorm(gcol, off, n, sqb, rs, pstat, "pstat",
                    lambda k, off=off, n=n: hn[:, k, off:off + n],
                    lambda k, ti=ti: [("hn", k, ti)])
        norm_tile(0)
        norm_tile(1)

        cnt = [0, 0]
        for g in range(G_FFN):
            for fi in range(FG):
                f = g * FG + fi
                b = f % NWB
                dma("pool", wgb[b].rearrange("p k c -> p (k c)"), wg_d[which][f], [], [("wg", b)])
                dma("pool", wub[b].rearrange("p k c -> p (k c)"), wu_d[which][f], [], [("wu", b)])
                for ti, (off, n) in enumerate(TT):
                    if f == 0 and ti + 2 < len(TT):
                        norm_tile(ti + 2)
                    pb = cnt[0] % 2
                    cnt[0] += 1
                    for k in range(KC):
                        mm(pg[pb][:, 0:n], wgb[b][:, k, :], hn[:, k, off:off + n], k == 0, k == KC - 1,
                           [("wg", b), ("hn", k, ti)], [("pg", pb)])
                    for k in range(KC):
                        mm(pu[pb][:, 0:n], wub[b][:, k, :], hn[:, k, off:off + n], k == 0, k == KC - 1,
                           [("wu", b), ("hn", k, ti)], [("pu", pb)])
                    actf(sgt[pb][:, 0:n], pg[pb][:, 0:n], AF.Silu, [("pg", pb)], [("sg", pb)])
                    tt("dve", actT[:, fi, off:off + n], sgt[pb][:, 0:n], pu[pb][:, 0:n], ALU.mult,
                       [("sg", pb), ("pu", pb)], [("act", fi, ti)])
            for d in range(KC):
                b = (g * KC + d) % NWB
                dma("pool", wdb[b].rearrange("p f c -> p (f c)"), wd_d[which][g, d], [], [("wd", b)])
                for ti, (off, n) in enumerate(TT):
                    pb = cnt[1] % 2
                    cnt[1] += 1
                    for fi in range(FG):
                        mm(py[pb][:, 0:n], wdb[b][:, fi, :], actT[:, fi, off:off + n], fi == 0, fi == FG - 1,
                           [("wd", b), ("act", fi, ti)], [("py", pb)])
                    hk = hkeys([d], off, n)
                    stt(H[:, d, off:off + n], py[pb][:, 0:n], 0.5, H[:, d, off:off + n], ALU.mult, ALU.add,
                        [("py", pb)] + hk, hk)

    def mixer_phase():
        P.barrier()
        cur[0] = phase_base
        win = alloc([128, KC, IN_DIM], BF16)
        wout = alloc([128, 12, D], BF16)
        wpool = alloc([128, 4, 128], BF16)
        for bi, (c0, c1) in enumerate(WBLK):
            dma("pool", win[:, :, c0:c1], win_d[:, :, c0:c1], [], [("winb", bi)])
            if bi == 0:
                dma("pool", wpool.rearrange("p g c -> p (g c)"), wpool_d.rearrange("p g c -> p (g c)"), [], ["wpool"])
        for j in range(12):
            dma("pool", wout[:, j, :], wout_d[:, j, :], [], [("wout", j)])
        WIN = [("win", k) for k in range(KC)]
        WOUT = [("wout", j) for j in range(12)]
        tile_base = cur[0]

        hnm = alloc([128, KC, MT], BF16)
        rs = alloc([128, MT], F32)
        U = alloc([128, 4, 15 + MT], F32)
        pooled = alloc([128, 4, MT], BF16)
        X = [alloc([128, 3 + MT], F32) for _ in range(2)]
        XH = alloc([128, 12, 3], F32)
        acc = [alloc([128, MT], F32) for _ in range(2)]
        xbcT = alloc([128, 12, MT], BF16)
        sqb = xbcT.rearrange("p a b -> p (a b)")[:, 0:KC * MT].rearrange("p (a b) -> p a b", b=MT)
        catT = alloc([128, 12, MT], BF16)
        NCH = MT // 128
        sm = alloc([128, 14, NCH * 16], F32)
        dtr, e1, dt_t, lndt, dtA, Acs, eAcs, bias_t, tw, wdec, cd, rr1, rr2 = [sm[:, i, :] for i in range(13)]
        dts = alloc([128, 3, NCH * 16], BF16)
        tri_bf = alloc([128, 128], BF16)
        ssq = alloc([128, 2, 2], F32)
        mhalf = alloc([128, 1], F32)
        sT = alloc([128, 128], F32)
        Dt = [alloc([128, 128], F32) for _ in range(4)]
        MTt = [alloc([128, 128], BF16) for _ in range(8)]
        xtokb = [alloc([128, 512], BF16) for _ in range(2)]
        Btokb = [alloc([128, 128], BF16) for _ in range(2)]
        xw = alloc([128, 512], BF16)
        szb = [alloc([128, 512], F32) for _ in range(2)]
        xdb = alloc([128, 512], BF16)
        tqb = [alloc([128, 512], F32) for _ in range(2)]
        yn = alloc([128, 512], BF16)
        hT = alloc([128, 1024], F32)
        hTb = alloc([128, 1024], BF16)
        prompt_end = cur[0]
        print("arena: phase_base", phase_base, "tile_base", tile_base, "prompt_end", prompt_end)

        B0 = [PS[0][:, :], PS[1][:, :]]
        B2, B3, B7 = PS[2], PS[3], PS[5]
        B3bf = PS[3][:, :].bitcast(BF16)
        B4bf = PS[4][:, :].bitcast(BF16)
        b0cnt = [0]

        def b0next():
            i = b0cnt[0] % 2
            b0cnt[0] += 1
            return B0[i], ("B0", i)

        BP = [(PS[0][:, :], ("B0", 0)), (PS[1][:, :], ("B0", 1)), (PS[2][:, :], "B2"), (PS[5][:, :], "B7")]
        bpcnt = [0]

        def bpnext():
            i = bpcnt[0] % len(BP)
            bpcnt[0] += 1
            return BP[i]

        P.add("dve", lambda e: e.memset(U[:, :, 0:15], 0.0), [], ["U"])
        P.add("dve", lambda e: e.memset(XH, 0.0), [], ["XH"])
        P.add("dve", lambda e: e.memset(hT, 0.0), [], ["hT0", "hT1"])
        P.add("pool", lambda e: e.memset(hTb, 0.0), [], ["hTb0", "hTb1"])

        P.add("pool", lambda e: e.memset(mhalf, -0.5), [], ["mhalf"])
        cp("pool", tri_bf, tri_f, ["tri_f"], ["tri_bf"])
        bc64 = lambda ap8: ap8.unsqueeze(2).to_broadcast([128, 8, 64])
        v64 = lambda ap: ap.rearrange("p (h q) -> p h q", q=64)

        def uctx(cl, g, ui):
            ub = ui % 2
            return dict(cl=cl, g=g, ub=ub, cs=cl * 128, hs=slice(cl * 16 + g * 8, cl * 16 + g * 8 + 8),
                        BTc=xbcT[:, 8 + g, cl * 128:cl * 128 + 128], CTc=xbcT[:, 10 + g, cl * 128:cl * 128 + 128],
                        xtok=xtokb[ub], Btok=Btokb[ub], sz=szb[ub], tq=tqb[ub], zk=[None, None])

        RK = [("R", hh) for hh in range(8)]

        def s1_pe(u):
            cl, g, cs = u["cl"], u["g"], u["cs"]
            for b in range(2):
                bank = PR[:, b * 512:(b + 1) * 512]
                mm(bank, ident_bf, mask4, True, False, ["ident_bf", "mask4"], [("R", b * 4)])
                for hq in range(4):
                    hh = b * 4 + hq
                    col = cl * 16 + g * 8 + hh
                    for part in range(3):
                        mm(PR[:, hh * 128:(hh + 1) * 128], dts[:, part, col:col + 1].to_broadcast([128, 128]), tri_bf,
                           False, hq == 3 and part == 2, ["dts", "tri_bf"], [("R", hh)])
            mm(B3[:, 128:256], u["BTc"], u["CTc"], True, True, [("xbc", 8 + g), ("xbc", 10 + g)], ["B3sc"])
            for i in range(4):
                trp(B4bf[:, i * 128:(i + 1) * 128], xbcT[:, g * 4 + i, cs:cs + 128], [("xbc", g * 4 + i)], ["B4x"])
            trp(B3bf[:, 512:640], u["BTc"], [("xbc", 8 + g)], ["B3bt"])
            psz, kz = b0next()
            u["zk"] = [psz, kz]
            for k in range(KC):
                mm(psz, hnm[:, k, cs:cs + 128], win[:, k, 512 + g * 512:1024 + g * 512],
                   k == 0, k == KC - 1, [("hnm", k), wkey(512 + g * 512)], [kz])
            mm(B2[:, :], u["CTc"], hTb[:, g * 512:(g + 1) * 512], True, True, [("xbc", 10 + g), "hTb%d" % g], ["B2"])

        def s1_elem(u):
            g, ub, hs = u["g"], u["ub"], u["hs"]
            r127 = PR[:, 127:1024:128]
            tt("dve", tw[:, hs], r127, Acs[:, hs], ALU.subtract, RK + ["Acs"], [("tw", ub)])
            actf(cd[:, hs], r127, AF.Exp, RK, [("cd", ub)])
            actf(wdec[:, hs], tw[:, hs], AF.Exp, [("tw", ub)], [("wdec", ub)])
            tt("dve", wdec[:, hs], wdec[:, hs], dt_t[:, hs], ALU.mult, [("wdec", ub), "dt"], [("wdec", ub)])
            cp("act", sT, B3[:, 128:256], ["B3sc"], ["sT"])
            cp("act", u["xtok"], B4bf[:, 0:512], ["B4x"], [("xtok", ub)])
            cp("act", u["Btok"], B3bf[:, 512:640], ["B3bt"], [("Btok", ub)])
            psz, kz = u["zk"]
            actf(u["sz"], psz, AF.Tanh, [kz], [("sz", ub)], scale=0.5)
            stt(u["sz"], u["sz"], 1.0, psz, ALU.add, ALU.mult, [("sz", ub), kz], [("sz", ub)])
            tt("pool", v64(xdb), v64(u["xtok"]), bc64(rows[:, R_DSK + g * 8:R_DSK + g * 8 + 8]), ALU.mult,
               [("xtok", ub), "rows"], ["xd"])

        def s1_tail(u):
            cl, g, ub, hs = u["cl"], u["g"], u["ub"], u["hs"]
            xtok, tq = u["xtok"], u["tq"]
            mm(B7[:, :], ident_bf, xdb, True, False, ["ident_bf", "xd"], ["B7"])
            for hh in range(8):
                col = cl * 16 + g * 8 + hh
                Dh = Dt[hh % 4]
                Mh = MTt[hh]
                actf(Dh, PR[:, hh * 128:(hh + 1) * 128], AF.Exp, [("R", hh), "bias"], [("D", hh % 4)],
                     bias=bias_t[:, col:col + 1], scale=1.0)
                tt("pool" if hh % 2 == 0 else "dve", Mh, Dh, sT, ALU.mult, [("D", hh % 4), "sT"], [("M", hh)])
                mm(B7[:, hh * 64:(hh + 1) * 64], Mh, xtok[:, hh * 64:(hh + 1) * 64], False, hh == 7,
                   [("M", hh), ("xtok", ub)], ["B7"])
            tt("dve", v64(tq), v64(B2[:, :]), bc64(eAcs[:, hs]), ALU.mult, ["B2", "eAcs"], [("tq", ub)])
            tt("dve", tq, tq, B7[:, :], ALU.add, [("tq", ub), "B7"], [("tq", ub)])

        def s2_front(u):
            g, ub = u["g"], u["ub"]
            tq, sz = u["tq"], u["sz"]
            sq = ssq[:, ub, :]
            stt(tq, tq, 0.5, sz, ALU.mult, ALU.mult, [("tq", ub), ("sz", ub)], [("tq", ub)])
            actf(yn, tq, AF.Square, [("tq", ub)], ["yn", ("ssq", ub)], accum=sq[:, 0:1])
            ts("pool", sq[:, 1:2], sq[:, 0:1], 1.0 / 512, ALU.mult, [("ssq", ub)], [("ssq", ub)], s2=EPS, op1=ALU.add)
            tt("pool", sq[:, 1:2], sq[:, 1:2], mhalf, ALU.pow, [("ssq", ub), "mhalf"], [("ssq", ub)])
            stt(yn, tq, sq[:, 1:2], rows[:, R_SSN + g * 512:R_SSN + (g + 1) * 512], ALU.mult, ALU.mult,
                [("tq", ub), ("ssq", ub), "rows"], ["yn"])

        def s2_mid(u):
            g, ub, cs, hs = u["g"], u["ub"], u["cs"], u["hs"]
            for i in range(4):
                trp(B4bf[:, 512 + i * 128:512 + (i + 1) * 128], yn[:, i * 128:(i + 1) * 128], ["yn"], ["B4y"])
            cp("act", catT[:, 4 + g * 4:8 + g * 4, cs:cs + 128],
               B4bf[:, 512:1024].rearrange("p (a b) -> p a b", b=128), ["B4y"],
               [("cat", 4 + g * 4 + i) for i in range(4)])

        def s2_state(u):
            g, ub, cs, hs = u["g"], u["ub"], u["cs"], u["hs"]
            tt("pool", v64(xw), v64(u["xtok"]), bc64(wdec[:, hs]), ALU.mult, [("xtok", ub), ("wdec", ub)], ["xw"])
            pst, kst = b0next()
            mm(pst, u["Btok"], xw, True, True, [("Btok", ub), "xw"], [kst])
            hg = hT[:, g * 512:(g + 1) * 512]
            tt("pool", v64(hg), v64(hg), bc64(cd[:, hs]), ALU.mult, ["hT%d" % g, ("cd", ub)], ["hT%d" % g])
            tt("dve", hg, hg, pst, ALU.add, ["hT%d" % g, kst], ["hT%d" % g])
            cp("act", hTb[:, g * 512:(g + 1) * 512], hg, ["hT%d" % g], ["hTb%d" % g])

        def tile_norm(ti):
            ps0, k0 = b0next()
            rmsnorm(V_NM, ti * MT, MT, sqb, rs, ps0[:, 0:MT], k0,
                    lambda k: hnm[:, k, :], lambda k: [("hnm", k)],
                    extra_w=[("xbc", c) for c in range(8)])

        def tile_dt():
            for cl in range(NCH):
                for k in range(KC):
                    mm(B3[:, cl * 16:(cl + 1) * 16], hnm[:, k, cl * 128:(cl + 1) * 128], win[:, k, 3072:3088],
                       k == 0, k == KC - 1, [("hnm", k), wkey(3072)], ["B3dt"])
            v16 = lambda ap: ap.rearrange("p (c h) -> p c h", h=16)
            tt("dve", v16(dtr), v16(B3[:, 0:NCH * 16]),
               rows[:, R_DTB:R_DTB + 16].unsqueeze(1).to_broadcast([128, NCH, 16]), ALU.add, ["B3dt", "rows"], ["dtr"])
            actf(e1, dtr, AF.Exp, ["dtr"], ["e1"])
            actf(dt_t, e1, AF.Ln, ["e1"], ["dt"], bias=1.0, scale=1.0)
            actf(lndt, dt_t, AF.Ln, ["dt"], ["lndt"])
            tt("dve", v16(dtA), v16(dt_t), A_b.unsqueeze(1).to_broadcast([128, NCH, 16]), ALU.mult, ["dt", "A_b"], ["dtA"])
            mm(B3[:, 64:64 + NCH * 16], tri_f, dtA, True, True, ["tri_f", "dtA"], ["B3acs"])
            cp("pool", dts[:, 0, :], dtA, ["dtA"], ["dts"])
            tt("pool", rr1, dtA, dts[:, 0, :], ALU.subtract, ["dtA", "dts"], ["rr1"])
            cp("pool", dts[:, 1, :], rr1, ["rr1"], ["dts"])
            tt("pool", rr2, rr1, dts[:, 1, :], ALU.subtract, ["rr1", "dts"], ["rr2"])
            cp("pool", dts[:, 2, :], rr2, ["rr2"], ["dts"])
            cp("dve", Acs, B3[:, 64:64 + NCH * 16], ["B3acs"], ["Acs"])
            actf(eAcs, Acs, AF.Exp, ["Acs"], ["eAcs"])
            tt("dve", bias_t, lndt, Acs, ALU.subtract, ["lndt", "Acs"], ["bias"])
        _laoff = 12 * MT // 2 - 4 * (15 + MT)
        LA = xbcT.rearrange("p a b -> p (a b)").bitcast(F32)[:, _laoff:_laoff + 4 * (15 + MT)].rearrange("p (g e) -> p g e", e=15 + MT)
        assert tqb[1].offset == tqb[0].offset + 512
        LB = arena[:, tqb[0].offset * 4:tqb[0].offset * 4 + 4096].bitcast(F32)[:, 0:3 * (15 + MT)].rearrange("p (g e) -> p g e", e=15 + MT)
        LAK = ["LA", "sqb"] + [("xbc", c) for c in range((_laoff * 2) // MT, 12)]
        LBK = ["LB", ("tq", 0), ("tq", 1)]

        def pool_inproj():
            for gi in range(4):
                ps0, k0 = bpnext()
                for k in range(KC):
                    mm(ps0[:, 0:MT], win[:, k, gi * 128:(gi + 1) * 128], hnm[:, k, :], k == 0, k == KC - 1,
                       [wkey(gi * 128), ("hnm", k)], [k0])
                cp("act", U[:, gi, 15:15 + MT], ps0[:, 0:MT], [k0], ["U"])

        def outproj(tj, d):
            ps0, k0 = bpnext()
            for j in range(12):
                mm(ps0[:, 0:MT], wout[:, j, d * 128:(d + 1) * 128], catT[:, j, :], j == 0, j == 11,
                   [("wout", j), ("cat", j)], [k0])
            hk = hkeys([d], tj * MT, MT)
            tt("dve", H[:, d, tj * MT:(tj + 1) * MT], ps0[:, 0:MT], H[:, d, tj * MT:(tj + 1) * MT], ALU.add,
               [k0] + hk, hk)

        NTILE = TP // MT
        for ti in range(NTILE):
            t0 = ti * MT
            if ti == 0:
                tile_norm(0)
            if ti == 0:
                tile_dt()
            if ti == 0:
                pool_inproj()
            E = 15 + MT
            tt("dve", LA[:, :, 1:E], U[:, :, 1:E], U[:, :, 0:E - 1], ALU.add, ["U"], LAK)
            tt("dve", LB[:, :, 3:E], LA[:, 1:4, 3:E], LA[:, 1:4, 1:E - 2], ALU.add, LAK, LBK + LAK)
            tt("dve", LA[:, 2:4, 7:E], LB[:, 1:3, 7:E], LB[:, 1:3, 3:E - 4], ALU.add, LBK, LAK + LBK)
            tt("dve", LB[:, 2, 15:E], LA[:, 3, 15:E], LA[:, 3, 7:E - 8], ALU.add, LAK, LBK + LAK)
            res = [LA[:, 0, 15:E], LB[:, 0, 15:E], LA[:, 2, 15:E], LB[:, 2, 15:E]]
            for gi in range(4):
                w = 2 << gi
                if ti == 0:
                    tt("dve", res[gi], res[gi], rows[:, R_INV + gi * MT:R_INV + (gi + 1) * MT], ALU.mult,
                       ["rows"], LAK + LBK)
                    tt("dve", pooled[:, gi, :], res[gi], U[:, gi, 15:E], ALU.subtract, LAK + LBK + ["U"],
                       [("pooled", gi)])
                else:
                    stt(pooled[:, gi, :], res[gi], 1.0 / w, U[:, gi, 15:E], ALU.mult, ALU.subtract,
                        LAK + LBK + ["U"], [("pooled", gi)])
            for c in range(12):
                ps0, k0 = bpnext()
                col = 1536 + c * 128
                for k in range(KC):
                    mm(ps0[:, 0:MT], win[:, k, col:col + 128], hnm[:, k, :], k == 0, k == KC - 1,
                       [wkey(col), ("hnm", k)], [k0])
                xb = c % 2
                Xc = X[xb]
                cp("pool", Xc[:, 0:3], XH[:, c, :], ["XH"], [("X", xb)])
                cp("act", Xc[:, 3:3 + MT], ps0[:, 0:MT], [k0], [("X", xb)])
                cp("dve", XH[:, c, :], Xc[:, MT:MT + 3], [("X", xb)], ["XH"])
                a = acc[xb]
                cw = lambda tap: vecs[:, V_CW + tap * 12 + c:V_CW + tap * 12 + c + 1]
                actf(a, Xc[:, 0:MT], AF.Identity, [("X", xb), "vecs"], [("acc", xb)],
                     bias=vecs[:, V_CB + c:V_CB + c + 1], scale=cw(0))
                if c > 0:
                    actf(xbcT[:, c - 1, :], acc[(c - 1) % 2], AF.Silu, [("acc", (c - 1) % 2)], [("xbc", c - 1), "sqb"])
                for tap in (1, 2, 3):
                    stt(a, Xc[:, tap:tap + MT], cw(tap), a, ALU.mult, ALU.add,
                        [("X", xb), "vecs", ("acc", xb)], [("acc", xb)])
                if ti > 0 and c < KC:
                    outproj(ti - 1, c)
            actf(xbcT[:, 11, :], acc[1], AF.Silu, [("acc", 1)], [("xbc", 11), "sqb"])
            cp("pool", U[:, :, 0:15], U[:, :, MT:MT + 15], ["U"], ["U"])
            for gi in range(4):
                ps1, k1 = bpnext()
                mm(ps1[:, 0:MT], wpool[:, gi, :], pooled[:, gi, :], True, True, ["wpool", ("pooled", gi)], [k1])
                ts("dve", catT[:, gi, :], ps1[:, 0:MT], vecs[:, V_PS + gi:V_PS + gi + 1], ALU.mult, [k1, "vecs"],
                   [("cat", gi)])
            units = [uctx(cl, g, cl * 2 + g) for cl in range(MT // 128) for g in range(2)]
            s1_pe(units[0])
            s1_elem(units[0])
            s1_tail(units[0])
            for ui in range(len(units)):
                nxt = units[ui + 1] if ui + 1 < len(units) else None
                if nxt is None and ti + 1 < NTILE:
                    tile_norm(ti + 1)
                    tile_dt()
                s2_front(units[ui])
                if nxt is not None:
                    s1_pe(nxt)
                    s1_elem(nxt)
                elif ti + 1 < NTILE:
                    pool_inproj()
                s2_mid(units[ui])
                if nxt is not None:
                    s1_tail(nxt)
                s2_state(units[ui])
            if ti + 1 == NTILE:
                for d in range(KC):
                    outproj(ti, d)

        dma("sp", poolp_o.rearrange("p (g j) -> p g j", j=15), U[:, :, 0:15], ["U"], [])
        dma("sp", convp_o.rearrange("p (c j) -> p c j", j=3), XH, ["XH"], [])
        dma("sp", ssmp_o, hT, ["hT0", "hT1"], [])
        dump("hmix", H[:, :, 0:TP], hkeys(range(KC), 0, TP))

        P.barrier()
        cur[0] = tile_base
        S0 = TP
        hns = alloc([128, KC, NS], BF16)
        sqs = alloc([128, KC, NS], BF16)
        rss = alloc([128, NS], F32)
        us = alloc([128, 4, NS], F32)
        Pp = alloc([128, 4, NS, 15], F32)
        NPp = alloc([128, 4, NS, 15], F32)
        ssum = alloc([128, 4, NS], F32)
        pls = alloc([128, 4, NS], BF16)
        xsr = alloc([128, 12, NS], F32)
        CP = alloc([128, 12, NS, 3], F32)
        NCv = alloc([128, 12, NS, 3], F32)
        xsa = alloc([128, 12, NS], F32)
        ctmpb = [alloc([128, 12, NS], F32) for _ in range(3)]
        zs = alloc([128, 8, NS], F32)
        dth = alloc([128, 8, NS], F32)
        ah = alloc([128, 8, NS], F32)
        Ah = alloc([128, 8], F32)
        xdt = alloc([128, 8, NS], F32)
        ys = alloc([128, 8, NS], F32)
        ygs = alloc([128, 8, NS], F32)
        rgs = alloc([128, 2, NS], F32)
        cats = alloc([128, 12, NS], BF16)
        HB = 8
        Bb = alloc([128, NS, 128], F32)
        Cb = alloc([128, NS, 128], F32)
        H0 = [alloc([128, HB, 128], F32) for _ in range(3)]
        Hn = [alloc([128, HB, 128], F32) for _ in range(2)]
        print("arena: sample_end", cur[0])

        dma("sp", Pp.rearrange("p g b j -> p (g b j)"), spool_d, [], ["Pp"])
        dma("sp", CP.rearrange("p c b j -> p (c b j)"), sconv_d, [], ["CP"])
        rmsnorm(V_NM, S0, NS, sqs, rss, PS[0][:, 0:NS], ("B0", 0),
                lambda k: hns[:, k, :], lambda k: [("hns", k)])
        for gi in range(4):
            ps0, k0 = b0next()
            for k in range(KC):
                mm(ps0[:, 0:NS], win[:, k, gi * 128:(gi + 1) * 128], hns[:, k, :], k == 0, k == KC - 1,
                   [wkey(gi * 128), ("hns", k)], [k0])
            cp("act", us[:, gi, :], ps0[:, 0:NS], [k0], ["us"])
        for c in range(12):
            ps0, k0 = b0next()
            col = 1536 + c * 128
            for k in range(KC):
                mm(ps0[:, 0:NS], win[:, k, col:col + 128], hns[:, k, :], k == 0, k == KC - 1,
                   [wkey(col), ("hns", k)], [k0])
            cp("act", xsr[:, c, :], ps0[:, 0:NS], [k0], ["xsr"])
        for j in range(8):
            ps0, k0 = b0next()
            col = 512 + j * 128
            for k in range(KC):
                mm(ps0[:, 0:NS], win[:, k, col:col + 128], hns[:, k, :], k == 0, k == KC - 1,
                   [wkey(col), ("hns", k)], [k0])
            actf(zs[:, j, :], ps0[:, 0:NS], AF.Silu, [k0], ["zs"])
        for j in range(8):
            for h2 in range(2):
                for k in range(KC):
                    c0 = 3072 + 2 * j + h2
                    lw = win[:, k, c0:c0 + 1].to_broadcast([128, 64])
                    mm(B3[h2 * 64:(h2 + 1) * 64, j * NS:(j + 1) * NS], lw, hns[:, k, :], k == 0, k == KC - 1,
                       [wkey(3072), ("hns", k)], ["B3s"])
        B3v = B3[:, 0:8 * NS].rearrange("p (j b) -> p j b", b=NS)
        bc8 = lambda col: vecs[:, col:col + 8].unsqueeze(2).to_broadcast([128, 8, NS])
        tt("dve", dth, B3v, bc8(V_DTB), ALU.add, ["B3s", "vecs"], ["dth"])
        actf(dth, dth, AF.Exp, ["dth"], ["dth"])
        actf(dth, dth, AF.Ln, ["dth"], ["dth"], bias=1.0, scale=1.0)
        actf(Ah, vecs[:, V_ALOG:V_ALOG + 8], AF.Exp, ["vecs"], ["Ah"])
        tt("dve", ah, dth, Ah.unsqueeze(2).to_broadcast([128, 8, NS]), ALU.mult, ["dth", "Ah"], ["ah"])
        actf(ah, ah, AF.Exp, ["ah"], ["ah"], scale=-1.0)
        for gi in range(4):
            w = 2 << gi
            P.add("dve", lambda e, gi=gi, w=w: e.tensor_reduce(out=ssum[:, gi, :], in_=Pp[:, gi, :, 16 - w:15],
                                                              axis=AX.X, op=ALU.add), ["Pp"], ["ssum"])
        tt("dve", ssum, ssum, us, ALU.add, ["ssum", "us"], ["ssum"])
        for gi in range(4):
            w = 2 << gi
            stt(pls[:, gi, :], ssum[:, gi, :], 1.0 / w, us[:, gi, :], ALU.mult, ALU.subtract, ["ssum", "us"], ["pls"])
            ps0, k0 = b0next()
            mm(ps0[:, 0:NS], wpool[:, gi, :], pls[:, gi, :], True, True, ["wpool", "pls"], [k0])
            ts("dve", cats[:, gi, :], ps0[:, 0:NS], vecs[:, V_PS + gi:V_PS + gi + 1], ALU.mult, [k0, "vecs"],
               [("cats", gi)])
        cp("pool", NPp[:, :, :, 0:14], Pp[:, :, :, 1:15], ["Pp"], ["NPp"])
        cp("pool", NPp[:, :, :, 14], us, ["us"], ["NPp"])
        dma("sp", pools_o, NPp.rearrange("p g b j -> p (g b j)"), ["NPp"], [])
        cwb = lambda tap: vecs[:, V_CW + tap * 12:V_CW + tap * 12 + 12].unsqueeze(2).to_broadcast([128, 12, NS])
        tt("dve", xsa, xsr, cwb(3), ALU.mult, ["xsr", "vecs"], ["xsa"])
        for tap in range(3):
            ctmp = ctmpb[tap]
            tt("dve", ctmp, CP[:, :, :, tap], cwb(tap), ALU.mult, ["CP", "vecs"], [("ctmp", tap)])
            tt("dve", xsa, xsa, ctmp, ALU.add, ["xsa", ("ctmp", tap)], ["xsa"])
        tt("dve", xsa, xsa, vecs[:, V_CB:V_CB + 12].unsqueeze(2).to_broadcast([128, 12, NS]), ALU.add,
           ["xsa", "vecs"], ["xsa"])
        actf(xsa, xsa, AF.Silu, ["xsa"], ["xsa"])
        cp("pool", NCv[:, :, :, 0:2], CP[:, :, :, 1:3], ["CP"], ["NCv"])
        cp("pool", NCv[:, :, :, 2], xsr, ["xsr"], ["NCv"])
        dma("sp", convs_o, NCv.rearrange("p c b j -> p (c b j)"), ["NCv"], [])
        tt("dve", xdt, dth, xsa[:, 0:8, :], ALU.mult, ["dth", "xsa"], ["xdt"])
        RB = [PR[:, 0:512], PR[:, 512:1024], PS[1][:, :], PS[2][:, :]]

        def ssm_load(it):
            j, hb = it // (NS // HB), it % (NS // HB)
            src = sssm_d[:, j, hb * HB * 128:(hb + 1) * HB * 128]
            dma("sp", H0[it % 3].rearrange("p b n -> p (b n)"), src, [], [("H0", it % 3)])

        for g in range(2):
            for which, dstb, chunk in ((0, Bb, 8 + g), (1, Cb, 10 + g)):
                for q in range(NS // 4):
                    rb = RB[(which * 4 + q) % 4]
                    for bi in range(4):
                        b = q * 4 + bi
                        mm(rb[:, bi * 128:(bi + 1) * 128], xsa[:, chunk, b:b + 1].to_broadcast([128, 128]), ident_f,
                           True, True, ["xsa", "ident_f"], [("RB", (which * 4 + q) % 4)])
                    cp("act", dstb[:, q * 4:(q + 1) * 4, :], rb.rearrange("p (a n) -> p a n", n=128),
                       [("RB", (which * 4 + q) % 4)], ["Bb" if which == 0 else "Cb"])
            for j in range(g * 4, g * 4 + 4):
                for hb in range(NS // HB):
                    bs = slice(hb * HB, (hb + 1) * HB)
                    it = j * 2 + hb
                    i0, i1 = it % 3, it % 2
                    if it == 0:
                        ssm_load(0)
                        ssm_load(1)
                    if it + 2 < 8 * (NS // HB):
                        ssm_load(it + 2)
                    tt("dve", H0[i0], H0[i0], ah[:, j, bs].unsqueeze(2).to_broadcast([128, HB, 128]), ALU.mult,
                       [("H0", i0), "ah"], [("H0", i0)])
                    tt("pool", Hn[i1], Bb[:, bs, :], xdt[:, j, bs].unsqueeze(2).to_broadcast([128, HB, 128]), ALU.mult,
                       ["Bb", "xdt"], [("Hn", i1)])
                    tt("pool", Hn[i1], Hn[i1], H0[i0], ALU.add, [("Hn", i1), ("H0", i0)], [("Hn", i1)])
                    dma("sp", ssms_o[:, j, hb * HB * 128:(hb + 1) * HB * 128], Hn[i1].rearrange("p b n -> p (b n)"),
                        [("Hn", i1)], [])
                    tt("dve", H0[i0], Hn[i1], Cb[:, bs, :], ALU.mult, [("Hn", i1), "Cb"], [("H0", i0)])
                    P.add("dve", lambda e, j=j, bs=bs, i0=i0: e.tensor_reduce(out=ys[:, j, bs], in_=H0[i0], axis=AX.X,
                                                                          op=ALU.add), [("H0", i0)], ["ys"])
        tt("dve", ygs, xsa[:, 0:8, :], bc8(V_DSK), ALU.mult, ["xsa", "vecs"], ["ygs"])
        tt("dve", ygs, ygs, ys, ALU.add, ["ygs", "ys"], ["ygs"])
        tt("dve", ygs, ygs, zs, ALU.mult, ["ygs", "zs"], ["ygs"])
        tt("dve", ys, ygs, ygs, ALU.mult, ["ygs"], ["ys"])
        ones_f = H0[0].rearrange("p b n -> p (b n)")[:, 0:128]
        P.add("dve", lambda e: e.memset(ones_f, 1.0), [], [("H0", 0)])
        for g in range(2):
            for jj in range(4):
                mm(B3[:, 256 + g * NS:256 + (g + 1) * NS], ones_f, ys[:, g * 4 + jj, :], jj == 0, jj == 3,
                   [("H0", 0), "ys"], ["B3g"])
        actf(rgs, B3[:, 256:256 + 2 * NS].rearrange("p (g b) -> p g b", b=NS), AF.Sqrt, ["B3g"], ["rgs"],
             bias=EPS, scale=1.0 / 512)
        P.add("dve", lambda e: e.reciprocal(out=rgs, in_=rgs), ["rgs"], ["rgs"])
        for g in range(2):
            tt("dve", ygs[:, g * 4:g * 4 + 4, :], ygs[:, g * 4:g * 4 + 4, :],
               rgs[:, g:g + 1, :].to_broadcast([128, 4, NS]), ALU.mult, ["ygs", "rgs"], ["ygs"])
        tt("dve", cats[:, 4:12, :], ygs, bc8(V_SSN), ALU.mult, ["ygs", "vecs"], [("cats", j) for j in range(4, 12)])
        for d in range(KC):
            ps0, k0 = b0next()
            for j in range(12):
                mm(ps0[:, 0:NS], wout[:, j, d * 128:(d + 1) * 128], cats[:, j, :], j == 0, j == 11,
                   [("wout", j), ("cats", j)], [k0])
            hk = hkeys([d], S0, NS)
            tt("dve", H[:, d, S0:S0 + NS], ps0[:, 0:NS], H[:, d, S0:S0 + NS], ALU.add, [k0] + hk, hk)

    def final_phase():
        P.barrier()
        cur[0] = phase_base
        sqb = alloc([128, KC, 512], BF16)
        rs = alloc([128, 512], F32)
        yst = [alloc([128, KC, 512], F32) for _ in range(2)]
        yTv = yT.rearrange("k p t -> p k t")
        for ti, (off, n) in enumerate(TT):
            yb = yst[ti % 2]
            rmsnorm(V_NF, off, n, sqb, rs, PR[:, 0:512], "pstat",
                    lambda k, yb=yb, n=n: yb[:, k, 0:n], lambda k, ti=ti: [("yst", ti % 2)])
            dma("sp", yTv[:, :, off:off + n], yb[:, :, 0:n], [("yst", ti % 2)], [])

    ffn_phase(0, V_N1)
    dump("h1", H[:, :, :], hkeys(range(KC), 0, NT))
    mixer_phase()
    dump("h2", H[:, :, :], hkeys(range(KC), 0, NT))
    ffn_phase(1, V_N2)
    final_phase()
    P.emit()
    return nc


_NC_CACHE = {}


def _prep_shared(inp):
    f = lambda a: np.ascontiguousarray(np.asarray(a, dtype=np.float32))
    sh = {}
    for i, (g, u, d) in enumerate((("ffn1_w_gate", "ffn1_w_up", "ffn1_w_down"),
                                   ("ffn2_w_gate", "ffn2_w_up", "ffn2_w_down")), start=1):
        wg = f(inp[g])[0].reshape(KC, 128, FC, 128)
        wu = f(inp[u])[0].reshape(KC, 128, FC, 128)
        sh["wg%d" % i] = np.ascontiguousarray(wg.transpose(2, 1, 0, 3)).reshape(FC, 128, D)
        sh["wu%d" % i] = np.ascontiguousarray(wu.transpose(2, 1, 0, 3)).reshape(FC, 128, D)
        wd = f(inp[d])[0].reshape(G_FFN, FG, 128, KC, 128)
        sh["wd%d" % i] = np.ascontiguousarray(wd.transpose(0, 3, 2, 1, 4)).reshape(G_FFN, KC, 128, FG * 128)
    sh["win"] = np.ascontiguousarray(f(inp["w_in"])[0].reshape(KC, 128, IN_DIM).transpose(1, 0, 2))
    sh["wout"] = np.ascontiguousarray(f(inp["w_out"])[0].reshape(12, 128, D).transpose(1, 0, 2))
    sh["wpool"] = np.ascontiguousarray(f(inp["w_pool"])[0].transpose(1, 0, 2))
    vecs = np.zeros((128, NV), np.float32)
    col = lambda v, n: np.ascontiguousarray(f(v).reshape(n, 128).T)
    vecs[:, V_N1:V_N1 + 8] = col(inp["ffn1_norm"][0], 8)
    vecs[:, V_NM:V_NM + 8] = col(inp["mix_norm"][0], 8)
    vecs[:, V_N2:V_N2 + 8] = col(inp["ffn2_norm"][0], 8)
    vecs[:, V_NF:V_NF + 8] = col(inp["final_norm"], 8)
    vecs[:, V_PS:V_PS + 4] = col(inp["pool_scale"][0], 4)
    cw = f(inp["conv_w"])[0]
    for tap in range(4):
        vecs[:, V_CW + tap * 12:V_CW + tap * 12 + 12] = col(cw[tap], 12)
    vecs[:, V_CB:V_CB + 12] = col(inp["conv_b"][0], 12)
    hp = lambda v: np.ascontiguousarray(np.repeat(f(v).reshape(8, 2, 1), 64, axis=2).reshape(8, 128).T)
    vecs[:, V_DTB:V_DTB + 8] = hp(inp["dt_bias"][0])
    vecs[:, V_ALOG:V_ALOG + 8] = hp(inp["a_log"][0])
    vecs[:, V_DSK:V_DSK + 8] = hp(inp["d_skip"][0])
    vecs[:, V_SSN:V_SSN + 8] = col(inp["ssd_norm"][0], 8)
    sh["vecs"] = vecs
    rows = np.zeros((1, NROWS), np.float32)
    rows[0, R_DTB:R_DTB + 16] = f(inp["dt_bias"])[0]
    rows[0, R_ALOG:R_ALOG + 16] = f(inp["a_log"])[0]
    rows[0, R_DSK:R_DSK + 16] = f(inp["d_skip"])[0]
    rows[0, R_SSN:R_SSN + 1024] = f(inp["ssd_norm"])[0]
    t = np.arange(MT)
    for gi in range(4):
        rows[0, R_INV + gi * MT:R_INV + (gi + 1) * MT] = 1.0 / np.minimum(2 << gi, t + 1)
    sh["rows"] = rows
    return sh


def _run(inp, dbg=None):
    key = tuple(sorted(dbg)) if dbg else None
    if key not in _NC_CACHE:
        _NC_CACHE[key] = build_program(dbg)
    nc = _NC_CACHE[key]
    f = lambda a: np.asarray(a, dtype=np.float32)
    sh = _prep_shared(inp)
    xp, xs = f(inp["x_prompt"]), f(inp["x_sample"])
    sp, sc, ss = f(inp["state_pool"])[0], f(inp["state_conv"])[0], f(inp["state_ssm"])[0]
    in_maps = []
    for i in range(NCORES):
        sl = slice(i * NS, (i + 1) * NS)
        x = np.concatenate([xp[i], xs[sl, 0, :]], axis=0)
        m = dict(sh)
        m["xT"] = np.ascontiguousarray(x.T).reshape(KC, 128, NT)
        m["spool"] = np.ascontiguousarray(sp[sl].reshape(NS, 15, 4, 128).transpose(3, 2, 0, 1)).reshape(128, -1)
        m["sconv"] = np.ascontiguousarray(sc[sl].reshape(NS, 3, 12, 128).transpose(3, 2, 0, 1)).reshape(128, -1)
        m["sssm"] = np.ascontiguousarray(ss[sl].reshape(NS, 8, 2, 64, 128).transpose(2, 3, 1, 0, 4)).reshape(128, 8, NS * 128)
        in_maps.append(m)
    res = run_bass_kernel_spmd(nc, in_maps, core_ids=list(range(NCORES)))
    R = res.results
    y_p = np.empty((8, TP, D), np.float32)
    y_s = np.empty((128, 1, D), np.float32)
    pool_p = np.empty((1, 8, 15, 512), np.float32)
    conv_p = np.empty((1, 8, 3, 1536), np.float32)
    ssm_p = np.empty((1, 8, 16, 64, 128), np.float32)
    pool_s = np.empty((1, 128, 15, 512), np.float32)
    conv_s = np.empty((1, 128, 3, 1536), np.float32)
    ssm_s = np.empty((1, 128, 16, 64, 128), np.float32)
    for i in range(NCORES):
        r = R[i]
        sl = slice(i * NS, (i + 1) * NS)
        y = np.asarray(r["yT"]).reshape(D, NT).T
        y_p[i] = y[:TP]
        y_s[sl, 0, :] = y[TP:]
        pool_p[0, i] = np.asarray(r["poolp"]).reshape(128, 4, 15).transpose(2, 1, 0).reshape(15, 512)
        conv_p[0, i] = np.asarray(r["convp"]).reshape(128, 12, 3).transpose(2, 1, 0).reshape(3, 1536)
        ssm_p[0, i] = np.asarray(r["ssmp"]).reshape(128, 16, 64).transpose(1, 2, 0)
        pool_s[0, sl] = np.asarray(r["pools"]).reshape(128, 4, NS, 15).transpose(2, 3, 1, 0).reshape(NS, 15, 512)
        conv_s[0, sl] = np.asarray(r["convs"]).reshape(128, 12, NS, 3).transpose(2, 3, 1, 0).reshape(NS, 3, 1536)
        ssm_s[0, sl] = np.asarray(r["ssms"]).reshape(2, 64, 8, NS, 128).transpose(3, 2, 0, 1, 4).reshape(NS, 16, 64, 128)
    outs = (y_p, y_s, pool_p, conv_p, ssm_p, pool_s, conv_s, ssm_s)
    if dbg:
        return outs, [{k: np.asarray(v) for k, v in r.items() if k.startswith("dbg_")} for r in R]
    return outs


def kernel(**inputs):
    return _run(inputs)
```

```python
import numpy as np
import concourse.bass as bass
import concourse.mybir as mybir
from concourse.bass_utils import run_bass_kernel_spmd

F32 = mybir.dt.float32
BF16 = mybir.dt.bfloat16
U8 = mybir.dt.uint8
ALU = mybir.AluOpType
AF = mybir.ActivationFunctionType
AX = mybir.AxisListType

NCORES = 8
D = 1024
KC = 8
TP = 2048
NS = 16
NT = TP + NS
FF = 2816
FC = 22
G_FFN = 2
FG = FC // G_FFN
IN_DIM = 3088
EPS = 1e-6
TT = [(i * 344, 344) for i in range(6)]
MT = 256
NEG = -30000.0

COMPUTE = ("pe", "act", "dve", "pool")
ENGS = ("pe", "act", "dve", "pool", "sp")
NDMA_SEM = 24


class Op:
    __slots__ = ("eng", "fn", "idx", "waits", "signal", "sigval", "is_dma",
                 "slot", "slotval", "clock", "dma_known")

    def __init__(self, eng, fn, is_dma=False):
        self.eng = eng
        self.fn = fn
        self.is_dma = is_dma
        self.waits = []
        self.signal = False
        self.sigval = None
        self.slot = None
        self.slotval = None


class Prog:
    def __init__(self, nc):
        self.nc = nc
        self.ops = {e: [] for e in ENGS}
        self.clock = {e: {} for e in ENGS}
        self.dma_known = {e: set() for e in ENGS}
        self.last_w = {}
        self.readers = {}
        self.dma_slot_last = [None] * NDMA_SEM
        self.dma_slot_cnt = [0] * NDMA_SEM
        self.dma_next = 0
        self.dma_next_sw = 0
        self.pending = {e: [] for e in ENGS}
        self.bank_fn = lambda k: None

    def _need(self, eng, dep, raw):
        if dep.is_dma:
            return dep not in self.dma_known[eng]
        if dep.eng == eng:
            if eng == "pe" or eng == "sp":
                return False
            if not raw:
                return False
        return self.clock[eng].get(dep.eng, -1) < dep.idx

    def _learn(self, eng, dep):
        ck = self.clock[eng]
        for e2, i2 in dep.clock.items():
            if ck.get(e2, -1) < i2:
                ck[e2] = i2
        self.dma_known[eng] |= dep.dma_known
        if dep.is_dma:
            self.dma_known[eng].add(dep)
        elif ck.get(dep.eng, -1) < dep.idx:
            ck[dep.eng] = dep.idx

    def barrier(self):
        lasts = []
        for e in COMPUTE:
            if self.ops[e]:
                lasts.append(self.ops[e][-1])
        dmas = [d for d in self.dma_slot_last if d is not None]
        for e in ENGS:
            self.pending[e] = [(d, True) for d in lasts + dmas]

    def add(self, eng, fn, reads=(), writes=(), dma=False):
        op = Op(eng, fn, is_dma=dma)
        op.idx = len(self.ops[eng])
        banks = set()
        for k in list(reads) + list(writes):
            b = self.bank_fn(k)
            if b is not None:
                banks.add(("bank", b))
        writes = list(writes) + list(banks)
        deps = list(self.pending[eng])
        self.pending[eng] = []
        for k in reads:
            w = self.last_w.get(k)
            if w is not None:
                deps.append((w, True))
        for k in writes:
            w = self.last_w.get(k)
            if w is not None:
                deps.append((w, False))
            for r in self.readers.get(k, ()):
                deps.append((r, False))
        if dma:
            half = NDMA_SEM // 2
            if eng == "pool":
                slot = half + self.dma_next_sw % half
                self.dma_next_sw += 1
            else:
                slot = self.dma_next % half
                self.dma_next += 1
            prev = self.dma_slot_last[slot]
            if prev is not None:
                deps.append((prev, True))
            op.slot = slot
            self.dma_slot_cnt[slot] += 16
            op.slotval = self.dma_slot_cnt[slot]
            self.dma_slot_last[slot] = op
        seen = set()
        for dep, raw in sorted(deps, key=lambda t: -int(t[1])):
            if dep is op or id(dep) in seen:
                continue
            if self._need(eng, dep, raw):
                seen.add(id(dep))
                op.waits.append(dep)
                self._learn(eng, dep)
        best = {}
        keep = []
        for d in op.waits:
            if d.is_dma:
                keep.append(d)
            else:
                b = best.get(d.eng)
                if b is None or d.idx > b.idx:
                    best[d.eng] = d
        op.waits = keep + list(best.values())
        for d in best.values():
            d.signal = True
        op.clock = dict(self.clock[eng])
        op.dma_known = set(self.dma_known[eng])
        self.ops[eng].append(op)
        for k in reads:
            self.readers.setdefault(k, []).append(op)
        for k in writes:
            self.last_w[k] = op
            self.readers[k] = []
        return op

    def emit(self):
        nc = self.nc
        sems = {}
        ctx = []
        for e in COMPUTE:
            s = nc.semaphore("s_" + e)
            ctx.append(s)
            sems[e] = s.__enter__()
        dsems = []
        for i in range(NDMA_SEM):
            s = nc.semaphore("d_%d" % i)
            ctx.append(s)
            dsems.append(s.__enter__())
        for e in COMPUTE:
            c = 0
            for op in self.ops[e]:
                if op.signal:
                    c += 1
                    op.sigval = c
        final = [(dsems[i], self.dma_slot_cnt[i]) for i in range(NDMA_SEM)
                 if self.dma_slot_cnt[i] > 0]

        def run(e, eng):
            for op in self.ops[e]:
                for d in op.waits:
                    if d.is_dma:
                        eng.wait_ge(dsems[d.slot], d.slotval)
                    else:
                        eng.wait_ge(sems[d.eng], d.sigval)
                ins = op.fn(eng)
                if op.is_dma:
                    ins.then_inc(dsems[op.slot], 16)
                elif op.signal:
                    ins.then_inc(sems[e], 1)
            if e == "sp":
                for s, v in final:
                    eng.wait_ge(s, v)

        with nc.Block() as block:
            @block.tensor
            def _(eng):
                run("pe", eng)

            @block.scalar
            def _(eng):
                run("act", eng)

            @block.vector
            def _(eng):
                run("dve", eng)

            @block.gpsimd
            def _(eng):
                run("pool", eng)

            @block.sync
            def _(eng):
                run("sp", eng)
        for s in reversed(ctx):
            s.__exit__(None, None, None)


V_N1, V_NM, V_N2, V_NF = 0, 8, 16, 24
V_PS = 32
V_CW = 36
V_CB = 84
V_DTB = 96
V_ALOG = 104
V_DSK = 112
V_SSN = 120
NV = 128
R_DTB, R_ALOG, R_DSK, R_SSN, R_INV = 0, 16, 32, 48, 48 + 1024
NROWS = R_INV + 4 * MT


def build_program(dbg=None):
    nc = bass.Bass("TRN2", target_bir_lowering=False)
    P = Prog(nc)

    def din(name, shape):
        return nc.dram_tensor(name, list(shape), F32, kind="ExternalInput").ap()

    def dout(name, shape):
        return nc.dram_tensor(name, list(shape), F32, kind="ExternalOutput").ap()

    xT = din("xT", [KC, 128, NT])
    vecs_d = din("vecs", [128, NV])
    rows_d = din("rows", [1, NROWS])
    wg_d = [din("wg1", [FC, 128, D]), din("wg2", [FC, 128, D])]
    wu_d = [din("wu1", [FC, 128, D]), din("wu2", [FC, 128, D])]
    wd_d = [din("wd1", [G_FFN, KC, 128, FG * 128]), din("wd2", [G_FFN, KC, 128, FG * 128])]
    win_d = din("win", [128, KC, IN_DIM])
    wout_d = din("wout", [128, 12, D])
    wpool_d = din("wpool", [128, 4, 128])
    spool_d = din("spool", [128, 4 * NS * 15])
    sconv_d = din("sconv", [128, 12 * NS * 3])
    sssm_d = din("sssm", [128, 8, NS * 128])

    yT = dout("yT", [KC, 128, NT])
    poolp_o = dout("poolp", [128, 4 * 15])
    convp_o = dout("convp", [128, 12 * 3])
    ssmp_o = dout("ssmp", [128, 1024])
    pools_o = dout("pools", [128, 4 * NS * 15])
    convs_o = dout("convs", [128, 12 * NS * 3])
    ssms_o = dout("ssms", [128, 8, NS * 128])

    ARENA = 210432
    arena = nc.alloc_sbuf_tensor("arena", [128, ARENA], U8)
    cur = [0]

    def alloc(shape, dt):
        n = int(np.prod(shape[1:])) * (4 if dt == F32 else 2)
        n = (n + 63) // 64 * 64
        assert cur[0] + n <= ARENA, ("arena overflow", cur[0], n)
        a = arena[:, cur[0]:cur[0] + n].bitcast(dt)
        nel = int(np.prod(shape[1:]))
        a = a[:, 0:nel]
        cur[0] += n
        if len(shape) == 3:
            a = a.rearrange("p (a b) -> p a b", b=shape[2])
        elif len(shape) == 4:
            a = a.rearrange("p (a b c) -> p a b c", b=shape[2], c=shape[3])
        return a

    H = alloc([128, KC, NT], F32)
    vecs = alloc([128, NV], F32)
    rows = alloc([128, NROWS], F32)
    A_b = alloc([128, 16], F32)
    ones_bf = alloc([128, 128], BF16)
    ident_bf = alloc([128, 128], BF16)
    ident_f = alloc([128, 128], F32)
    tri_f = alloc([128, 128], F32)
    mask4 = alloc([128, 512], BF16)
    phase_base = cur[0]

    PS = [nc.alloc_psum_tensor("ps%d" % i, [128, 512], F32) for i in range(6)]
    PR = nc.alloc_psum_tensor("pr", [128, 1024], F32)

    def bank_fn(k):
        name = k[0] if isinstance(k, tuple) else k
        if not isinstance(name, str):
            return None
        if name == "pg":
            return k[1]
        if name == "pu":
            return 2 + k[1]
        if name == "py":
            return 4 + k[1]
        if name == "pstat":
            return 6
        if name == "B0":
            return k[1]
        if name == "B2":
            return 2
        if name.startswith("B3"):
            return 3
        if name.startswith("B4"):
            return 4
        if name == "B7":
            return 5
        if name == "R":
            return 6 if k[1] < 4 else 7
        if name == "RB":
            return (6, 7, 1, 2)[k[1]]
        return None
    P.bank_fn = bank_fn

    def mm(out, lhsT, rhs, start, stop, r, w):
        P.add("pe", lambda e: e.matmul(out, lhsT=lhsT, rhs=rhs, start=start, stop=stop), r, w)

    def trp(out, in_, r, w):
        P.add("pe", lambda e: e.transpose(out, in_, ident_bf), list(r) + ["ident_bf"], w)

    def actf(out, in_, func, r, w, bias=None, scale=None, accum=None):
        kw = {}
        if bias is not None:
            kw["bias"] = bias
        if scale is not None:
            kw["scale"] = scale
        if accum is not None:
            kw["accum_out"] = accum
        P.add("act", lambda e: e.activation(out=out, in_=in_, func=func, **kw), r, w)

    def tt(eng, out, a, b, op, r, w):
        P.add(eng, lambda e: e.tensor_tensor(out=out, in0=a, in1=b, op=op), r, w)

    def ts(eng, out, a, s1, op0, r, w, s2=None, op1=None):
        if op1 is None:
            P.add(eng, lambda e: e.tensor_scalar(out=out, in0=a, scalar1=s1, scalar2=None, op0=op0), r, w)
        else:
            P.add(eng, lambda e: e.tensor_scalar(out=out, in0=a, scalar1=s1, scalar2=s2, op0=op0, op1=op1), r, w)

    def stt(out, a, s, b, op0, op1, r, w):
        P.add("dve", lambda e: e.scalar_tensor_tensor(out=out, in0=a, scalar=s, in1=b, op0=op0, op1=op1), r, w)

    def cp(eng, out, in_, r, w):
        if eng == "act":
            P.add("act", lambda e: e.copy(out=out, in_=in_), r, w)
        else:
            P.add(eng, lambda e: e.tensor_copy(out=out, in_=in_), r, w)

    def dma(q, out, in_, r, w):
        P.add(q, lambda e: e.dma_start(out=out, in_=in_), r, w, dma=True)

    ndump = [0]

    def dump(name, ap, key):
        if dbg is None or name not in dbg:
            return
        shape = list(ap.shape)
        o = dout("dbg_" + name, shape)
        dma("sp", o, ap, [key] if not isinstance(key, list) else key, [])

    def hkeys(ks, off, n):
        q0, q1 = off // 256, (off + n - 1) // 256
        return [("h", k, q) for k in ks for q in range(q0, q1 + 1)]

    XH0 = 3 * TT[0][1]
    for (c0, c1) in ((0, XH0), (XH0, NT)):
        for k in range(KC):
            dma("sp", H[:, k, c0:c1], xT[k][:, c0:c1], [], hkeys([k], c0, c1 - c0))
    dma("sp", vecs, vecs_d, [], ["vecs"])
    dma("sp", rows, rows_d.partition_broadcast(128).rearrange("p a n -> p (a n)"), [], ["rows"])
    actf(A_b, rows[:, R_ALOG:R_ALOG + 16], AF.Exp, ["rows"], ["A_b"])
    ts("dve", A_b, A_b, -1.0, ALU.mult, ["A_b"], ["A_b"])
    P.add("pool", lambda e: e.memset(ones_bf, 1.0), [], ["ones_bf"])
    P.add("pool", lambda e: e.memset(ident_bf, 1.0), [], ["ident_bf"])
    P.add("pool", lambda e: e.affine_select(out=ident_bf, in_=ident_bf, pattern=[[1, 128]], compare_op=ALU.is_equal,
                                            fill=0.0, base=0, channel_multiplier=-1), ["ident_bf"], ["ident_bf"])
    P.add("pool", lambda e: e.memset(ident_f, 1.0), [], ["ident_f"])
    P.add("pool", lambda e: e.affine_select(out=ident_f, in_=ident_f, pattern=[[1, 128]], compare_op=ALU.is_equal,
                                            fill=0.0, base=0, channel_multiplier=-1), ["ident_f"], ["ident_f"])
    P.add("pool", lambda e: e.memset(tri_f, 1.0), [], ["tri_f"])
    P.add("pool", lambda e: e.affine_select(out=tri_f, in_=tri_f, pattern=[[1, 128]], compare_op=ALU.is_ge,
                                            fill=0.0, base=0, channel_multiplier=-1), ["tri_f"], ["tri_f"])
    P.add("pool", lambda e: e.memset(mask4, NEG), [], ["mask4"])
    P.add("pool", lambda e: e.affine_select(out=mask4, in_=mask4, pattern=[[0, 4], [-1, 128]], compare_op=ALU.is_gt,
                                            fill=0.0, base=0, channel_multiplier=1), ["mask4"], ["mask4"])

    def rmsnorm(gcol, off, n, sqb, rs, pstat, pkey, out_fn, out_keys_fn, extra_w=()):
        hk = hkeys(range(KC), off, n)
        actf(sqb[:, :, 0:n], H[:, :, off:off + n], AF.Square, hk, ["sqb"] + list(extra_w))
        for k in range(KC):
            mm(pstat[:, 0:n], ones_bf, sqb[:, k, 0:n], k == 0, k == KC - 1, ["ones_bf", "sqb"], [pkey])
        actf(rs[:, 0:n], pstat[:, 0:n], AF.Ln, [pkey], ["rs"], bias=EPS, scale=1.0 / D)
        actf(rs[:, 0:n], rs[:, 0:n], AF.Exp, ["rs"], ["rs"], scale=-0.5)
        for k in range(KC):
            stt(out_fn(k), H[:, k, off:off + n], vecs[:, gcol + k:gcol + k + 1], rs[:, 0:n], ALU.mult, ALU.mult,
                hkeys([k], off, n) + ["vecs", "rs"], out_keys_fn(k))

    WBLK = [(0, 512), (3072, 3088), (1536, 2304), (2304, 3072), (512, 1024), (1024, 1536)]

    def wkey(col):
        for bi, (c0, c1) in enumerate(WBLK):
            if c0 <= col < c1:
                return ("winb", bi)
        raise AssertionError(col)

    def ffn_phase(which, gcol):
        P.barrier()
        cur[0] = phase_base
        hn = alloc([128, KC, NT], BF16)
        actT = alloc([128, FG, NT], BF16)
        NWB = 3
        wgb = [alloc([128, KC, 128], BF16) for _ in range(NWB)]
        wub = [alloc([128, KC, 128], BF16) for _ in range(NWB)]
        wdb = [alloc([128, FG, 128], BF16) for _ in range(NWB)]
        sgt = [alloc([128, 512], F32) for _ in range(2)]
        sqb = alloc([128, KC, 512], BF16)
        rs = alloc([128, 512], F32)
        pg, pu, py = [PS[0], PS[1]], [PS[2], PS[3]], [PS[4], PS[5]]
        pstat = PR[:, 0:512]

        def norm_tile(ti):
            off, n = TT[ti]
            rmsnorm(gcol, off, n, sqb, rs, pstat, "pstat",
                    lambda k, off=off, n=n: hn[:, k, off:off + n],
                    lambda k, ti=ti: [("hn", k, ti)])
        norm_tile(0)
        norm_tile(1)

        cnt = [0, 0]
        for g in range(G_FFN):
            for fi in range(FG):
                f = g * FG + fi
                b = f % NWB
                dma("pool", wgb[b].rearrange("p k c -> p (k c)"), wg_d[which][f], [], [("wg", b)])
                dma("pool", wub[b].rearrange("p k c -> p (k c)"), wu_d[which][f], [], [("wu", b)])
                for ti, (off, n) in enumerate(TT):
                    if f == 0 and ti + 2 < len(TT):
                        norm_tile(ti + 2)
                    pb = cnt[0] % 2
                    cnt[0] += 1
                    for k in range(KC):
                        mm(pg[pb][:, 0:n], wgb[b][:, k, :], hn[:, k, off:off + n], k == 0, k == KC - 1,
                           [("wg", b), ("hn", k, ti)], [("pg", pb)])
                    for k in range(KC):
                        mm(pu[pb][:, 0:n], wub[b][:, k, :], hn[:, k, off:off + n], k == 0, k == KC - 1,
                           [("wu", b), ("hn", k, ti)], [("pu", pb)])
                    actf(sgt[pb][:, 0:n], pg[pb][:, 0:n], AF.Silu, [("pg", pb)], [("sg", pb)])
                    tt("dve", actT[:, fi, off:off + n], sgt[pb][:, 0:n], pu[pb][:, 0:n], ALU.mult,
                       [("sg", pb), ("pu", pb)], [("act", fi, ti)])
            for d in range(KC):
                b = (g * KC + d) % NWB
                dma("pool", wdb[b].rearrange("p f c -> p (f c)"), wd_d[which][g, d], [], [("wd", b)])
                for ti, (off, n) in enumerate(TT):
                    pb = cnt[1] % 2
                    cnt[1] += 1
                    for fi in range(FG):
                        mm(py[pb][:, 0:n], wdb[b][:, fi, :], actT[:, fi, off:off + n], fi == 0, fi == FG - 1,
                           [("wd", b), ("act", fi, ti)], [("py", pb)])
                    hk = hkeys([d], off, n)
                    stt(H[:, d, off:off + n], py[pb][:, 0:n], 0.5, H[:, d, off:off + n], ALU.mult, ALU.add,
                        [("py", pb)] + hk, hk)

    def mixer_phase():
        P.barrier()
        cur[0] = phase_base
        win = alloc([128, KC, IN_DIM], BF16)
        wout = alloc([128, 12, D], BF16)
        wpool = alloc([128, 4, 128], BF16)
        for bi, (c0, c1) in enumerate(WBLK):
            dma("pool", win[:, :, c0:c1], win_d[:, :, c0:c1], [], [("winb", bi)])
            if bi == 0:
                dma("pool", wpool.rearrange("p g c -> p (g c)"), wpool_d.rearrange("p g c -> p (g c)"), [], ["wpool"])
        for j in range(12):
            dma("pool", wout[:, j, :], wout_d[:, j, :], [], [("wout", j)])
        WIN = [("win", k) for k in range(KC)]
        WOUT = [("wout", j) for j in range(12)]
        tile_base = cur[0]

        hnm = alloc([128, KC, MT], BF16)
        rs = alloc([128, MT], F32)
        U = alloc([128, 4, 15 + MT], F32)
        pooled = alloc([128, 4, MT], BF16)
        X = [alloc([128, 3 + MT], F32) for _ in range(2)]
        XH = alloc([128, 12, 3], F32)
        acc = [alloc([128, MT], F32) for _ in range(2)]
        xbcT = alloc([128, 12, MT], BF16)
        sqb = xbcT.rearrange("p a b -> p (a b)")[:, 0:KC * MT].rearrange("p (a b) -> p a b", b=MT)
        catT = alloc([128, 12, MT], BF16)
        NCH = MT // 128
        sm = alloc([128, 14, NCH * 16], F32)
        dtr, e1, dt_t, lndt, dtA, Acs, eAcs, bias_t, tw, wdec, cd, rr1, rr2 = [sm[:, i, :] for i in range(13)]
        dts = alloc([128, 3, NCH * 16], BF16)
        tri_bf = alloc([128, 128], BF16)
        ssq = alloc([128, 2, 2], F32)
        mhalf = alloc([128, 1], F32)
        sT = alloc([128, 128], F32)
        Dt = [alloc([128, 128], F32) for _ in range(4)]
        MTt = [alloc([128, 128], BF16) for _ in range(8)]
        xtokb = [alloc([128, 512], BF16) for _ in range(2)]
        Btokb = [alloc([128, 128], BF16) for _ in range(2)]
        xw = alloc([128, 512], BF16)
        szb = [alloc([128, 512], F32) for _ in range(2)]
        xdb = alloc([128, 512], BF16)
        tqb = [alloc([128, 512], F32) for _ in range(2)]
        yn = alloc([128, 512], BF16)
        hT = alloc([128, 1024], F32)
        hTb = alloc([128, 1024], BF16)
        prompt_end = cur[0]
        print("arena: phase_base", phase_base, "tile_base", tile_base, "prompt_end", prompt_end)

        B0 = [PS[0][:, :], PS[1][:, :]]
        B2, B3, B7 = PS[2], PS[3], PS[5]
        B3bf = PS[3][:, :].bitcast(BF16)
        B4bf = PS[4][:, :].bitcast(BF16)
        b0cnt = [0]

        def b0next():
            i = b0cnt[0] % 2
            b0cnt[0] += 1
            return B0[i], ("B0", i)

        BP = [(PS[0][:, :], ("B0", 0)), (PS[1][:, :], ("B0", 1)), (PS[2][:, :], "B2"), (PS[5][:, :], "B7")]
        bpcnt = [0]

        def bpnext():
            i = bpcnt[0] % len(BP)
            bpcnt[0] += 1
            return BP[i]

        P.add("dve", lambda e: e.memset(U[:, :, 0:15], 0.0), [], ["U"])
        P.add("dve", lambda e: e.memset(XH, 0.0), [], ["XH"])
        P.add("dve", lambda e: e.memset(hT, 0.0), [], ["hT0", "hT1"])
        P.add("pool", lambda e: e.memset(hTb, 0.0), [], ["hTb0", "hTb1"])

        P.add("pool", lambda e: e.memset(mhalf, -0.5), [], ["mhalf"])
        cp("pool", tri_bf, tri_f, ["tri_f"], ["tri_bf"])
        bc64 = lambda ap8: ap8.unsqueeze(2).to_broadcast([128, 8, 64])
        v64 = lambda ap: ap.rearrange("p (h q) -> p h q", q=64)

        def uctx(cl, g, ui):
            ub = ui % 2
            return dict(cl=cl, g=g, ub=ub, cs=cl * 128, hs=slice(cl * 16 + g * 8, cl * 16 + g * 8 + 8),
                        BTc=xbcT[:, 8 + g, cl * 128:cl * 128 + 128], CTc=xbcT[:, 10 + g, cl * 128:cl * 128 + 128],
                        xtok=xtokb[ub], Btok=Btokb[ub], sz=szb[ub], tq=tqb[ub], zk=[None, None])

        RK = [("R", hh) for hh in range(8)]

        def s1_pe(u):
            cl, g, cs = u["cl"], u["g"], u["cs"]
            for b in range(2):
                bank = PR[:, b * 512:(b + 1) * 512]
                mm(bank, ident_bf, mask4, True, False, ["ident_bf", "mask4"], [("R", b * 4)])
                for hq in range(4):
                    hh = b * 4 + hq
                    col = cl * 16 + g * 8 + hh
                    for part in range(3):
                        mm(PR[:, hh * 128:(hh + 1) * 128], dts[:, part, col:col + 1].to_broadcast([128, 128]), tri_bf,
                           False, hq == 3 and part == 2, ["dts", "tri_bf"], [("R", hh)])
            mm(B3[:, 128:256], u["BTc"], u["CTc"], True, True, [("xbc", 8 + g), ("xbc", 10 + g)], ["B3sc"])
            for i in range(4):
                trp(B4bf[:, i * 128:(i + 1) * 128], xbcT[:, g * 4 + i, cs:cs + 128], [("xbc", g * 4 + i)], ["B4x"])
            trp(B3bf[:, 512:640], u["BTc"], [("xbc", 8 + g)], ["B3bt"])
            psz, kz = b0next()
            u["zk"] = [psz, kz]
            for k in range(KC):
                mm(psz, hnm[:, k, cs:cs + 128], win[:, k, 512 + g * 512:1024 + g * 512],
                   k == 0, k == KC - 1, [("hnm", k), wkey(512 + g * 512)], [kz])
            mm(B2[:, :], u["CTc"], hTb[:, g * 512:(g + 1) * 512], True, True, [("xbc", 10 + g), "hTb%d" % g], ["B2"])

        def s1_elem(u):
            g, ub, hs = u["g"], u["ub"], u["hs"]
            r127 = PR[:, 127:1024:128]
            tt("dve", tw[:, hs], r127, Acs[:, hs], ALU.subtract, RK + ["Acs"], [("tw", ub)])
            actf(cd[:, hs], r127, AF.Exp, RK, [("cd", ub)])
            actf(wdec[:, hs], tw[:, hs], AF.Exp, [("tw", ub)], [("wdec", ub)])
            tt("dve", wdec[:, hs], wdec[:, hs], dt_t[:, hs], ALU.mult, [("wdec", ub), "dt"], [("wdec", ub)])
            cp("act", sT, B3[:, 128:256], ["B3sc"], ["sT"])
            cp("act", u["xtok"], B4bf[:, 0:512], ["B4x"], [("xtok", ub)])
            cp("act", u["Btok"], B3bf[:, 512:640], ["B3bt"], [("Btok", ub)])
            psz, kz = u["zk"]
            actf(u["sz"], psz, AF.Tanh, [kz], [("sz", ub)], scale=0.5)
            stt(u["sz"], u["sz"], 1.0, psz, ALU.add, ALU.mult, [("sz", ub), kz], [("sz", ub)])
            tt("pool", v64(xdb), v64(u["xtok"]), bc64(rows[:, R_DSK + g * 8:R_DSK + g * 8 + 8]), ALU.mult,
               [("xtok", ub), "rows"], ["xd"])

        def s1_tail(u):
            cl, g, ub, hs = u["cl"], u["g"], u["ub"], u["hs"]
            xtok, tq = u["xtok"], u["tq"]
            mm(B7[:, :], ident_bf, xdb, True, False, ["ident_bf", "xd"], ["B7"])
            for hh in range(8):
                col = cl * 16 + g * 8 + hh
                Dh = Dt[hh % 4]
                Mh = MTt[hh]
                actf(Dh, PR[:, hh * 128:(hh + 1) * 128], AF.Exp, [("R", hh), "bias"], [("D", hh % 4)],
                     bias=bias_t[:, col:col + 1], scale=1.0)
                tt("pool" if hh % 2 == 0 else "dve", Mh, Dh, sT, ALU.mult, [("D", hh % 4), "sT"], [("M", hh)])
                mm(B7[:, hh * 64:(hh + 1) * 64], Mh, xtok[:, hh * 64:(hh + 1) * 64], False, hh == 7,
                   [("M", hh), ("xtok", ub)], ["B7"])
            tt("dve", v64(tq), v64(B2[:, :]), bc64(eAcs[:, hs]), ALU.mult, ["B2", "eAcs"], [("tq", ub)])
            tt("dve", tq, tq, B7[:, :], ALU.add, [("tq", ub), "B7"], [("tq", ub)])

        def s2_front(u):
            g, ub = u["g"], u["ub"]
            tq, sz = u["tq"], u["sz"]
            sq = ssq[:, ub, :]
            stt(tq, tq, 0.5, sz, ALU.mult, ALU.mult, [("tq", ub), ("sz", ub)], [("tq", ub)])
            actf(yn, tq, AF.Square, [("tq", ub)], ["yn", ("ssq", ub)], accum=sq[:, 0:1])
            ts("pool", sq[:, 1:2], sq[:, 0:1], 1.0 / 512, ALU.mult, [("ssq", ub)], [("ssq", ub)], s2=EPS, op1=ALU.add)
            tt("pool", sq[:, 1:2], sq[:, 1:2], mhalf, ALU.pow, [("ssq", ub), "mhalf"], [("ssq", ub)])
            stt(yn, tq, sq[:, 1:2], rows[:, R_SSN + g * 512:R_SSN + (g + 1) * 512], ALU.mult, ALU.mult,
                [("tq", ub), ("ssq", ub), "rows"], ["yn"])

        def s2_mid(u):
            g, ub, cs, hs = u["g"], u["ub"], u["cs"], u["hs"]
            for i in range(4):
                trp(B4bf[:, 512 + i * 128:512 + (i + 1) * 128], yn[:, i * 128:(i + 1) * 128], ["yn"], ["B4y"])
            cp("act", catT[:, 4 + g * 4:8 + g * 4, cs:cs + 128],
               B4bf[:, 512:1024].rearrange("p (a b) -> p a b", b=128), ["B4y"],
               [("cat", 4 + g * 4 + i) for i in range(4)])

        def s2_state(u):
            g, ub, cs, hs = u["g"], u["ub"], u["cs"], u["hs"]
            tt("pool", v64(xw), v64(u["xtok"]), bc64(wdec[:, hs]), ALU.mult, [("xtok", ub), ("wdec", ub)], ["xw"])
            pst, kst = b0next()
            mm(pst, u["Btok"], xw, True, True, [("Btok", ub), "xw"], [kst])
            hg = hT[:, g * 512:(g + 1) * 512]
            tt("pool", v64(hg), v64(hg), bc64(cd[:, hs]), ALU.mult, ["hT%d" % g, ("cd", ub)], ["hT%d" % g])
            tt("dve", hg, hg, pst, ALU.add, ["hT%d" % g, kst], ["hT%d" % g])
            cp("act", hTb[:, g * 512:(g + 1) * 512], hg, ["hT%d" % g], ["hTb%d" % g])

        def tile_norm(ti):
            ps0, k0 = bpnext()
            rmsnorm(V_NM, ti * MT, MT, sqb, rs, ps0[:, 0:MT], k0,
                    lambda k: hnm[:, k, :], lambda k: [("hnm", k)],
                    extra_w=[("xbc", c) for c in range(8)])

        def tile_dt():
            for cl in range(NCH):
                for k in range(KC):
                    mm(B3[:, cl * 16:(cl + 1) * 16], hnm[:, k, cl * 128:(cl + 1) * 128], win[:, k, 3072:3088],
                       k == 0, k == KC - 1, [("hnm", k), wkey(3072)], ["B3dt"])
            v16 = lambda ap: ap.rearrange("p (c h) -> p c h", h=16)
            tt("dve", v16(dtr), v16(B3[:, 0:NCH * 16]),
               rows[:, R_DTB:R_DTB + 16].unsqueeze(1).to_broadcast([128, NCH, 16]), ALU.add, ["B3dt", "rows"], ["dtr"])
            actf(e1, dtr, AF.Exp, ["dtr"], ["e1"])
            actf(dt_t, e1, AF.Ln, ["e1"], ["dt"], bias=1.0, scale=1.0)
            actf(lndt, dt_t, AF.Ln, ["dt"], ["lndt"])
            tt("dve", v16(dtA), v16(dt_t), A_b.unsqueeze(1).to_broadcast([128, NCH, 16]), ALU.mult, ["dt", "A_b"], ["dtA"])
            mm(B3[:, 64:64 + NCH * 16], tri_f, dtA, True, True, ["tri_f", "dtA"], ["B3acs"])
            cp("pool", dts[:, 0, :], dtA, ["dtA"], ["dts"])
            tt("pool", rr1, dtA, dts[:, 0, :], ALU.subtract, ["dtA", "dts"], ["rr1"])
            cp("pool", dts[:, 1, :], rr1, ["rr1"], ["dts"])
            tt("pool", rr2, rr1, dts[:, 1, :], ALU.subtract, ["rr1", "dts"], ["rr2"])
            cp("pool", dts[:, 2, :], rr2, ["rr2"], ["dts"])
            cp("dve", Acs, B3[:, 64:64 + NCH * 16], ["B3acs"], ["Acs"])
            actf(eAcs, Acs, AF.Exp, ["Acs"], ["eAcs"])
            tt("dve", bias_t, lndt, Acs, ALU.subtract, ["lndt", "Acs"], ["bias"])
        _laoff = 12 * MT // 2 - 4 * (15 + MT)
        LA = xbcT.rearrange("p a b -> p (a b)").bitcast(F32)[:, _laoff:_laoff + 4 * (15 + MT)].rearrange("p (g e) -> p g e", e=15 + MT)
        assert tqb[1].offset == tqb[0].offset + 512
        LB = arena[:, tqb[0].offset * 4:tqb[0].offset * 4 + 4096].bitcast(F32)[:, 0:3 * (15 + MT)].rearrange("p (g e) -> p g e", e=15 + MT)
        LAK = ["LA", "sqb"] + [("xbc", c) for c in range((_laoff * 2) // MT, 12)]
        LBK = ["LB", ("tq", 0), ("tq", 1)]

        def pool_inproj():
            for gi in range(4):
                ps0, k0 = bpnext()
                for k in range(KC):
                    mm(ps0[:, 0:MT], win[:, k, gi * 128:(gi + 1) * 128], hnm[:, k, :], k == 0, k == KC - 1,
                       [wkey(gi * 128), ("hnm", k)], [k0])
                cp("act", U[:, gi, 15:15 + MT], ps0[:, 0:MT], [k0], ["U"])

        def outproj(tj, d):
            ps0, k0 = bpnext()
            for j in range(12):
                mm(ps0[:, 0:MT], wout[:, j, d * 128:(d + 1) * 128], catT[:, j, :], j == 0, j == 11,
                   [("wout", j), ("cat", j)], [k0])
            hk = hkeys([d], tj * MT, MT)
            tt("dve", H[:, d, tj * MT:(tj + 1) * MT], ps0[:, 0:MT], H[:, d, tj * MT:(tj + 1) * MT], ALU.add,
               [k0] + hk, hk)

        NTILE = TP // MT
        for ti in range(NTILE):
            t0 = ti * MT
            if ti == 0:
                tile_norm(0)
            if ti == 0:
                tile_dt()
            if ti == 0:
                pool_inproj()
            E = 15 + MT
            tt("dve", LA[:, :, 1:E], U[:, :, 1:E], U[:, :, 0:E - 1], ALU.add, ["U"], LAK)
            tt("dve", LB[:, :, 3:E], LA[:, 1:4, 3:E], LA[:, 1:4, 1:E - 2], ALU.add, LAK, LBK + LAK)
            tt("dve", LA[:, 2:4, 7:E], LB[:, 1:3, 7:E], LB[:, 1:3, 3:E - 4], ALU.add, LBK, LAK + LBK)
            tt("dve", LB[:, 2, 15:E], LA[:, 3, 15:E], LA[:, 3, 7:E - 8], ALU.add, LAK, LBK + LAK)
            res = [LA[:, 0, 15:E], LB[:, 0, 15:E], LA[:, 2, 15:E], LB[:, 2, 15:E]]
            for gi in range(4):
                w = 2 << gi
                if ti == 0:
                    tt("dve", res[gi], res[gi], rows[:, R_INV + gi * MT:R_INV + (gi + 1) * MT], ALU.mult,
                       ["rows"], LAK + LBK)
                    tt("dve", pooled[:, gi, :], res[gi], U[:, gi, 15:E], ALU.subtract, LAK + LBK + ["U"],
                       [("pooled", gi)])
                else:
                    stt(pooled[:, gi, :], res[gi], 1.0 / w, U[:, gi, 15:E], ALU.mult, ALU.subtract,
                        LAK + LBK + ["U"], [("pooled", gi)])
            for c in range(12):
                ps0, k0 = bpnext()
                col = 1536 + c * 128
                for k in range(KC):
                    mm(ps0[:, 0:MT], win[:, k, col:col + 128], hnm[:, k, :], k == 0, k == KC - 1,
                       [wkey(col), ("hnm", k)], [k0])
                xb = c % 2
                Xc = X[xb]
                cp("pool", Xc[:, 0:3], XH[:, c, :], ["XH"], [("X", xb)])
                cp("act", Xc[:, 3:3 + MT], ps0[:, 0:MT], [k0], [("X", xb)])
                cp("dve", XH[:, c, :], Xc[:, MT:MT + 3], [("X", xb)], ["XH"])
                a = acc[xb]
                cw = lambda tap: vecs[:, V_CW + tap * 12 + c:V_CW + tap * 12 + c + 1]
                actf(a, Xc[:, 0:MT], AF.Identity, [("X", xb), "vecs"], [("acc", xb)],
                     bias=vecs[:, V_CB + c:V_CB + c + 1], scale=cw(0))
                if c > 0:
                    actf(xbcT[:, c - 1, :], acc[(c - 1) % 2], AF.Silu, [("acc", (c - 1) % 2)], [("xbc", c - 1), "sqb"])
                for tap in (1, 2, 3):
                    stt(a, Xc[:, tap:tap + MT], cw(tap), a, ALU.mult, ALU.add,
                        [("X", xb), "vecs", ("acc", xb)], [("acc", xb)])
                if ti > 0 and c < KC:
                    outproj(ti - 1, c)
            actf(xbcT[:, 11, :], acc[1], AF.Silu, [("acc", 1)], [("xbc", 11), "sqb"])
            cp("pool", U[:, :, 0:15], U[:, :, MT:MT + 15], ["U"], ["U"])
            for gi in range(4):
                ps1, k1 = bpnext()
                mm(ps1[:, 0:MT], wpool[:, gi, :], pooled[:, gi, :], True, True, ["wpool", ("pooled", gi)], [k1])
                ts("dve", catT[:, gi, :], ps1[:, 0:MT], vecs[:, V_PS + gi:V_PS + gi + 1], ALU.mult, [k1, "vecs"],
                   [("cat", gi)])
            units = [uctx(cl, g, cl * 2 + g) for cl in range(MT // 128) for g in range(2)]
            s1_pe(units[0])
            s1_elem(units[0])
            s1_tail(units[0])
            for ui in range(len(units)):
                nxt = units[ui + 1] if ui + 1 < len(units) else None
                if nxt is None and ti + 1 < NTILE:
                    tile_norm(ti + 1)
                    tile_dt()
                s2_front(units[ui])
                if nxt is not None:
                    s1_pe(nxt)
                    s1_elem(nxt)
                elif ti + 1 < NTILE:
                    pool_inproj()
                s2_mid(units[ui])
                if nxt is not None:
                    s1_tail(nxt)
                s2_state(units[ui])
            if ti + 1 == NTILE:
                for d in range(KC):
                    outproj(ti, d)

        dma("sp", poolp_o.rearrange("p (g j) -> p g j", j=15), U[:, :, 0:15], ["U"], [])
        dma("sp", convp_o.rearrange("p (c j) -> p c j", j=3), XH, ["XH"], [])
        dma("sp", ssmp_o, hT, ["hT0", "hT1"], [])
        dump("hmix", H[:, :, 0:TP], hkeys(range(KC), 0, TP))

        P.barrier()
        cur[0] = tile_base
        S0 = TP
        hns = alloc([128, KC, NS], BF16)
        sqs = alloc([128, KC, NS], BF16)
        rss = alloc([128, NS], F32)
        us = alloc([128, 4, NS], F32)
        Pp = alloc([128, 4, NS, 15], F32)
        NPp = alloc([128, 4, NS, 15], F32)
        ssum = alloc([128, 4, NS], F32)
        pls = alloc([128, 4, NS], BF16)
        xsr = alloc([128, 12, NS], F32)
        CP = alloc([128, 12, NS, 3], F32)
        NCv = alloc([128, 12, NS, 3], F32)
        xsa = alloc([128, 12, NS], F32)
        ctmpb = [alloc([128, 12, NS], F32) for _ in range(3)]
        zs = alloc([128, 8, NS], F32)
        dth = alloc([128, 8, NS], F32)
        ah = alloc([128, 8, NS], F32)
        Ah = alloc([128, 8], F32)
        xdt = alloc([128, 8, NS], F32)
        ys = alloc([128, 8, NS], F32)
        ygs = alloc([128, 8, NS], F32)
        rgs = alloc([128, 2, NS], F32)
        cats = alloc([128, 12, NS], BF16)
        HB = 8
        Bb = alloc([128, NS, 128], F32)
        Cb = alloc([128, NS, 128], F32)
        H0 = [alloc([128, HB, 128], F32) for _ in range(3)]
        Hn = [alloc([128, HB, 128], F32) for _ in range(2)]
        print("arena: sample_end", cur[0])

        dma("sp", Pp.rearrange("p g b j -> p (g b j)"), spool_d, [], ["Pp"])
        dma("sp", CP.rearrange("p c b j -> p (c b j)"), sconv_d, [], ["CP"])
        rmsnorm(V_NM, S0, NS, sqs, rss, PS[0][:, 0:NS], ("B0", 0),
                lambda k: hns[:, k, :], lambda k: [("hns", k)])
        for gi in range(4):
            ps0, k0 = bpnext()
            for k in range(KC):
                mm(ps0[:, 0:NS], win[:, k, gi * 128:(gi + 1) * 128], hns[:, k, :], k == 0, k == KC - 1,
                   [wkey(gi * 128), ("hns", k)], [k0])
            cp("act", us[:, gi, :], ps0[:, 0:NS], [k0], ["us"])
        for c in range(12):
            ps0, k0 = bpnext()
            col = 1536 + c * 128
            for k in range(KC):
                mm(ps0[:, 0:NS], win[:, k, col:col + 128], hns[:, k, :], k == 0, k == KC - 1,
                   [wkey(col), ("hns", k)], [k0])
            cp("act", xsr[:, c, :], ps0[:, 0:NS], [k0], ["xsr"])
        for j in range(8):
            ps0, k0 = bpnext()
            col = 512 + j * 128
            for k in range(KC):
                mm(ps0[:, 0:NS], win[:, k, col:col + 128], hns[:, k, :], k == 0, k == KC - 1,
                   [wkey(col), ("hns", k)], [k0])
            actf(zs[:, j, :], ps0[:, 0:NS], AF.Silu, [k0], ["zs"])
        for j in range(8):
            for h2 in range(2):
                for k in range(KC):
                    c0 = 3072 + 2 * j + h2
                    lw = win[:, k, c0:c0 + 1].to_broadcast([128, 64])
                    mm(B3[h2 * 64:(h2 + 1) * 64, j * NS:(j + 1) * NS], lw, hns[:, k, :], k == 0, k == KC - 1,
                       [wkey(3072), ("hns", k)], ["B3s"])
        B3v = B3[:, 0:8 * NS].rearrange("p (j b) -> p j b", b=NS)
        bc8 = lambda col: vecs[:, col:col + 8].unsqueeze(2).to_broadcast([128, 8, NS])
        tt("dve", dth, B3v, bc8(V_DTB), ALU.add, ["B3s", "vecs"], ["dth"])
        actf(dth, dth, AF.Exp, ["dth"], ["dth"])
        actf(dth, dth, AF.Ln, ["dth"], ["dth"], bias=1.0, scale=1.0)
        actf(Ah, vecs[:, V_ALOG:V_ALOG + 8], AF.Exp, ["vecs"], ["Ah"])
        tt("dve", ah, dth, Ah.unsqueeze(2).to_broadcast([128, 8, NS]), ALU.mult, ["dth", "Ah"], ["ah"])
        actf(ah, ah, AF.Exp, ["ah"], ["ah"], scale=-1.0)
        for gi in range(4):
            w = 2 << gi
            P.add("dve", lambda e, gi=gi, w=w: e.tensor_reduce(out=ssum[:, gi, :], in_=Pp[:, gi, :, 16 - w:15],
                                                              axis=AX.X, op=ALU.add), ["Pp"], ["ssum"])
        tt("dve", ssum, ssum, us, ALU.add, ["ssum", "us"], ["ssum"])
        for gi in range(4):
            w = 2 << gi
            stt(pls[:, gi, :], ssum[:, gi, :], 1.0 / w, us[:, gi, :], ALU.mult, ALU.subtract, ["ssum", "us"], ["pls"])
            ps0, k0 = bpnext()
            mm(ps0[:, 0:NS], wpool[:, gi, :], pls[:, gi, :], True, True, ["wpool", "pls"], [k0])
            ts("dve", cats[:, gi, :], ps0[:, 0:NS], vecs[:, V_PS + gi:V_PS + gi + 1], ALU.mult, [k0, "vecs"],
               [("cats", gi)])
        cp("pool", NPp[:, :, :, 0:14], Pp[:, :, :, 1:15], ["Pp"], ["NPp"])
        cp("pool", NPp[:, :, :, 14], us, ["us"], ["NPp"])
        dma("sp", pools_o, NPp.rearrange("p g b j -> p (g b j)"), ["NPp"], [])
        cwb = lambda tap: vecs[:, V_CW + tap * 12:V_CW + tap * 12 + 12].unsqueeze(2).to_broadcast([128, 12, NS])
        tt("dve", xsa, xsr, cwb(3), ALU.mult, ["xsr", "vecs"], ["xsa"])
        for tap in range(3):
            ctmp = ctmpb[tap]
            tt("dve", ctmp, CP[:, :, :, tap], cwb(tap), ALU.mult, ["CP", "vecs"], [("ctmp", tap)])
            tt("dve", xsa, xsa, ctmp, ALU.add, ["xsa", ("ctmp", tap)], ["xsa"])
        tt("dve", xsa, xsa, vecs[:, V_CB:V_CB + 12].unsqueeze(2).to_broadcast([128, 12, NS]), ALU.add,
           ["xsa", "vecs"], ["xsa"])
        actf(xsa, xsa, AF.Silu, ["xsa"], ["xsa"])
        cp("pool", NCv[:, :, :, 0:2], CP[:, :, :, 1:3], ["CP"], ["NCv"])
        cp("pool", NCv[:, :, :, 2], xsr, ["xsr"], ["NCv"])
        dma("sp", convs_o, NCv.rearrange("p c b j -> p (c b j)"), ["NCv"], [])
        tt("dve", xdt, dth, xsa[:, 0:8, :], ALU.mult, ["dth", "xsa"], ["xdt"])
        RB = [PR[:, 0:512], PR[:, 512:1024], PS[1][:, :], PS[2][:, :]]

        def ssm_load(it):
            j, hb = it // (NS // HB), it % (NS // HB)
            src = sssm_d[:, j, hb * HB * 128:(hb + 1) * HB * 128]
            dma("sp", H0[it % 3].rearrange("p b n -> p (b n)"), src, [], [("H0", it % 3)])

        for g in range(2):
            for which, dstb, chunk in ((0, Bb, 8 + g), (1, Cb, 10 + g)):
                for q in range(NS // 4):
                    rb = RB[(which * 4 + q) % 4]
                    for bi in range(4):
                        b = q * 4 + bi
                        mm(rb[:, bi * 128:(bi + 1) * 128], xsa[:, chunk, b:b + 1].to_broadcast([128, 128]), ident_f,
                           True, True, ["xsa", "ident_f"], [("RB", (which * 4 + q) % 4)])
                    cp("act", dstb[:, q * 4:(q + 1) * 4, :], rb.rearrange("p (a n) -> p a n", n=128),
                       [("RB", (which * 4 + q) % 4)], ["Bb" if which == 0 else "Cb"])
            for j in range(g * 4, g * 4 + 4):
                for hb in range(NS // HB):
                    bs = slice(hb * HB, (hb + 1) * HB)
                    it = j * 2 + hb
                    i0, i1 = it % 3, it % 2
                    if it == 0:
                        ssm_load(0)
                        ssm_load(1)
                    if it + 2 < 8 * (NS // HB):
                        ssm_load(it + 2)
                    tt("dve", H0[i0], H0[i0], ah[:, j, bs].unsqueeze(2).to_broadcast([128, HB, 128]), ALU.mult,
                       [("H0", i0), "ah"], [("H0", i0)])
                    tt("pool", Hn[i1], Bb[:, bs, :], xdt[:, j, bs].unsqueeze(2).to_broadcast([128, HB, 128]), ALU.mult,
                       ["Bb", "xdt"], [("Hn", i1)])
                    tt("pool", Hn[i1], Hn[i1], H0[i0], ALU.add, [("Hn", i1), ("H0", i0)], [("Hn", i1)])
                    dma("sp", ssms_o[:, j, hb * HB * 128:(hb + 1) * HB * 128], Hn[i1].rearrange("p b n -> p (b n)"),
                        [("Hn", i1)], [])
                    tt("dve", H0[i0], Hn[i1], Cb[:, bs, :], ALU.mult, [("Hn", i1), "Cb"], [("H0", i0)])
                    P.add("dve", lambda e, j=j, bs=bs, i0=i0: e.tensor_reduce(out=ys[:, j, bs], in_=H0[i0], axis=AX.X,
                                                                          op=ALU.add), [("H0", i0)], ["ys"])
        tt("dve", ygs, xsa[:, 0:8, :], bc8(V_DSK), ALU.mult, ["xsa", "vecs"], ["ygs"])
        tt("dve", ygs, ygs, ys, ALU.add, ["ygs", "ys"], ["ygs"])
        tt("dve", ygs, ygs, zs, ALU.mult, ["ygs", "zs"], ["ygs"])
        tt("dve", ys, ygs, ygs, ALU.mult, ["ygs"], ["ys"])
        ones_f = H0[0].rearrange("p b n -> p (b n)")[:, 0:128]
        P.add("dve", lambda e: e.memset(ones_f, 1.0), [], [("H0", 0)])
        for g in range(2):
            for jj in range(4):
                mm(B3[:, 256 + g * NS:256 + (g + 1) * NS], ones_f, ys[:, g * 4 + jj, :], jj == 0, jj == 3,
                   [("H0", 0), "ys"], ["B3g"])
        actf(rgs, B3[:, 256:256 + 2 * NS].rearrange("p (g b) -> p g b", b=NS), AF.Sqrt, ["B3g"], ["rgs"],
             bias=EPS, scale=1.0 / 512)
        P.add("dve", lambda e: e.reciprocal(out=rgs, in_=rgs), ["rgs"], ["rgs"])
        for g in range(2):
            tt("dve", ygs[:, g * 4:g * 4 + 4, :], ygs[:, g * 4:g * 4 + 4, :],
               rgs[:, g:g + 1, :].to_broadcast([128, 4, NS]), ALU.mult, ["ygs", "rgs"], ["ygs"])
        tt("dve", cats[:, 4:12, :], ygs, bc8(V_SSN), ALU.mult, ["ygs", "vecs"], [("cats", j) for j in range(4, 12)])
        for d in range(KC):
            ps0, k0 = bpnext()
            for j in range(12):
                mm(ps0[:, 0:NS], wout[:, j, d * 128:(d + 1) * 128], cats[:, j, :], j == 0, j == 11,
                   [("wout", j), ("cats", j)], [k0])
            hk = hkeys([d], S0, NS)
            tt("dve", H[:, d, S0:S0 + NS], ps0[:, 0:NS], H[:, d, S0:S0 + NS], ALU.add, [k0] + hk, hk)

    def final_phase():
        P.barrier()
        cur[0] = phase_base
        sqb = alloc([128, KC, 512], BF16)
        rs = alloc([128, 512], F32)
        yst = [alloc([128, KC, 512], F32) for _ in range(2)]
        yTv = yT.rearrange("k p t -> p k t")
        for ti, (off, n) in enumerate(TT):
            yb = yst[ti % 2]
            rmsnorm(V_NF, off, n, sqb, rs, PR[:, 0:512], "pstat",
                    lambda k, yb=yb, n=n: yb[:, k, 0:n], lambda k, ti=ti: [("yst", ti % 2)])
            dma("sp", yTv[:, :, off:off + n], yb[:, :, 0:n], [("yst", ti % 2)], [])

    ffn_phase(0, V_N1)
    dump("h1", H[:, :, :], hkeys(range(KC), 0, NT))
    mixer_phase()
    dump("h2", H[:, :, :], hkeys(range(KC), 0, NT))
    ffn_phase(1, V_N2)
    final_phase()
    P.emit()
    return nc


_NC_CACHE = {}


def _prep_shared(inp):
    f = lambda a: np.ascontiguousarray(np.asarray(a, dtype=np.float32))
    sh = {}
    for i, (g, u, d) in enumerate((("ffn1_w_gate", "ffn1_w_up", "ffn1_w_down"),
                                   ("ffn2_w_gate", "ffn2_w_up", "ffn2_w_down")), start=1):
        wg = f(inp[g])[0].reshape(KC, 128, FC, 128)
        wu = f(inp[u])[0].reshape(KC, 128, FC, 128)
        sh["wg%d" % i] = np.ascontiguousarray(wg.transpose(2, 1, 0, 3)).reshape(FC, 128, D)
        sh["wu%d" % i] = np.ascontiguousarray(wu.transpose(2, 1, 0, 3)).reshape(FC, 128, D)
        wd = f(inp[d])[0].reshape(G_FFN, FG, 128, KC, 128)
        sh["wd%d" % i] = np.ascontiguousarray(wd.transpose(0, 3, 2, 1, 4)).reshape(G_FFN, KC, 128, FG * 128)
    sh["win"] = np.ascontiguousarray(f(inp["w_in"])[0].reshape(KC, 128, IN_DIM).transpose(1, 0, 2))
    sh["wout"] = np.ascontiguousarray(f(inp["w_out"])[0].reshape(12, 128, D).transpose(1, 0, 2))
    sh["wpool"] = np.ascontiguousarray(f(inp["w_pool"])[0].transpose(1, 0, 2))
    vecs = np.zeros((128, NV), np.float32)
    col = lambda v, n: np.ascontiguousarray(f(v).reshape(n, 128).T)
    vecs[:, V_N1:V_N1 + 8] = col(inp["ffn1_norm"][0], 8)
    vecs[:, V_NM:V_NM + 8] = col(inp["mix_norm"][0], 8)
    vecs[:, V_N2:V_N2 + 8] = col(inp["ffn2_norm"][0], 8)
    vecs[:, V_NF:V_NF + 8] = col(inp["final_norm"], 8)
    vecs[:, V_PS:V_PS + 4] = col(inp["pool_scale"][0], 4)
    cw = f(inp["conv_w"])[0]
    for tap in range(4):
        vecs[:, V_CW + tap * 12:V_CW + tap * 12 + 12] = col(cw[tap], 12)
    vecs[:, V_CB:V_CB + 12] = col(inp["conv_b"][0], 12)
    hp = lambda v: np.ascontiguousarray(np.repeat(f(v).reshape(8, 2, 1), 64, axis=2).reshape(8, 128).T)
    vecs[:, V_DTB:V_DTB + 8] = hp(inp["dt_bias"][0])
    vecs[:, V_ALOG:V_ALOG + 8] = hp(inp["a_log"][0])
    vecs[:, V_DSK:V_DSK + 8] = hp(inp["d_skip"][0])
    vecs[:, V_SSN:V_SSN + 8] = col(inp["ssd_norm"][0], 8)
    sh["vecs"] = vecs
    rows = np.zeros((1, NROWS), np.float32)
    rows[0, R_DTB:R_DTB + 16] = f(inp["dt_bias"])[0]
    rows[0, R_ALOG:R_ALOG + 16] = f(inp["a_log"])[0]
    rows[0, R_DSK:R_DSK + 16] = f(inp["d_skip"])[0]
    rows[0, R_SSN:R_SSN + 1024] = f(inp["ssd_norm"])[0]
    t = np.arange(MT)
    for gi in range(4):
        rows[0, R_INV + gi * MT:R_INV + (gi + 1) * MT] = 1.0 / np.minimum(2 << gi, t + 1)
    sh["rows"] = rows
    return sh


def _run(inp, dbg=None):
    key = tuple(sorted(dbg)) if dbg else None
    if key not in _NC_CACHE:
        _NC_CACHE[key] = build_program(dbg)
    nc = _NC_CACHE[key]
    f = lambda a: np.asarray(a, dtype=np.float32)
    sh = _prep_shared(inp)
    xp, xs = f(inp["x_prompt"]), f(inp["x_sample"])
    sp, sc, ss = f(inp["state_pool"])[0], f(inp["state_conv"])[0], f(inp["state_ssm"])[0]
    in_maps = []
    for i in range(NCORES):
        sl = slice(i * NS, (i + 1) * NS)
        x = np.concatenate([xp[i], xs[sl, 0, :]], axis=0)
        m = dict(sh)
        m["xT"] = np.ascontiguousarray(x.T).reshape(KC, 128, NT)
        m["spool"] = np.ascontiguousarray(sp[sl].reshape(NS, 15, 4, 128).transpose(3, 2, 0, 1)).reshape(128, -1)
        m["sconv"] = np.ascontiguousarray(sc[sl].reshape(NS, 3, 12, 128).transpose(3, 2, 0, 1)).reshape(128, -1)
        m["sssm"] = np.ascontiguousarray(ss[sl].reshape(NS, 8, 2, 64, 128).transpose(2, 3, 1, 0, 4)).reshape(128, 8, NS * 128)
        in_maps.append(m)
    res = run_bass_kernel_spmd(nc, in_maps, core_ids=list(range(NCORES)))
    R = res.results
    y_p = np.empty((8, TP, D), np.float32)
    y_s = np.empty((128, 1, D), np.float32)
    pool_p = np.empty((1, 8, 15, 512), np.float32)
    conv_p = np.empty((1, 8, 3, 1536), np.float32)
    ssm_p = np.empty((1, 8, 16, 64, 128), np.float32)
    pool_s = np.empty((1, 128, 15, 512), np.float32)
    conv_s = np.empty((1, 128, 3, 1536), np.float32)
    ssm_s = np.empty((1, 128, 16, 64, 128), np.float32)
    for i in range(NCORES):
        r = R[i]
        sl = slice(i * NS, (i + 1) * NS)
        y = np.asarray(r["yT"]).reshape(D, NT).T
        y_p[i] = y[:TP]
        y_s[sl, 0, :] = y[TP:]
        pool_p[0, i] = np.asarray(r["poolp"]).reshape(128, 4, 15).transpose(2, 1, 0).reshape(15, 512)
        conv_p[0, i] = np.asarray(r["convp"]).reshape(128, 12, 3).transpose(2, 1, 0).reshape(3, 1536)
        ssm_p[0, i] = np.asarray(r["ssmp"]).reshape(128, 16, 64).transpose(1, 2, 0)
        pool_s[0, sl] = np.asarray(r["pools"]).reshape(128, 4, NS, 15).transpose(2, 3, 1, 0).reshape(NS, 15, 512)
        conv_s[0, sl] = np.asarray(r["convs"]).reshape(128, 12, NS, 3).transpose(2, 3, 1, 0).reshape(NS, 3, 1536)
        ssm_s[0, sl] = np.asarray(r["ssms"]).reshape(2, 64, 8, NS, 128).transpose(3, 2, 0, 1, 4).reshape(NS, 16, 64, 128)
    outs = (y_p, y_s, pool_p, conv_p, ssm_p, pool_s, conv_s, ssm_s)
    if dbg:
        return outs, [{k: np.asarray(v) for k, v in r.items() if k.startswith("dbg_")} for r in R]
    return outs


def kernel(**inputs):
    return _run(inputs)
```
